# Optimizing a Trainium2 kernel written in Bass

```python
import jax, jax.numpy as jnp
from jax import lax
import numpy as np

D_MODEL = 1024
BATCH = 32
SEQ = 256
DEPTH = 2
DEC_BATCH = 4
DEC_SEQ = 1024
PAST_LEN = 512

GRID_W = 64
EPS = 1e-6
H_RET = 4
DK_RET = 64
DV_RET = 128
RET_W = H_RET * DV_RET
RET_CHUNK = 64
H_MLA = 8
Q_LORA = 384
KV_LORA = 256
D_NOPE = 64
D_ROPE = 32
D_VMLA = 64
MLA_W = H_MLA * D_VMLA
ROPE_BASE = 10000.0
Q_BLOCK = 128
F_GROUPS = 4
F_GROUP_W = 128
FOURIER_W = F_GROUPS * F_GROUP_W
N_BRANCH = 3
BRANCH_W = 512
SPLITS = [H_RET * DK_RET, H_RET * DK_RET, RET_W, RET_W,
          Q_LORA, KV_LORA, D_ROPE, MLA_W,
          FOURIER_W, FOURIER_W,
          N_BRANCH * D_MODEL]
IN_W = 256 + 256 + 512 + 512 + 384 + 256 + 32 + 512 + 512 + 512 + 3 * D_MODEL

kernel_name = "hybrid_diffusion_retention_mla_fourier_step"


def rms_norm(x, g):
    xf = x.astype(jnp.float32)
    y = xf * lax.rsqrt(jnp.mean(xf * xf, axis=-1, keepdims=True) + EPS)
    return (y * g.astype(jnp.float32)).astype(x.dtype)


def head_norm(o):
    of = o.astype(jnp.float32)
    mu = jnp.mean(of, axis=-1, keepdims=True)
    var = jnp.mean((of - mu) ** 2, axis=-1, keepdims=True)
    return ((of - mu) * lax.rsqrt(var + EPS)).astype(o.dtype)


def axial_rope(x):
    T = x.shape[1]
    rows = T // GRID_W
    row = jnp.repeat(jnp.arange(rows), GRID_W)
    col = jnp.tile(jnp.arange(GRID_W), rows)
    half = D_ROPE // 2
    nfreq = half // 2
    inv = ROPE_BASE ** (-jnp.arange(nfreq, dtype=jnp.float32) / nfreq)

    def rot(xa, pos):
        ang = pos.astype(jnp.float32)[:, None] * inv[None, :]
        cos = jnp.cos(ang)[None, :, None, :]
        sin = jnp.sin(ang)[None, :, None, :]
        x1, x2 = xa[..., :nfreq], xa[..., nfreq:]
        return jnp.concatenate([x1 * cos - x2 * sin, x1 * sin + x2 * cos], axis=-1)

    xf = x.astype(jnp.float32)
    out = jnp.concatenate([rot(xf[..., :half], row), rot(xf[..., half:], col)], axis=-1)
    return out.astype(x.dtype)


def retention_dir(q, k, v, log_gamma, s0):
    B, T, H, dk = q.shape
    dv = v.shape[-1]
    C = RET_CHUNK
    n = T // C
    dt = q.dtype
    lg = log_gamma.astype(jnp.float32)
    i = jnp.arange(C, dtype=jnp.float32)
    diff = i[:, None] - i[None, :]
    dmask = jnp.where(diff[None] >= 0,
                      jnp.exp(jnp.maximum(diff, 0.0)[None] * lg[:, None, None]), 0.0).astype(dt)
    xi = jnp.exp((i[:, None] + 1.0) * lg[None, :]).astype(dt)
    zeta = jnp.exp((C - 1.0 - i)[:, None] * lg[None, :]).astype(dt)
    chunk_decay = jnp.exp(C * lg).astype(dt)
    qc = q.reshape(B, n, C, H, dk)
    kc = k.reshape(B, n, C, H, dk)
    vc = v.reshape(B, n, C, H, dv)
    scores = jnp.einsum('bnihd,bnjhd->bnhij', qc, kc) * dmask[None, None]
    o_intra = jnp.einsum('bnhij,bnjhe->bnihe', scores, vc)
    kv = jnp.einsum('bnjhd,jh,bnjhe->nbhde', kc, zeta, vc)

    def step(s, kv_n):
        return s * chunk_decay[None, :, None, None] + kv_n, s

    s_final, s_before = lax.scan(step, s0.astype(dt), kv)
    o_cross = jnp.einsum('bnihd,nbhde->bnihe', qc, s_before) * xi[None, None, :, :, None]
    return (o_intra + o_cross).reshape(B, T, H, dv), s_final


def mla_attend(q_nope, q_rope, k_nope, k_rope, v):
    B, Tq, H, _ = q_nope.shape
    nb = Tq // Q_BLOCK
    scale = (D_NOPE + D_ROPE) ** -0.5

    def to_blocks(t):
        return t.reshape(B, nb, Q_BLOCK, H, t.shape[-1]).transpose(1, 0, 2, 3, 4)

    def block(qs):
        qn, qr = qs
        s = jnp.einsum('bqhd,bkhd->bhqk', qn, k_nope) + jnp.einsum('bqhd,bkd->bhqk', qr, k_rope)
        p = jax.nn.softmax(s.astype(jnp.float32) * scale, axis=-1).astype(v.dtype)
        return jnp.einsum('bhqk,bkhd->bqhd', p, v)

    o = lax.map(block, (to_blocks(q_nope), to_blocks(q_rope)))
    return o.transpose(1, 0, 2, 3, 4).reshape(B, Tq, H, v.shape[-1])


def fourier_mix(u):
    B, T, _ = u.shape
    ug = u.astype(jnp.float32).reshape(B, T, F_GROUPS, F_GROUP_W)
    f = jnp.fft.fft2(ug, axes=(1, 3)).real * ((T * F_GROUP_W) ** -0.5)
    return f.reshape(B, T, FOURIER_W).astype(u.dtype)


def layer(x, cvec, ctx, norm_g, w_mod, b_mod, w_in, ret_logit, q_norm_g, w_q_up,
          kv_norm_g, w_kv_up, w_branch, w_out):
    B, T, _ = x.shape
    mod = jax.nn.silu(cvec) @ w_mod + b_mod
    shift, scale, gate = jnp.split(mod[:, None, :], 3, axis=-1)
    h = rms_norm(x, norm_g) * (1 + scale) + shift
    split_at = np.cumsum(SPLITS)[:-1].tolist()
    rq, rk, rv, rz, q_lat, kv_lat, k_r, mz, fu, fz, gl = jnp.split(h @ w_in, split_at, axis=-1)

    rq = rq.reshape(B, T, H_RET, DK_RET)
    rk = rk.reshape(B, T, H_RET, DK_RET) * (DK_RET ** -0.5)
    rv = rv.reshape(B, T, H_RET, DV_RET)
    log_g = jax.nn.log_sigmoid(ret_logit.astype(jnp.float32))
    if ctx is None:
        s0f = jnp.zeros((B, H_RET, DK_RET, DV_RET), x.dtype)
        s0b = s0f
    else:
        ctx_ckv, ctx_kr, ctx_ret = ctx
        s0f, s0b = ctx_ret[:, 0], ctx_ret[:, 1]
    o_fw, s_f = retention_dir(rq, rk, rv, log_g[0], s0f)
    o_bw, s_b = retention_dir(rq[:, ::-1], rk[:, ::-1], rv[:, ::-1], log_g[1], s0b)
    o_r = head_norm(o_fw + o_bw[:, ::-1]).reshape(B, T, RET_W)

    q = (rms_norm(q_lat, q_norm_g) @ w_q_up).reshape(B, T, H_MLA, D_NOPE + D_ROPE)
    q_nope, q_rope = q[..., :D_NOPE], q[..., D_NOPE:]
    ckv = rms_norm(kv_lat, kv_norm_g)
    if ctx is None:
        keys_ckv, keys_kr = ckv, k_r
    else:
        q_rope = axial_rope(q_rope)
        keys_ckv = jnp.concatenate([ctx_ckv.astype(ckv.dtype), ckv], axis=1)
        keys_kr = jnp.concatenate([ctx_kr.astype(k_r.dtype),
                                   axial_rope(k_r[:, :, None, :])[:, :, 0]], axis=1)
    kv = (keys_ckv @ w_kv_up).reshape(B, -1, H_MLA, D_NOPE + D_VMLA)
    o_m = mla_attend(q_nope, q_rope, kv[..., :D_NOPE], keys_kr, kv[..., D_NOPE:]).reshape(B, T, MLA_W)

    o_f = fourier_mix(fu)

    branches = jnp.stack([o_r * jax.nn.silu(rz), o_m * jax.nn.silu(mz), o_f * jax.nn.silu(fz)], axis=2)
    proj = jnp.einsum('btnw,nwd->btnd', branches, w_branch)
    merged = jnp.sum(proj * jax.nn.sigmoid(gl.reshape(B, T, N_BRANCH, D_MODEL)), axis=2)
    x = x + gate * (merged @ w_out)
    new = (ckv, k_r, jnp.stack([s_f, s_b], axis=1)) if ctx is None else None
    return x, new


def setup_inputs(seed: int = 0) -> dict:
    key = jax.random.key(seed)
    ks = jax.random.split(key, 20)
    f32 = jnp.float32
    nrm = lambda k, shape, s: jax.random.normal(k, shape, f32) * s
    gam = 1.0 - 2.0 ** (-5.0 - jnp.arange(H_RET, dtype=f32))
    logit = jnp.log(gam) - jnp.log1p(-gam)
    return {
        "x_prompt": nrm(ks[0], (BATCH, SEQ, D_MODEL), 1.0),
        "x_sample": nrm(ks[1], (DEC_BATCH, DEC_SEQ, D_MODEL), 1.0),
        "cache_ckv": nrm(ks[2], (DEC_BATCH, DEPTH, PAST_LEN, KV_LORA), 1.0),
        "cache_krope": nrm(ks[3], (DEC_BATCH, DEPTH, PAST_LEN, D_ROPE), 1.0),
        "state_ret": nrm(ks[4], (DEC_BATCH, DEPTH, 2, H_RET, DK_RET, DV_RET), 0.5),
        "c": nrm(ks[5], (DEC_BATCH, D_MODEL), 1.0),
        "c_ctx": nrm(ks[6], (D_MODEL,), 1.0),
        "norm_g": 1.0 + nrm(ks[7], (DEPTH, D_MODEL), 0.02),
        "w_mod": nrm(ks[8], (DEPTH, D_MODEL, 3 * D_MODEL), 0.3 * D_MODEL ** -0.5),
        "b_mod": nrm(ks[9], (DEPTH, 3 * D_MODEL), 0.01),
        "w_in": nrm(ks[10], (DEPTH, D_MODEL, IN_W), D_MODEL ** -0.5),
        "ret_decay_logit": logit[None, None, :] + nrm(ks[11], (DEPTH, 2, H_RET), 0.05),
        "q_norm_g": 1.0 + nrm(ks[12], (DEPTH, Q_LORA), 0.02),
        "w_q_up": nrm(ks[13], (DEPTH, Q_LORA, H_MLA * (D_NOPE + D_ROPE)), Q_LORA ** -0.5),
        "kv_norm_g": 1.0 + nrm(ks[14], (DEPTH, KV_LORA), 0.02),
        "w_kv_up": nrm(ks[15], (DEPTH, KV_LORA, H_MLA * (D_NOPE + D_VMLA)), KV_LORA ** -0.5),
        "w_branch": nrm(ks[16], (DEPTH, N_BRANCH, BRANCH_W, D_MODEL), BRANCH_W ** -0.5),
        "w_out": nrm(ks[17], (DEPTH, D_MODEL, D_MODEL), D_MODEL ** -0.5),
        "final_norm_g": 1.0 + nrm(ks[18], (D_MODEL,), 0.02),
    }


def reference(x_prompt, x_sample, cache_ckv, cache_krope, state_ret, c, c_ctx, norm_g, w_mod,
              b_mod, w_in, ret_decay_logit, q_norm_g, w_q_up, kv_norm_g, w_kv_up, w_branch,
              w_out, final_norm_g):
    h = x_prompt
    ckvs, krs, rets = [], [], []
    for l in range(DEPTH):
        h, (ckv, kr, st) = layer(h, c_ctx[None, :], None, norm_g[l], w_mod[l], b_mod[l], w_in[l],
                                 ret_decay_logit[l], q_norm_g[l], w_q_up[l], kv_norm_g[l],
                                 w_kv_up[l], w_branch[l], w_out[l])
        ckvs.append(ckv)
        krs.append(kr)
        rets.append(st)
    y_prompt = rms_norm(h, final_norm_g)
    new_ckv = jnp.stack(ckvs, axis=1)
    new_krope = jnp.stack(krs, axis=1)
    new_ret = jnp.stack(rets, axis=1)

    z = x_sample
    for l in range(DEPTH):
        ctx = (cache_ckv[:, l], cache_krope[:, l], state_ret[:, l])
        z, _ = layer(z, c, ctx, norm_g[l], w_mod[l], b_mod[l], w_in[l], ret_decay_logit[l],
                     q_norm_g[l], w_q_up[l], kv_norm_g[l], w_kv_up[l], w_branch[l], w_out[l])
    y_sample = rms_norm(z, final_norm_g)
    return (y_prompt, y_sample, new_ckv, new_krope, new_ret)
```

```python
import os
import numpy as np
import ml_dtypes
import concourse.bass as bass
import concourse.mybir as mybir
from concourse.bass_utils import run_bass_kernel_spmd

F32, BF16 = mybir.dt.float32, mybir.dt.bfloat16
AF = mybir.ActivationFunctionType
ALU = mybir.AluOpType
AX = mybir.AxisListType
EPS = 1e-6
NSLOT = 12
PF = 5
JOBS = [(0, 1024, 512), (1024, 512, 0)]
SLOT0 = [0, 4]
NTOK = 1536
SC = float(96 ** -0.5)
BIG = 30000.0
ARENA = 73728
STAGE = float(os.environ.get('KSTAGE', '999'))


class _Stop(Exception):
    pass


def ck(n):
    if n >= STAGE:
        raise _Stop()


class Buf:
    __slots__ = ("w", "r", "name", "excl")

    def __init__(self, name="", excl=False):
        self.w = None
        self.r = {}
        self.name = name
        self.excl = excl


class Sched:
    ENG = ("pe", "act", "dve", "pool", "sp")

    def __init__(self, nc, n_dma_sems=10):
        self.nc = nc
        self.plan = False
        self.prog = {e: [] for e in self.ENG}
        self.sem = {e: nc.alloc_semaphore(f"c_{e}") for e in self.ENG}
        self.cnt = {e: 0 for e in self.ENG}
        self.waited = {e: {} for e in self.ENG}
        self.nd = n_dma_sems
        self.dsem = {q: [nc.alloc_semaphore(f"d_{q}{i}") for i in range(n_dma_sems)] for q in ("sp", "pool")}
        self.dcnt = {q: [0] * n_dma_sems for q in self.dsem}
        self.didx = {q: 0 for q in self.dsem}

    def _collect(self, reads, writes):
        deps = {}
        for b in reads:
            if b.w is not None and deps.get(b.w[0], 0) < b.w[1]:
                deps[b.w[0]] = b.w[1]
            if b.excl:
                for s, v in b.r.items():
                    if deps.get(s, 0) < v:
                        deps[s] = v
        for b in writes:
            if b.w is not None and deps.get(b.w[0], 0) < b.w[1]:
                deps[b.w[0]] = b.w[1]
            for s, v in b.r.items():
                if deps.get(s, 0) < v:
                    deps[s] = v
        return deps

    def _waits(self, eng, deps):
        waits = []
        wd = self.waited[eng]
        for s, v in deps.items():
            if eng == "pe" and s is self.sem["pe"]:
                continue
            if wd.get(s, 0) >= v:
                continue
            wd[s] = v
            waits.append((s, v))
        return waits

    def _commit(self, ev, reads, writes):
        s, v = ev
        for b in reads:
            if b.r.get(s, 0) < v:
                b.r[s] = v
        for b in writes:
            b.w = ev
            b.r = {}

    def op(self, eng, fn, reads=(), writes=()):
        if self.plan:
            return
        waits = self._waits(eng, self._collect(reads, writes))
        self.cnt[eng] += 1
        ev = (self.sem[eng], self.cnt[eng])
        self.prog[eng].append((waits, fn, ev[0], 1))
        self._commit(ev, reads, writes)

    def dma(self, q, fn, reads=(), writes=()):
        if self.plan:
            return
        i = self.didx[q]
        self.didx[q] = (i + 1) % self.nd
        sem = self.dsem[q][i]
        prev = self.dcnt[q][i]
        deps = self._collect(reads, writes)
        if prev > 0 and deps.get(sem, 0) < prev:
            deps[sem] = prev
        waits = self._waits(q, deps)
        self.dcnt[q][i] = prev + 16
        ev = (sem, prev + 16)
        self.prog[q].append((waits, fn, sem, 16))
        self._commit(ev, reads, writes)

    def barrier(self):
        if self.plan:
            return
        for e in self.ENG:
            deps = {}
            for e2 in self.ENG:
                if e2 != e and self.cnt[e2] > 0:
                    deps[self.sem[e2]] = self.cnt[e2]
            for i, s in enumerate(self.dsem["sp"]):
                if self.dcnt["sp"][i] > 0:
                    deps[s] = self.dcnt["sp"][i]
            waits = self._waits(e, deps)
            if waits:
                self.prog[e].append((waits, None, None, 0))

    def finish(self):
        waits = []
        for q in self.dsem:
            for i, s in enumerate(self.dsem[q]):
                if self.dcnt[q][i] > 0:
                    waits.append((s, self.dcnt[q][i]))
        for e in self.ENG:
            if e != "sp" and self.cnt[e] > 0:
                waits.append((self.sem[e], self.cnt[e]))
        self.prog["sp"].append((waits, None, None, 0))

    def emit(self):
        prog = self.prog

        def replay(name, eng):
            for waits, fn, sem, inc in prog[name]:
                for s, v in waits:
                    eng.wait_ge(s, v)
                if fn is not None:
                    fn(eng).then_inc(sem, inc)

        with self.nc.Block() as block:
            @block.tensor
            def _(e):
                replay("pe", e)

            @block.scalar
            def _(e):
                replay("act", e)

            @block.vector
            def _(e):
                replay("dve", e)

            @block.gpsimd
            def _(e):
                replay("pool", e)

            @block.sync
            def _(e):
                replay("sp", e)


def build_program():
    nc = bass.Bass("TRN2", target_bir_lowering=False)
    S = Sched(nc)

    def din(name, shape, dt=F32):
        return nc.dram_tensor(name, list(shape), dt, kind="ExternalInput").ap()

    def dout(name, shape):
        return nc.dram_tensor(name, list(shape), F32, kind="ExternalOutput").ap()

    D = dict(
        xin=din("xin", [NTOK, 1024]), cvT=din("cvT", [128, 8, 2]),
        cckv=din("cckv", [2, 512, 256]), ckr=din("ckr", [2, 512, 32]), s0=din("s0", [2, 2, 4, 64, 128]),
        ngT=din("ngT", [128, 2, 8]), bmT=din("bmT", [128, 2, 24]), fgT=din("fgT", [128, 8]),
        gqT=din("gqT", [128, 2, 3]), gkv=din("gkv", [512]), rdl=din("rdl", [16]),
        w_mod=din("w_mod", [2, 1024, 3072]), w_in=din("w_in", [2, 1024, 6816]),
        w_rqd=din("w_rqd", [2, 1024, 512]), w_krp=din("w_krp", [2, 1024, 192]),
        w_qu2=din("w_qu2", [2, 384, 1536]), w_kvu=din("w_kvu", [2, 256, 1024]),
        w_br=din("w_br", [2, 3, 512, 1024]), w_out=din("w_out", [2, 1024, 1024]),
        ident=din("ident", [128, 128]), cwsw=din("cwsw", [128, 256], BF16),
        dftc0=din("dftc0", [1024, 1024], BF16), dfts0=din("dfts0", [1024, 1024], BF16),
        dftc1=din("dftc1", [512, 512], BF16), dfts1=din("dfts1", [512, 512], BF16),
        rope0=din("rope0", [32, 2, 1024]), rope1=din("rope1", [32, 2, 512]),
        mku0=din("mku0", [5, 1536], BF16), mkw0=din("mkw0", [5, 1024], BF16),
        mku1=din("mku1", [5, 512], BF16), mkw1=din("mkw1", [5, 512], BF16),
        keep=din("keep", [128, 32]), rc=din("rc", [128, 4, 128]), ez=din("ez", [128, 2]),
        y=dout("y", [NTOK, 1024]), o_ckv=dout("o_ckv", [2, NTOK, 256]), o_kr=dout("o_kr", [2, NTOK, 32]),
        o_ret=dout("o_ret", [2, 6, 4, 128, 128]),
    )

    def T(name, shape, dt):
        return nc.alloc_sbuf_tensor("sb_" + name, list(shape), dt), Buf(name)

    xT, XT = T("xT", [128, 8, 1024], F32)
    hT, HT = T("hT", [128, 8, 1024], BF16)
    brT, BRT = T("brT", [128, 12, 1024], BF16)
    slots = [T(f"ws{i}", [128, 2048], BF16) for i in range(NSLOT)]
    ident, IDENT = T("ident", [128, 128], F32)
    onesb, ONESB = T("onesb", [128, 128], BF16)
    cwsw, CWSW = T("cwsw", [128, 256], BF16)
    rc, RC = T("rc", [128, 4, 128], F32)
    ez, EZ = T("ez", [128, 2], F32)
    lg, LG = T("lg", [128, 16], F32)
    lgcol, LGCOL = T("lgcol", [128, 8], F32)
    Dm, DM = T("Dm", [128, 8, 128], F32)
    Xi, XI = T("Xi", [128, 8, 128], F32)
    Zt, ZT = T("Zt", [128, 2, 4, 2], F32)
    gC, GC = T("gC", [128, 16], F32)
    keep, KEEP = T("keep", [128, 2, 2, 8], F32)
    modT, MODT = T("modT", [128, 2, 24, 2], F32)
    g1, G1 = T("g1", [128, 2, 2, 8], F32)
    ngT, NGT = T("ngT", [128, 2, 8], F32)
    bmT, BMT = T("bmT", [128, 2, 24], F32)
    fgT, FGT = T("fgT", [128, 8], F32)
    gqT, GQT = T("gqT", [128, 2, 3], F32)
    gkv, GKV = T("gkv", [128, 2, 256], F32)
    cvT, CVT = T("cvT", [128, 8, 2], F32)
    scv, SCV = T("scv", [128, 8, 2], BF16)
    epsc, EPSC = T("epsc", [128, 1], F32)
    tmp16, TMP16 = T("tmp16", [128, 16], F32)
    arena = nc.alloc_sbuf_tensor("arena", [128, ARENA // 2], BF16)
    ps = nc.alloc_psum_tensor("ps", [128, 4096], F32)
    PSB = [Buf(f"ps{i}", excl=True) for i in range(8)]
    st = {"rot": 0, "aoff": 0}

    def bank(i):
        return ps[:, i * 512:(i + 1) * 512]

    def nb():
        i = st["rot"] % st.get("nrot", 6)
        st["rot"] = i + 1
        return i

    def areset():
        inh = st.setdefault("inh", {})
        for b in st.setdefault("pbufs", []):
            if b.w is not None and inh.get(b.w[0], 0) < b.w[1]:
                inh[b.w[0]] = b.w[1]
            for s_, v_ in b.r.items():
                if inh.get(s_, 0) < v_:
                    inh[s_] = v_
        st["pbufs"] = []
        st["aoff"] = 0

    def aget(shape, dt, name=""):
        assert shape[0] == 128
        fs = list(shape[1:])
        n = int(np.prod(fs))
        nbytes = n * (4 if dt == F32 else 2)
        off = st["aoff"]
        st["aoff"] = off + ((nbytes + 31) // 32) * 32
        assert st["aoff"] <= ARENA, (name, st["aoff"])
        a = arena[:, off // 2:(off + nbytes) // 2]
        if dt == F32:
            a = a.bitcast(F32)
        if len(fs) == 2:
            a = a.rearrange("p (a b) -> p a b", b=fs[1])
        elif len(fs) == 3:
            a = a.rearrange("p (a b c) -> p a b c", b=fs[1], c=fs[2])
        nbuf = Buf(name)
        nbuf.r = dict(st.get("inh", {}))
        st.setdefault("pbufs", []).append(nbuf)
        return a, nbuf

    def mm(out, lhsT, rhs, start, stop, rd, wr):
        S.op("pe", lambda e: e.matmul(out, lhsT=lhsT, rhs=rhs, start=start, stop=stop), rd, wr)

    def tr(out, in_, rd, wr):
        S.op("pe", lambda e: e.transpose(out, in_, ident[:]), list(rd) + [IDENT], wr)

    def act(out, in_, func, rd, wr, **kw):
        S.op("act", lambda e: e.activation(out=out, in_=in_, func=func, **kw), rd, wr)

    def tt(out, in0, in1, op, rd, wr):
        S.op("dve", lambda e: e.tensor_tensor(out=out, in0=in0, in1=in1, op=op), rd, wr)

    def ts(out, in0, s1, op0, rd, wr, s2=None, op1=None):
        if op1 is None:
            S.op("dve", lambda e: e.tensor_scalar(out=out, in0=in0, scalar1=s1, scalar2=None, op0=op0), rd, wr)
        else:
            S.op("dve", lambda e: e.tensor_scalar(out=out, in0=in0, scalar1=s1, scalar2=s2, op0=op0, op1=op1), rd, wr)

    def stt(out, in0, scalar, in1, op0, op1, rd, wr):
        S.op("dve", lambda e: e.scalar_tensor_tensor(out=out, in0=in0, scalar=scalar, in1=in1, op0=op0, op1=op1), rd, wr)

    def pool_ts(out, in0, s1, op0, rd, wr):
        S.op("pool", lambda e: e.tensor_scalar(out=out, in0=in0, scalar1=s1, scalar2=None, op0=op0), rd, wr)

    def pool_tt(out, in0, in1, op, rd, wr):
        S.op("pool", lambda e: e.tensor_tensor(out=out, in0=in0, in1=in1, op=op), rd, wr)

    def cp(out, in_, rd, wr):
        S.op("dve", lambda e: e.tensor_copy(out=out, in_=in_), rd, wr)

    def recip(out, in_, rd, wr):
        S.op("dve", lambda e: e.reciprocal(out=out, in_=in_), rd, wr)

    def rsum(out, in_, rd, wr):
        S.op("dve", lambda e: e.tensor_reduce(out=out, in_=in_, axis=AX.X, op=ALU.add), rd, wr)

    def mset(ap, val, wr):
        S.op("dve", lambda e: e.memset(ap, val), (), wr)

    def dma(out, in_, rd, wr, q="sp"):
        S.dma(q, lambda e: e.dma_start(out=out, in_=in_), rd, wr)

    class WS:
        def __init__(self):
            self.specs = []
            self.i = 0
            self.issued = 0

        def get(self, src, kc, ncols):
            if S.plan:
                self.specs.append((src, kc, ncols))
                t, b = slots[0]
                return t[:, 0:kc * ncols].rearrange("p (k c) -> p k c", c=ncols), b
            while self.issued < min(len(self.specs), self.i + PF + 1):
                j = self.issued
                s_src, s_kc, s_nc = self.specs[j]
                t, b = slots[j % NSLOT]
                dma(t[:, 0:s_kc * s_nc].rearrange("p (k c) -> p k c", c=s_nc), s_src, (), [b], q="pool")
                self.issued += 1
            t, b = slots[self.i % NSLOT]
            self.i += 1
            return t[:, 0:kc * ncols].rearrange("p (k c) -> p k c", c=ncols), b

    ws = WS()

    def wsrc(ap2d, c0, ncols):
        return ap2d.rearrange("(k p) c -> p k c", p=128)[:, :, c0:c0 + ncols]

    def body():
        st["rot"] = 0
        st["aoff"] = 0
        st["inh"] = {}
        st["pbufs"] = []
        dma(ident[:], D["ident"], (), [IDENT])
        dma(cwsw[:], D["cwsw"], (), [CWSW])
        dma(rc[:], D["rc"], (), [RC])
        dma(ez[:], D["ez"], (), [EZ])
        dma(keep[:], D["keep"].rearrange("p (a b c) -> p a b c", a=2, b=2), (), [KEEP])
        dma(ngT[:], D["ngT"], (), [NGT])
        dma(bmT[:], D["bmT"], (), [BMT])
        dma(fgT[:], D["fgT"], (), [FGT])
        dma(gqT[:], D["gqT"], (), [GQT])
        dma(cvT[:], D["cvT"], (), [CVT])
        dma(gkv[:], D["gkv"].partition_broadcast(128).rearrange("p (a b) -> p a b", a=2), (), [GKV])
        dma(lg[:], D["rdl"].partition_broadcast(128), (), [LG])
        ck(0.2)
        mset(onesb[:], 1.0, [ONESB])
        mset(epsc[:], EPS, [EPSC])
        act(lg[:], lg[:], AF.Exp, [LG], [LG], scale=-1.0)
        act(lg[:], lg[:], AF.Ln, [LG], [LG], bias=1.0)
        ts(lg[:], lg[:], -1.0, ALU.mult, [LG], [LG])
        ck(0.4)
        lgv = lg[:].rearrange("p (l d h) -> p l d h", l=2, d=2)
        lgc = lgcol[:].rearrange("p (l h) -> p l h", l=2)
        cp(lgc[0:64], lgv[0:64, :, 0, :], [LG], [LGCOL])
        cp(lgc[64:128], lgv[64:128, :, 1, :], [LG], [LGCOL])
        act(gC[:], lg[:], AF.Exp, [LG], [GC], scale=128.0)
        ck(0.6)
        for l in range(2):
            for h in range(4):
                jf, jb, j = l * 8 + h, l * 8 + 4 + h, l * 4 + h
                ts(Dm[:, j, :], rc[:, 0, :], lg[:, jf:jf + 1], ALU.mult, [RC, LG], [DM])
                stt(Dm[:, j, :], rc[:, 1, :], lg[:, jb:jb + 1], Dm[:, j, :], ALU.mult, ALU.add, [RC, LG, DM], [DM])
                act(Dm[:, j, :], Dm[:, j, :], AF.Exp, [DM], [DM])
                tt(Dm[:, j, :], Dm[:, j, :], rc[:, 2, :], ALU.mult, [DM, RC], [DM])
                ck(0.7)
                act(Xi[:, j, :], rc[:, 3, :], AF.Exp, [RC, LGCOL], [XI], scale=lgcol[:, j:j + 1])
                ck(0.8)
                for d in range(2):
                    jj = l * 8 + d * 4 + h
                    act(Zt[:, l, h, d:d + 1], ez[:, d:d + 1], AF.Exp, [EZ, LG], [ZT], scale=lg[:, jj:jj + 1])
                ck(0.9)
        ck(0.95)
        ts(Zt[:], Zt[:], 0.125, ALU.mult, [ZT], [ZT])
        ck(0.97)
        ck(1)
        act(scv[:], cvT[:], AF.Silu, [CVT], [SCV])

        def emit_mod(l, b0=0, b1=12, with_g1=True):
            for blk in range(b0, b1):
                w, WB = ws.get(wsrc(D["w_mod"][l], blk * 256, 256), 8, 256)
                for fc in range(2):
                    f = blk * 2 + fc
                    for k in range(8):
                        mm(bank(7)[:, f * 2:f * 2 + 2], w[:, k, fc * 128:fc * 128 + 128], scv[:, k, :], k == 0, k == 7,
                           [WB, SCV], [PSB[7]])
            f0, f1 = 2 * b0, 2 * b1
            tt(modT[:, l, f0:f1, :], bank(7)[:, f0 * 2:f1 * 2].rearrange("p (f j) -> p f j", j=2),
               bmT[:, l, f0:f1].unsqueeze(2).broadcast_to([128, f1 - f0, 2]), ALU.add, [PSB[7], BMT], [MODT])
            if with_g1:
                for j in range(2):
                    ts(g1[:, l, j, :], modT[:, l, 8:16, j], 1.0, ALU.add, [MODT], [G1])
                    tt(g1[:, l, j, :], g1[:, l, j, :], ngT[:, l, :], ALU.mult, [G1, NGT], [G1])

        emit_mod(0, 0, 8)
        ck(2)
        for j, (tok0, TT, NP) in enumerate(JOBS):
            NC, NTG, NK = TT // 128, TT // 512, NP + TT
            NKT = NK // 128
            areset()
            xs = [aget([128, 1024], F32, f"xs{i}") for i in range(2)]
            for n in range(NC):
                xa, XA = xs[n % 2]
                dma(xa, D["xin"][tok0 + n * 128:tok0 + (n + 1) * 128, :], (), [XA])
                for kq in range(2):
                    b = nb()
                    for c in range(4):
                        k = kq * 4 + c
                        tr(bank(b)[:, c * 128:(c + 1) * 128], xa[:, k * 128:(k + 1) * 128], [XA], [PSB[b]])
                    o = xT[:, kq * 4:kq * 4 + 4, n * 128:(n + 1) * 128]
                    i_ = bank(b).rearrange("p (c t) -> p c t", c=4)
                    if kq == 0:
                        cp(o, i_, [PSB[b]], [XT])
                    else:
                        act(o, i_, AF.Copy, [PSB[b]], [XT])

            ck(3)

            def norm_stats(rs, RS, sq, tgsl):
                b = nb()
                for k in range(8):
                    sa, SA = sq[k % 2]
                    act(sa, xT[:, k, tgsl], AF.Square, [XT], [SA])
                    mm(bank(b), onesb[:], sa, k == 0, k == 7, [ONESB, SA], [PSB[b]])
                act(rs, bank(b), AF.Ln, [PSB[b], EPSC], [RS], scale=1.0 / 1024, bias=epsc[:, 0:1])
                act(rs, rs, AF.Exp, [RS], [RS], scale=-0.5)

            for l in range(2):
                W_in = D["w_in"][l]
                areset()
                sq = [aget([128, 512], BF16, f"sq{i}") for i in range(2)]
                rsb = [aget([128, 512], F32, f"rs{i}") for i in range(NTG)]
                tf = [aget([128, 512], F32, f"tf{i}") for i in range(2)]
                for tg in range(NTG):
                    norm_stats(rsb[tg][0], rsb[tg][1], sq, slice(tg * 512, (tg + 1) * 512))
                for tg in range(NTG):
                    tgsl = slice(tg * 512, (tg + 1) * 512)
                    rs, RS = rsb[tg]
                    for k in range(8):
                        ta, TA = tf[k % 2]
                        stt(ta, xT[:, k, tgsl], g1[:, l, j, k:k + 1], rs, ALU.mult, ALU.mult, [XT, G1, RS], [TA])
                        act(hT[:, k, tgsl], ta, AF.Identity, [TA, MODT], [HT], bias=modT[:, l, k, j:j + 1], scale=1.0)
                if j == 0 and l == 0:
                    emit_mod(0, 8, 12, with_g1=False)

                ck(4)
                areset()
                Kz, KZ = aget([128, NC, 4, 128], BF16, "Kz")
                Vr, VR = aget([128, NC, 512], BF16, "Vr")
                Vc, VC = aget([128, NC, 512], BF16, "Vc")
                vs4, VS4 = aget([128, 4], F32, "vs4")
                QT = [aget([128, TT], BF16, f"QT{i}") for i in range(2)]
                QTs = [aget([128, TT], BF16, f"QTs{i}") for i in range(2)]
                KT = [aget([128, TT], BF16, f"KT{i}") for i in range(2)]
                gR = [aget([128, TT], BF16, f"gR{i}") for i in range(2)]
                R, RB = aget([128, NC + 2, 128], F32, "R")
                RBB = Buf("Rb")
                RBB.r = dict(RB.r)
                st["pbufs"].append(RBB)
                Sin, SIN = aget([128, NC, 128], F32, "Sin")
                SINB = Buf("Sinb")
                SINB.r = dict(SIN.r)
                st["pbufs"].append(SINB)
                Scs = [aget([128, NC, 128], BF16, f"Sc{i}") for i in range(2)]
                srs, SRS = aget([128, NC], F32, "srs")
                dk, DKB = aget([128, NC], F32, "dk")
                AT = [aget([128, 512], BF16, f"AT{i}") for i in range(2)]
                sqo2 = [aget([128, 512], BF16, f"sqo{i}") for i in range(2)]
                rso2 = [aget([128, 512], F32, f"rso{i}") for i in range(2)]
                to2 = [aget([128, 512], F32, f"to{i}") for i in range(2)]
                wq = [ws.get(wsrc(D["w_rqd"][l], i * 256, 256), 8, 256) for i in range(2)]
                wk, WK = ws.get(wsrc(W_in, 256, 256), 8, 256)
                wv = [ws.get(wsrc(W_in, 512 + i * 256, 256), 8, 256) for i in range(2)]
                wz = [ws.get(wsrc(W_in, 1024 + i * 256, 256), 8, 256) for i in range(2)]
                for n in range(NC):
                    nsl = slice(n * 128, (n + 1) * 128)
                    bK = nb()
                    for k in range(8):
                        mm(bank(bK)[:, 0:256], hT[:, k, nsl], wk[:, k, :], k == 0, k == 7, [HT, WK], [PSB[bK]])
                    bV = nb()
                    for i in range(2):
                        for k in range(8):
                            mm(bank(bV)[:, i * 256:(i + 1) * 256], hT[:, k, nsl], wv[i][0][:, k, :], k == 0, k == 7,
                               [HT, wv[i][1]], [PSB[bV]])
                    tt(Kz[:, n].rearrange("p h (d e) -> p h d e", d=2),
                       bank(bK)[:, 0:256].rearrange("p (h e) -> p h e", h=4).unsqueeze(2).broadcast_to([128, 4, 2, 64]),
                       Zt[:, l].unsqueeze(3).broadcast_to([128, 4, 2, 64]), ALU.mult, [PSB[bK], ZT], [KZ])
                    act(Vr[:, n, :], bank(bV), AF.Copy, [PSB[bV]], [VR])
                    rsum(vs4, bank(bV).rearrange("p (h e) -> p h e", h=4), [PSB[bV]], [VS4])
                    ts(vs4, vs4, -1.0 / 128, ALU.mult, [VS4], [VS4])
                    tt(Vc[:, n, :].rearrange("p (h e) -> p h e", h=4), bank(bV).rearrange("p (h e) -> p h e", h=4),
                       vs4.unsqueeze(2).broadcast_to([128, 4, 128]), ALU.add, [PSB[bV], VS4], [VC])
                def head_vars(h):
                    bi = h % 2
                    return (bi,) + QT[bi] + QTs[bi] + KT[bi] + gR[bi] + wq[h // 2] + wz[h // 2] + ((h % 2) * 128,) + Scs[bi]

                def stage_a1(h):
                    bi, qt, QTB, qs, QSB, kt_, KTB, gr, GRB, wqh, WQH, wzh, WZH, co, Sc, SCB = head_vars(h)
                    for tg in range(NTG):
                        tgsl = slice(tg * 512, (tg + 1) * 512)
                        b = nb()
                        for k in range(8):
                            mm(bank(b), wqh[:, k, co:co + 128], hT[:, k, tgsl], k == 0, k == 7, [WQH, HT], [PSB[b]])
                        act(qt[0:64, tgsl], bank(b)[0:64, :], AF.Copy, [PSB[b]], [QTB])
                        tt(qs[:, tgsl].rearrange("p (c t) -> p c t", c=4), bank(b).rearrange("p (c t) -> p c t", c=4),
                           Xi[:, l * 4 + h, :].unsqueeze(1).broadcast_to([128, 4, 128]), ALU.mult, [PSB[b], XI], [QSB])
                        b = nb()
                        for k in range(8):
                            mm(bank(b)[0:64, :], wk[:, k, h * 64:(h + 1) * 64], hT[:, k, tgsl], k == 0, k == 7, [WK, HT], [PSB[b]])
                        act(kt_[0:64, tgsl], bank(b)[0:64, :], AF.Copy, [PSB[b]], [KTB])

                def stage_a2(h):
                    bi, qt, QTB, qs, QSB, kt_, KTB, gr, GRB, wqh, WQH, wzh, WZH, co, Sc, SCB = head_vars(h)
                    dma(R[0:64, 0, :], D["s0"][l, 0, h], (), [RB])
                    dma(R[64:128, NC, :], D["s0"][l, 1, h], (), [RBB])
                    kvb = []
                    for n in range(NC):
                        if n % 4 == 0:
                            kvb.append(6 + (n // 4))
                        b = kvb[-1]
                        mm(bank(b)[:, (n % 4) * 128:(n % 4 + 1) * 128], Kz[:, n, h, :], Vr[:, n, h * 128:(h + 1) * 128],
                           True, True, [KZ, VR], [PSB[b]])
                    jf, jb = l * 8 + h, l * 8 + 4 + h
                    ts(dk[0:64, :], keep[0:64, j, 0, 0:NC], gC[0:64, jf:jf + 1], ALU.mult, [KEEP, GC], [DKB])
                    ts(dk[64:128, :], keep[64:128, j, 1, 0:NC], gC[64:128, jb:jb + 1], ALU.mult, [KEEP, GC], [DKB])
                    for n in range(NC):
                        b = kvb[n // 4]
                        stt(R[0:64, n + 1, :], R[0:64, n, :], dk[0:64, n:n + 1], bank(b)[0:64, (n % 4) * 128:(n % 4 + 1) * 128],
                            ALU.mult, ALU.add, [RB, DKB, PSB[b]], [RB])
                    for n in range(NC - 1, -1, -1):
                        b = kvb[n // 4]
                        stt(R[64:128, n, :], R[64:128, n + 1, :], dk[64:128, n:n + 1], bank(b)[64:128, (n % 4) * 128:(n % 4 + 1) * 128],
                            ALU.mult, ALU.add, [RBB, DKB, PSB[b]], [RBB])
                    tt(Sin[0:64], R[0:64, 0:NC, :], keep[0:64, j, 0, 0:NC].unsqueeze(2).broadcast_to([64, NC, 128]), ALU.mult,
                       [RB, KEEP], [SIN])
                    tt(Sin[64:128], R[64:128, 1:NC + 1, :], keep[64:128, j, 1, 0:NC].unsqueeze(2).broadcast_to([64, NC, 128]), ALU.mult,
                       [RBB, KEEP], [SINB])
                    rsum(srs, Sin, [SIN, SINB], [SRS])
                    ts(srs, srs, -1.0 / 128, ALU.mult, [SRS], [SRS])
                    tt(Sc, Sin, srs.unsqueeze(2).broadcast_to([128, NC, 128]), ALU.add, [SIN, SINB, SRS], [SCB])
                    ns = NC // 2
                    dma(D["o_ret"][l, SLOT0[j]:SLOT0[j] + ns, h, 0:64, :].rearrange("s p e -> p s e"),
                        R[0:64, 2:NC + 2, :].rearrange("p (s two) e -> p s two e", two=2)[:, :, 0, :], [RB], ())
                    dma(D["o_ret"][l, SLOT0[j]:SLOT0[j] + ns, h, 64:128, :].rearrange("s p e -> p s e"),
                        R[64:128, 0:NC, :].rearrange("p (s two) e -> p s two e", two=2)[:, :, 0, :], [RBB], ())

                def stage_b(h):
                    bi, qt, QTB, qs, QSB, kt_, KTB, gr, GRB, wqh, WQH, wzh, WZH, co, Sc, SCB = head_vars(h)
                    tgs = list(range(NTG))
                    bS, bO, bv = {}, {}, {}
                    for tg in tgs:
                        bS[tg] = nb()
                        for c4 in range(4):
                            n = tg * 4 + c4
                            nsl = slice(n * 128, (n + 1) * 128)
                            mm(bank(bS[tg])[:, c4 * 128:(c4 + 1) * 128], kt_[0:64, nsl], qt[0:64, nsl], True, True, [KTB, QTB], [PSB[bS[tg]]])
                    for tg in tgs:
                        at, ATB = AT[tg % 2]
                        tt(at.rearrange("p (c t) -> p c t", c=4), bank(bS[tg]).rearrange("p (c t) -> p c t", c=4),
                           Dm[:, l * 4 + h, :].unsqueeze(1).broadcast_to([128, 4, 128]), ALU.mult, [PSB[bS[tg]], DM], [ATB])
                    for tg in tgs:
                        at, ATB = AT[tg % 2]
                        bO[tg] = nb()
                        for c4 in range(4):
                            n = tg * 4 + c4
                            nsl = slice(n * 128, (n + 1) * 128)
                            osl = bank(bO[tg])[:, c4 * 128:(c4 + 1) * 128]
                            mm(osl, Vc[:, n, h * 128:(h + 1) * 128], at[:, c4 * 128:(c4 + 1) * 128], True, False, [VC, ATB], [PSB[bO[tg]]])
                            mm(osl, Sc[:, n, :], qs[:, nsl], False, True, [SCB, QSB], [PSB[bO[tg]]])
                    for tg in tgs:
                        act(sqo2[tg % 2][0], bank(bO[tg]), AF.Square, [PSB[bO[tg]]], [sqo2[tg % 2][1]])
                    for tg in tgs:
                        bv[tg] = nb()
                        mm(bank(bv[tg]), onesb[:], sqo2[tg % 2][0], True, True, [ONESB, sqo2[tg % 2][1]], [PSB[bv[tg]]])
                    for tg in tgs:
                        r_, RSB_ = rso2[tg % 2]
                        act(r_, bank(bv[tg]), AF.Ln, [PSB[bv[tg]], EPSC], [RSB_], scale=1.0 / 128, bias=epsc[:, 0:1])
                    for tg in tgs:
                        r_, RSB_ = rso2[tg % 2]
                        act(r_, r_, AF.Exp, [RSB_], [RSB_], scale=-0.5)
                    for tg in tgs:
                        r_, RSB_ = rso2[tg % 2]
                        t_, TOB_ = to2[tg % 2]
                        tt(t_, bank(bO[tg]), r_, ALU.mult, [PSB[bO[tg]], RSB_], [TOB_])
                    for tg in tgs:
                        tgsl = slice(tg * 512, (tg + 1) * 512)
                        t_, TOB_ = to2[tg % 2]
                        tt(brT[:, h, tgsl], t_, brT[:, 4 + h, tgsl], ALU.mult, [TOB_, BRT], [BRT])

                for h in range(4):
                    wzh, WZH = wz[h // 2]
                    co = (h % 2) * 128
                    for tg in range(NTG):
                        tgsl = slice(tg * 512, (tg + 1) * 512)
                        b = nb()
                        for k in range(8):
                            mm(bank(b), wzh[:, k, co:co + 128], hT[:, k, tgsl], k == 0, k == 7, [WZH, HT], [PSB[b]])
                        act(brT[:, 4 + h, tgsl], bank(b), AF.Silu, [PSB[b]], [BRT])
                stage_a1(0)
                stage_a2(0)
                stage_a1(1)
                for h in range(4):
                    stage_b(h)
                    if h + 1 < 4:
                        stage_a2(h + 1)
                    if h + 2 < 4:
                        stage_a1(h + 2)

                ck(5)
                if j == 0 and l == 0:
                    emit_mod(1)
                areset()
                kvu, KVU = aget([128, 2, 1024], BF16, "kvu")
                ckvT, CKVT = aget([128, 2, NK], BF16, "ckvT")
                KRM, KRMB = aget([128, NK], BF16, "KRM")
                Kaug = [aget([128, NK], BF16, f"Kaug{i}") for i in range(2)]
                Qaug = [aget([128, 512], BF16, f"Qaug{i}") for i in range(2)]
                qlT, QLT = aget([128, 3, TT], BF16, "qlT")
                gM = [aget([128, TT], BF16, f"gM{i}") for i in range(2)]
                Vx, VX = aget([128, NKT, 4, 128], BF16, "Vx")
                PT = [aget([128, 512], BF16, f"PT{i}") for i in range(3)]
                stg = [aget([128, 288], F32, f"stg{i}") for i in range(4)]
                kst, KST = aget([128, 96], F32, "kst")
                t1, T1 = aget([128, 512], F32, "t1")
                t2, T2 = aget([128, 512], F32, "t2")
                t3, T3 = aget([128, 512], F32, "t3")
                rq, RQ = aget([128, 512], F32, "rq")
                sqm = [aget([128, 512], BF16, f"sqm{i}") for i in range(4)]
                rden, RDEN = rq, RQ
                rope, ROPE = aget([128, 2, TT], F32, "rope")
                ssq, SSQ = aget([128, NC], F32, "ssq")
                rsd, RSD = aget([128, NC], F32, "rsd")
                dma(kvu, D["w_kvu"][l].rearrange("(k p) c -> p k c", p=128), (), [KVU], q="pool")
                dma(rope[64:96], D[f"rope{j}"], (), [ROPE])
                dma(KRM[96:101, :], D[f"mku{j}"], (), [KRMB])
                for qi in range(2):
                    tg = qi % NTG
                    dma(Qaug[qi][0][96:101, :], D[f"mkw{j}"][:, tg * 512:(tg + 1) * 512], (), [Qaug[qi][1]])
                mset(kst, 0.0, [KST])
                mset(ssq, 0.0, [SSQ])
                vx5 = Vx.rearrange("p k (a two) e -> p k a two e", two=2)
                mset(vx5[:, :, :, 0, 64:128], 1.0, [VX])
                mset(vx5[:, :, :, 1, 0:64], 1.0, [VX])
                wkl, WKL = ws.get(wsrc(W_in, 1920, 256), 8, 256)
                wkr, WKR = ws.get(wsrc(D["w_krp"][l], 0, 192), 8, 192)
                pairs = [[n0, n0 + 1] for n0 in range(0, NC, 2)]
                bk, btk = {}, {}

                def m1_proj(pair):
                    for n in pair:
                        nsl = slice(n * 128, (n + 1) * 128)
                        bk[n] = nb()
                        for k in range(8):
                            mm(bank(bk[n])[:, 0:256], hT[:, k, nsl], wkl[:, k, :], k == 0, k == 7, [HT, WKL], [PSB[bk[n]]])
                        for k in range(8):
                            mm(bank(bk[n])[:, 256:288], hT[:, k, nsl], wkr[:, k, 64:96], k == 0, k == 7, [HT, WKR], [PSB[bk[n]]])

                def m1_norm(pair):
                    for n in pair:
                        sa, SA = sqm[n % 4]
                        act(sa[:, 0:256], bank(bk[n])[:, 0:256], AF.Square, [PSB[bk[n]]], [SA, SSQ], accum_out=ssq[:, n:n + 1])
                    for n in pair:
                        act(rsd[:, n:n + 1], ssq[:, n:n + 1], AF.Ln, [SSQ, EPSC], [RSD], scale=1.0 / 256, bias=epsc[:, 0:1])
                    for n in pair:
                        act(rsd[:, n:n + 1], rsd[:, n:n + 1], AF.Exp, [RSD], [RSD], scale=-0.5)
                    for n in pair:
                        sg, SG = stg[n % 4]
                        stt(sg[:, 0:256], bank(bk[n])[:, 0:256], rsd[:, n:n + 1], gkv[:, l, :], ALU.mult, ALU.mult, [PSB[bk[n]], RSD, GKV], [SG])
                        cp(sg[:, 256:288], bank(bk[n])[:, 256:288], [PSB[bk[n]]], [SG])
                    for n in pair:
                        sg, SG = stg[n % 4]
                        dma(D["o_ckv"][l, tok0 + n * 128:tok0 + (n + 1) * 128, :], sg[:, 0:256], [SG], ())
                        dma(D["o_kr"][l, tok0 + n * 128:tok0 + (n + 1) * 128, :], sg[:, 256:288], [SG], ())

                def m1_tr(pair):
                    for n in pair:
                        sg, SG = stg[n % 4]
                        btk[n] = nb()
                        for c2 in range(2):
                            tr(bank(btk[n])[:, c2 * 128:(c2 + 1) * 128], sg[:, c2 * 128:(c2 + 1) * 128], [SG], [PSB[btk[n]]])
                    for n in pair:
                        act(ckvT[:, :, NP + n * 128:NP + (n + 1) * 128], bank(btk[n])[:, 0:256].rearrange("p (c t) -> p c t", c=2),
                            AF.Copy, [PSB[btk[n]]], [CKVT])

                m1_proj(pairs[0])
                for pi, pair in enumerate(pairs):
                    m1_norm(pair)
                    if pi + 1 < len(pairs):
                        m1_proj(pairs[pi + 1])
                    m1_tr(pair)
                for pt in range(NP // 128):
                    sg, SG = stg[pt % 4]
                    dma(sg[:, 0:256], D["cckv"][l, pt * 128:(pt + 1) * 128, :], (), [SG])
                    dma(kst[:, 64:96], D["ckr"][l, pt * 128:(pt + 1) * 128, :], (), [KST])
                    bt = nb()
                    for c2 in range(2):
                        tr(bank(bt)[:, c2 * 128:(c2 + 1) * 128], sg[:, c2 * 128:(c2 + 1) * 128], [SG], [PSB[bt]])
                    tr(bank(bt)[0:96, 256:384], kst[:, 0:96], [KST], [PSB[bt]])
                    act(ckvT[:, :, pt * 128:(pt + 1) * 128], bank(bt)[:, 0:256].rearrange("p (c t) -> p c t", c=2),
                        AF.Copy, [PSB[bt]], [CKVT])
                    cp(KRM[64:96, pt * 128:(pt + 1) * 128], bank(bt)[64:96, 256:384], [PSB[bt]], [KRMB])
                for tg in range(NTG):
                    tgsl = slice(tg * 512, (tg + 1) * 512)
                    bA = nb()
                    for k in range(8):
                        mm(bank(bA)[0:96, :], wkr[:, k, 0:96], hT[:, k, tgsl], k == 0, k == 7, [WKR, HT], [PSB[bA]])
                    bB = nb()
                    for k in range(8):
                        mm(bank(bB)[0:96, :], wkr[:, k, 96:192], hT[:, k, tgsl], k == 0, k == 7, [WKR, HT], [PSB[bB]])
                    tt(t1[64:96], bank(bA)[64:96, :], rope[64:96, 0, tgsl], ALU.mult, [PSB[bA], ROPE], [T1])
                    tt(t2[64:96], bank(bB)[64:96, :], rope[64:96, 1, tgsl], ALU.mult, [PSB[bB], ROPE], [T2])
                    tt(KRM[64:96, NP + tg * 512:NP + (tg + 1) * 512], t1[64:96], t2[64:96], ALU.add, [T1, T2], [KRMB])
                act(Kaug[0][0][64:101, :], KRM[64:101, :], AF.Copy, [KRMB], [Kaug[0][1]])
                cp(Kaug[1][0][64:101, :], KRM[64:101, :], [KRMB], [Kaug[1][1]])
                wq0, WQ0 = ws.get(wsrc(W_in, 1536, 256), 8, 256)
                wq1, WQ1 = ws.get(wsrc(W_in, 1792, 128), 8, 128)
                for tg in range(NTG):
                    tgsl = slice(tg * 512, (tg + 1) * 512)
                    bc = [nb() for _ in range(3)]
                    for c in range(3):
                        for k in range(8):
                            if c < 2:
                                mm(bank(bc[c]), wq0[:, k, c * 128:(c + 1) * 128], hT[:, k, tgsl], k == 0, k == 7, [WQ0, HT], [PSB[bc[c]]])
                            else:
                                mm(bank(bc[c]), wq1[:, k, 0:128], hT[:, k, tgsl], k == 0, k == 7, [WQ1, HT], [PSB[bc[c]]])
                    bs = nb()
                    for c in range(3):
                        sa, SA = sqm[c % 2]
                        act(sa, bank(bc[c]), AF.Square, [PSB[bc[c]]], [SA])
                        mm(bank(bs), onesb[:], sa, c == 0, c == 2, [ONESB, SA], [PSB[bs]])
                    act(rq, bank(bs), AF.Ln, [PSB[bs], EPSC], [RQ], scale=1.0 / 384, bias=epsc[:, 0:1])
                    act(rq, rq, AF.Exp, [RQ], [RQ], scale=-0.5)
                    for c in range(3):
                        stt(qlT[:, c, tgsl], bank(bc[c]), gqT[:, l, c:c + 1], rq, ALU.mult, ALU.mult, [PSB[bc[c]], GQT, RQ], [QLT])
                kvu3 = kvu.rearrange("p k (h x) -> p k h x", h=8)
                wst = {}

                def prep_head(h):
                    c = h // 2
                    ka, KA = Kaug[h % 2]
                    gm, GMB = gM[c % 2]
                    if h % 2 == 0:
                        wst["wqu"] = ws.get(D["w_qu2"][l].rearrange("(c p) x -> p c x", p=128)[:, :, c * 384:(c + 1) * 384], 3, 384)
                    wst[("wqu", h)] = wst["wqu"]
                    for kg in range(NK // 512):
                        b = 3
                        for c2 in range(2):
                            mm(bank(b)[0:64, :], kvu[:, c2, h * 128:h * 128 + 64], ckvT[:, c2, kg * 512:(kg + 1) * 512],
                               c2 == 0, c2 == 1, [KVU, CKVT], [PSB[b]])
                        act(ka[0:64, kg * 512:(kg + 1) * 512], bank(b)[0:64, :], AF.Copy, [PSB[b]], [KA])

                def prep_q(h, tg, bc):
                    tgsl = slice(tg * 512, (tg + 1) * 512)
                    qa, QA = Qaug[bc % 2]
                    wqu, WQU = wst[("wqu", h)]
                    qo = (h % 2) * 192
                    bA = 4
                    for c3 in range(3):
                        mm(bank(bA)[0:96, :], wqu[:, c3, qo:qo + 96], qlT[:, c3, tgsl], c3 == 0, c3 == 2, [WQU, QLT], [PSB[bA]])
                    bB = 5
                    for c3 in range(3):
                        mm(bank(bB)[0:96, :], wqu[:, c3, qo + 96:qo + 192], qlT[:, c3, tgsl], c3 == 0, c3 == 2, [WQU, QLT], [PSB[bB]])
                    act(qa[0:64, :], bank(bA)[0:64, :], AF.Copy, [PSB[bA]], [QA])
                    tt(t1[64:96], bank(bA)[64:96, :], rope[64:96, 0, tgsl], ALU.mult, [PSB[bA], ROPE], [T1])
                    tt(t2[64:96], bank(bB)[64:96, :], rope[64:96, 1, tgsl], ALU.mult, [PSB[bB], ROPE], [T2])
                    tt(qa[64:96, :], t1[64:96], t2[64:96], ALU.add, [T1, T2], [QA])

                def finalize(h, tg, bcur):
                    tgsl = slice(tg * 512, (tg + 1) * 512)
                    par = h % 2
                    po, pd = par * 64, (1 - par) * 64
                    c = h // 2
                    bo = 6 + (bcur % 2)
                    if j == 1:
                        act(rden[po:po + 64], bank(bo)[pd:pd + 64, :], AF.Ln, [PSB[bo]], [RDEN])
                        act(rden[po:po + 64], rden[po:po + 64], AF.Exp, [RDEN], [RDEN], scale=-1.0)
                    else:
                        recip(rden[po:po + 64], bank(bo)[pd:pd + 64, :], [PSB[bo]], [RDEN])
                    tt(t3[po:po + 64], bank(bo)[po:po + 64, :], rden[po:po + 64], ALU.mult, [PSB[bo], RDEN], [T3])
                    tt(brT[po:po + 64, 4 + c, tgsl], t3[po:po + 64], brT[po:po + 64, 8 + c, tgsl], ALU.mult, [T3, BRT], [BRT])

                for c in range(4):
                    if c % 2 == 0:
                        wmz, WMZ = ws.get(wsrc(W_in, 2208 + (c // 2) * 256, 256), 8, 256)
                    for tg in range(NTG):
                        tgsl = slice(tg * 512, (tg + 1) * 512)
                        b = nb()
                        for k in range(8):
                            mm(bank(b), wmz[:, k, (c % 2) * 128:(c % 2 + 1) * 128], hT[:, k, tgsl], k == 0, k == 7, [WMZ, HT], [PSB[b]])
                        act(brT[:, 8 + c, tgsl], bank(b), AF.Silu, [PSB[b]], [BRT])
                bc = 0
                gstep = 0
                st["nrot"] = 3
                for hg in range(2):
                    for kt in range(NKT):
                        b = nb()
                        for c2 in range(2):
                            mm(bank(b)[:, 0:256], ckvT[:, c2, kt * 128:(kt + 1) * 128], kvu3[:, c2, hg * 4:hg * 4 + 4, 64:128],
                               c2 == 0, c2 == 1, [CKVT, KVU], [PSB[b]])
                        bv = bank(b)[:, 0:256].rearrange("p (a two e) -> p a two e", two=2, e=64)
                        act(vx5[:, kt, :, 0, 0:64], bv[:, :, 0, :], AF.Copy, [PSB[b]], [VX])
                        cp(vx5[:, kt, :, 1, 64:128], bv[:, :, 1, :], [PSB[b]], [VX])
                    hblocks = [(h, tg) for h in range(hg * 4, hg * 4 + 4) for tg in range(NTG)]
                    nbk = len(hblocks)
                    prep_head(hblocks[0][0])
                    prep_q(hblocks[0][0], hblocks[0][1], bc)
                    steps = [(bi, kt) for bi in range(nbk) for kt in range(NKT)]
                    sbank = {}

                    def score(si):
                        bi_, kt_s = steps[si]
                        h_s, tg_s = hblocks[bi_]
                        ka, KA = Kaug[h_s % 2]
                        qa, QA = Qaug[(bc + bi_) % 2]
                        sbank[si] = nb()
                        mm(bank(sbank[si]), ka[0:101, kt_s * 128:(kt_s + 1) * 128], qa[0:101, :], True, True,
                           [KA, QA], [PSB[sbank[si]]])
                    LA = 2
                    for si in range(min(LA, len(steps))):
                        score(si)
                    for si, (bi, kt) in enumerate(steps):
                        h, tg = hblocks[bi]
                        bcur = bc + bi
                        bo = 6 + (bcur % 2)
                        pt_, PTB = PT[(gstep + si) % 3]
                        act(pt_, bank(sbank[si]), AF.Exp, [PSB[sbank[si]]], [PTB], scale=SC)
                        if si + LA < len(steps):
                            score(si + LA)
                        mm(bank(bo), Vx[:, kt, h % 4, :], pt_, kt == 0, kt == NKT - 1, [VX, PTB], [PSB[bo]])
                        if kt == 0 and bi + 1 < nbk:
                            h2, tg2 = hblocks[bi + 1]
                            if h2 != h:
                                prep_head(h2)
                            prep_q(h2, tg2, bcur + 1)
                        if kt == NKT - 1:
                            finalize(h, tg, bcur)
                    gstep += len(steps)
                    bc += nbk
                st["nrot"] = 6

                areset()
                fuT, FUT = aget([128, 4, TT], BF16, "fuT")
                gF, GFB = aget([128, 4, TT], BF16, "gF")
                AB, ABB = aget([128, NC, 4, 256], BF16, "AB")
                wf = [ws.get(wsrc(W_in, 2720 + i * 256, 256), 8, 256) for i in range(2)]
                for tg in range(NTG):
                    tgsl = slice(tg * 512, (tg + 1) * 512)
                    for g in range(4):
                        b = nb()
                        for k in range(8):
                            mm(bank(b), wf[g // 2][0][:, k, (g % 2) * 128:(g % 2 + 1) * 128], hT[:, k, tgsl], k == 0, k == 7,
                               [wf[g // 2][1], HT], [PSB[b]])
                        act(fuT[:, g, tgsl], bank(b), AF.Copy, [PSB[b]], [FUT])
                wfz = [ws.get(wsrc(W_in, 3232 + i * 256, 256), 8, 256) for i in range(2)]
                for tg in range(NTG):
                    tgsl = slice(tg * 512, (tg + 1) * 512)
                    for g in range(4):
                        b = nb()
                        for k in range(8):
                            mm(bank(b), wfz[g // 2][0][:, k, (g % 2) * 128:(g % 2 + 1) * 128], hT[:, k, tgsl], k == 0, k == 7,
                               [wfz[g // 2][1], HT], [PSB[b]])
                        act(gF[:, g, tgsl], bank(b), AF.Silu, [PSB[b]], [GFB])
                for n in range(NC):
                    nsl = slice(n * 128, (n + 1) * 128)
                    bb = [nb(), nb()]
                    for g in range(4):
                        mm(bank(bb[g // 2])[:, (g % 2) * 256:(g % 2 + 1) * 256], fuT[:, g, nsl], cwsw[:], True, True, [FUT, CWSW], [PSB[bb[g // 2]]])
                    act(AB[:, n, 0:2, :], bank(bb[0]).rearrange("p (g x) -> p g x", g=2), AF.Copy, [PSB[bb[0]]], [ABB])
                    cp(AB[:, n, 2:4, :], bank(bb[1]).rearrange("p (g x) -> p g x", g=2), [PSB[bb[1]]], [ABB])
                for kb in range(TT // 256):
                    wc, WC = ws.get(wsrc(D[f"dftc{j}"], kb * 256, 256), NC, 256)
                    wsn, WSN = ws.get(wsrc(D[f"dfts{j}"], kb * 256, 256), NC, 256)
                    ksl = slice(kb * 256, (kb + 1) * 256)
                    for g in range(4):
                        b = nb()
                        for n in range(NC):
                            mm(bank(b)[:, 0:256], AB[:, n, g, 0:128], wc[:, n, :], n == 0, False, [ABB, WC], [PSB[b]])
                            mm(bank(b)[:, 0:256], AB[:, n, g, 128:256], wsn[:, n, :], False, n == NC - 1, [ABB, WSN], [PSB[b]])
                        tt(brT[:, 8 + g, ksl], bank(b)[:, 0:256], gF[:, g, ksl], ALU.mult, [PSB[b], GFB], [BRT])

                ck(7)
                areset()
                mT, MT = aget([128, 8, TT], BF16, "mT")
                sig = [aget([128, 512], F32, f"sig{i}") for i in range(3)]
                tm = [aget([128, 512], F32, f"tm{i}") for i in range(3)]
                for cpair in range(4):
                    gl = [ws.get(wsrc(W_in, 3744 + n * 1024 + cpair * 256, 256), 8, 256) for n in range(3)]
                    wb = [ws.get(wsrc(D["w_br"][l, n], cpair * 256, 256), 4, 256) for n in range(3)]
                    for cc in range(2):
                        c = cpair * 2 + cc
                        csl = slice(cc * 128, (cc + 1) * 128)
                        for tg in range(NTG):
                            tgsl = slice(tg * 512, (tg + 1) * 512)
                            for n in range(3):
                                bg = nb()
                                for k in range(8):
                                    mm(bank(bg), gl[n][0][:, k, csl], hT[:, k, tgsl], k == 0, k == 7, [gl[n][1], HT], [PSB[bg]])
                                act(sig[n][0], bank(bg), AF.Sigmoid, [PSB[bg]], [sig[n][1]])
                                bp = nb()
                                for k4 in range(4):
                                    mm(bank(bp), wb[n][0][:, k4, csl], brT[:, n * 4 + k4, tgsl], k4 == 0, k4 == 3, [wb[n][1], BRT], [PSB[bp]])
                                tt(tm[n][0], bank(bp), sig[n][0], ALU.mult, [PSB[bp], sig[n][1]], [tm[n][1]])
                            tt(tm[0][0], tm[0][0], tm[1][0], ALU.add, [tm[0][1], tm[1][1]], [tm[0][1]])
                            tt(mT[:, c, tgsl], tm[0][0], tm[2][0], ALU.add, [tm[0][1], tm[2][1]], [MT])
                for cpair in range(4):
                    wo, WO = ws.get(wsrc(D["w_out"][l], cpair * 256, 256), 8, 256)
                    for cc in range(2):
                        c = cpair * 2 + cc
                        csl = slice(cc * 128, (cc + 1) * 128)
                        for tg in range(NTG):
                            tgsl = slice(tg * 512, (tg + 1) * 512)
                            b = nb()
                            for k in range(8):
                                mm(bank(b), wo[:, k, csl], mT[:, k, tgsl], k == 0, k == 7, [WO, MT], [PSB[b]])
                            stt(xT[:, c, tgsl], bank(b), modT[:, l, 16 + c, j:j + 1], xT[:, c, tgsl], ALU.mult, ALU.add,
                                [PSB[b], MODT, XT], [XT])

                ck(8 + l)
            areset()
            sq = [aget([128, 512], BF16, f"fsq{i}") for i in range(2)]
            rsb = [aget([128, 512], F32, f"frs{i}") for i in range(NTG)]
            yT, YT = aget([128, 8, 512], F32, "yT")
            yst = [aget([128, 1024], F32, f"yst{i}") for i in range(2)]
            for tg in range(NTG):
                norm_stats(rsb[tg][0], rsb[tg][1], sq, slice(tg * 512, (tg + 1) * 512))
            for tg in range(NTG):
                tgsl = slice(tg * 512, (tg + 1) * 512)
                rs, RS = rsb[tg]
                for k in range(8):
                    stt(yT[:, k, :], xT[:, k, tgsl], fgT[:, k:k + 1], rs, ALU.mult, ALU.mult, [XT, FGT, RS], [YT])
                for c4 in range(4):
                    n = tg * 4 + c4
                    ya, YA = yst[n % 2]
                    bA, bB = nb(), nb()
                    for k in range(8):
                        bk = bA if k < 4 else bB
                        tr(bank(bk)[:, (k % 4) * 128:(k % 4 + 1) * 128], yT[:, k, c4 * 128:(c4 + 1) * 128], [YT], [PSB[bk]])
                    cp(ya[:, 0:512], bank(bA), [PSB[bA]], [YA])
                    act(ya[:, 512:1024], bank(bB), AF.Copy, [PSB[bB]], [YA])
                    dma(D["y"][tok0 + n * 128:tok0 + (n + 1) * 128, :], ya, [YA], ())

    S.plan = True
    try:
        body()
    except _Stop:
        pass
    S.plan = False
    try:
        body()
    except _Stop:
        pass
    print('ops', S.cnt, {q: sum(v) // 16 for q, v in S.dcnt.items()}, flush=True)
    S.finish()
    S.emit()
    return nc


def _consts(is_sample):
    bf = ml_dtypes.bfloat16
    c = {}
    c["ident"] = np.eye(128, dtype=np.float32)
    ci = np.arange(128)
    ang = 2 * np.pi * np.outer(ci, ci) / 128
    c["cwsw"] = np.concatenate([np.cos(ang), np.sin(ang)], 1).astype(bf)

    def dft(T, seq):
        t = np.arange(T)
        same = (t[:, None] // seq) == (t[None, :] // seq)
        a = 2 * np.pi * np.outer(t % seq, t % seq) / seq
        nrm = (seq * 128) ** -0.5
        return (np.cos(a) * same * nrm).astype(bf), (-np.sin(a) * same * nrm).astype(bf)
    c["dftc0"], c["dfts0"] = dft(1024, 1024 if is_sample else 256)
    c["dftc1"], c["dfts1"] = dft(512, 256)

    def rope(T, real):
        out = np.zeros((32, 2, T), np.float32)
        out[:, 0, :] = 1.0
        if real:
            t = np.arange(T)
            pos = [t // 64, t % 64]
            inv = 10000.0 ** (-np.arange(8, dtype=np.float32) / 8)
            for d in range(32):
                hh, e = d // 16, d % 16
                f, second = e % 8, e // 8
                a = pos[hh].astype(np.float32) * inv[f]
                out[d, 0] = np.cos(a)
                out[d, 1] = np.sin(a) * (1.0 if second else -1.0)
        return out
    c["rope0"] = rope(1024, is_sample)
    c["rope1"] = rope(512, False)

    def masks(T, NP, seq, past_ok):
        NK = NP + T
        U = np.zeros((5, NK), np.float32)
        W = np.full((5, T), -BIG, np.float32)
        U[0, :NP] = 1
        if past_ok:
            W[0, :] = 0
        kj = np.arange(T)
        for s in range(T // seq):
            U[1 + s, NP + s * seq:NP + (s + 1) * seq] = 1
            W[1 + s, s * seq:(s + 1) * seq] = 0
        return U.astype(bf), W.astype(bf)
    c["mku0"], c["mkw0"] = masks(1024, 512, 1024 if is_sample else 256, is_sample)
    c["mku1"], c["mkw1"] = masks(512, 0, 256, False)
    keep = np.ones((2, 2, 8), np.float32)

    def packed(NC):
        kf = np.array([0.0 if n % 2 == 0 else 1.0 for n in range(8)], np.float32)
        kb = np.array([0.0 if n % 2 == 1 else 1.0 for n in range(8)], np.float32)
        return kf, kb
    if not is_sample:
        keep[0, 0], keep[0, 1] = packed(8)
    keep[1, 0], keep[1, 1] = packed(4)
    c["keep"] = np.broadcast_to(keep.reshape(1, 32), (128, 32)).copy()
    jj, ii = np.meshgrid(np.arange(128), np.arange(128), indexing="ij")
    rc = np.zeros((128, 4, 128), np.float32)
    rc[:, 0] = np.maximum(ii - jj, 0)
    rc[:, 1] = np.maximum(jj - ii, 0)
    rc[:, 2] = np.where(ii == jj, 2.0, 1.0) * 0.125
    rc[:64, 3] = np.arange(128)[None, :] + 1
    rc[64:, 3] = 128 - np.arange(128)[None, :]
    c["rc"] = rc
    c["ez"] = np.stack([127 - np.arange(128), np.arange(128)], 1).astype(np.float32)
    return c


_NC_CACHE = {}


def kernel(x_prompt, x_sample, cache_ckv, cache_krope, state_ret, c, c_ctx, norm_g, w_mod, b_mod, w_in,
           ret_decay_logit, q_norm_g, w_q_up, kv_norm_g, w_kv_up, w_branch, w_out, final_norm_g):
    f32 = np.float32
    A = lambda a: np.ascontiguousarray(np.asarray(a, dtype=f32))
    x_prompt, x_sample, cache_ckv, cache_krope, state_ret = map(A, (x_prompt, x_sample, cache_ckv, cache_krope, state_ret))
    c, c_ctx, norm_g, w_mod, b_mod, w_in = map(A, (c, c_ctx, norm_g, w_mod, b_mod, w_in))
    ret_decay_logit, q_norm_g, w_q_up, kv_norm_g, w_kv_up, w_branch, w_out, final_norm_g = map(
        A, (ret_decay_logit, q_norm_g, w_q_up, kv_norm_g, w_kv_up, w_branch, w_out, final_norm_g))
    if "nc" not in _NC_CACHE:
        _NC_CACHE["nc"] = build_program()
    nc = _NC_CACHE["nc"]

    rq = w_in[:, :, 0:256].reshape(2, 1024, 4, 1, 64)
    w_rqd = A(np.broadcast_to(rq, (2, 1024, 4, 2, 64)).reshape(2, 1024, 512))
    swap = np.array([(d // 16) * 16 + ((d % 16) + 8) % 16 for d in range(32)])
    kr = w_in[:, :, 2176:2208]
    z64 = np.zeros((2, 1024, 64), f32)
    w_krp = A(np.concatenate([z64, kr, z64, kr[:, :, swap]], 2))
    qu = w_q_up.reshape(2, 384, 8, 96)
    qsw = np.concatenate([np.zeros((2, 384, 8, 64), f32), qu[:, :, :, 64:][:, :, :, swap]], 3)
    w_qu2 = A(np.concatenate([qu, qsw], 3).reshape(2, 384, 1536))
    fm = lambda v, k: A(v.reshape(k, 128).T)
    shared = dict(
        ngT=A(np.stack([fm(norm_g[l], 8) for l in range(2)], 1)),
        bmT=A(np.stack([fm(b_mod[l], 24) for l in range(2)], 1)),
        fgT=fm(final_norm_g, 8),
        gqT=A(np.stack([fm(q_norm_g[l], 3) for l in range(2)], 1)),
        gkv=A(kv_norm_g.reshape(512)), rdl=A(ret_decay_logit.reshape(16)),
        w_mod=w_mod, w_in=w_in, w_rqd=w_rqd, w_krp=w_krp, w_qu2=w_qu2, w_kvu=w_kv_up, w_br=w_branch, w_out=w_out,
    )
    cs = {True: _consts(True), False: _consts(False)}
    in_maps = []
    for core in range(8):
        samp = core < 4
        sp = [2 * core, 2 * core + 1]
        if samp:
            xl = x_sample[core]
            cv_l = c[core]
            cck, ckr_, s0 = cache_ckv[core], cache_krope[core], state_ret[core]
        else:
            lp = [16 + 4 * (core - 4) + s for s in range(4)]
            xl = x_prompt[lp].reshape(1024, 1024)
            cv_l = c_ctx
            cck, ckr_, s0 = np.zeros((2, 512, 256), f32), np.zeros((2, 512, 32), f32), np.zeros((2, 2, 4, 64, 128), f32)
        xin = A(np.concatenate([xl, x_prompt[sp].reshape(512, 1024)], 0))
        cvT = A(np.stack([fm(cv_l, 8), fm(c_ctx, 8)], 2))
        m = dict(shared)
        m.update(cs[samp])
        m.update(xin=xin, cvT=cvT, cckv=A(cck), ckr=A(ckr_), s0=A(s0))
        in_maps.append(m)
    res = run_bass_kernel_spmd(nc, in_maps, core_ids=list(range(8)))
    R = res.results
    y_prompt = np.zeros((32, 256, 1024), f32)
    y_sample = np.zeros((4, 1024, 1024), f32)
    new_ckv = np.zeros((32, 2, 256, 256), f32)
    new_kr = np.zeros((32, 2, 256, 32), f32)
    new_ret = np.zeros((32, 2, 2, 4, 64, 128), f32)
    for b in range(32):
        if b < 16:
            core, off, slot = b // 2, 1024 + (b % 2) * 256, 4 + (b % 2)
        else:
            core, off, slot = 4 + (b - 16) // 4, ((b - 16) % 4) * 256, (b - 16) % 4
        r = R[core]
        y_prompt[b] = r["y"][off:off + 256]
        new_ckv[b] = r["o_ckv"][:, off:off + 256]
        new_kr[b] = r["o_kr"][:, off:off + 256]
        st_ = r["o_ret"][:, slot]
        new_ret[b] = st_.reshape(2, 4, 2, 64, 128).transpose(0, 2, 1, 3, 4)
    for b in range(4):
        y_sample[b] = R[b]["y"][0:1024]
    return (y_prompt, y_sample, new_ckv, new_kr, new_ret)
```

```python
import os
import numpy as np
import ml_dtypes
import concourse.bass as bass
import concourse.mybir as mybir
from concourse.bass_utils import run_bass_kernel_spmd

F32, BF16 = mybir.dt.float32, mybir.dt.bfloat16
AF = mybir.ActivationFunctionType
ALU = mybir.AluOpType
AX = mybir.AxisListType
EPS = 1e-6
NSLOT = 12
PF = 5
JOBS = [(0, 1024, 512), (1024, 512, 0)]
SLOT0 = [0, 4]
NTOK = 1536
SC = float(96 ** -0.5)
BIG = 30000.0
ARENA = 73728
STAGE = float(os.environ.get('KSTAGE', '999'))


class _Stop(Exception):
    pass


def ck(n):
    if n >= STAGE:
        raise _Stop()


class Buf:
    __slots__ = ("w", "r", "name", "excl")

    def __init__(self, name="", excl=False):
        self.w = None
        self.r = {}
        self.name = name
        self.excl = excl


class Sched:
    ENG = ("pe", "act", "dve", "pool", "sp")

    def __init__(self, nc, n_dma_sems=10):
        self.nc = nc
        self.plan = False
        self.prog = {e: [] for e in self.ENG}
        self.sem = {e: nc.alloc_semaphore(f"c_{e}") for e in self.ENG}
        self.cnt = {e: 0 for e in self.ENG}
        self.waited = {e: {} for e in self.ENG}
        self.nd = n_dma_sems
        self.dsem = {q: [nc.alloc_semaphore(f"d_{q}{i}") for i in range(n_dma_sems)] for q in ("sp", "pool")}
        self.dcnt = {q: [0] * n_dma_sems for q in self.dsem}
        self.didx = {q: 0 for q in self.dsem}

    def _collect(self, reads, writes):
        deps = {}
        for b in reads:
            if b.w is not None and deps.get(b.w[0], 0) < b.w[1]:
                deps[b.w[0]] = b.w[1]
            if b.excl:
                for s, v in b.r.items():
                    if deps.get(s, 0) < v:
                        deps[s] = v
        for b in writes:
            if b.w is not None and deps.get(b.w[0], 0) < b.w[1]:
                deps[b.w[0]] = b.w[1]
            for s, v in b.r.items():
                if deps.get(s, 0) < v:
                    deps[s] = v
        return deps

    def _waits(self, eng, deps):
        waits = []
        wd = self.waited[eng]
        for s, v in deps.items():
            if eng == "pe" and s is self.sem["pe"]:
                continue
            if wd.get(s, 0) >= v:
                continue
            wd[s] = v
            waits.append((s, v))
        return waits

    def _commit(self, ev, reads, writes):
        s, v = ev
        for b in reads:
            if b.r.get(s, 0) < v:
                b.r[s] = v
        for b in writes:
            b.w = ev
            b.r = {}

    def op(self, eng, fn, reads=(), writes=()):
        if self.plan:
            return
        waits = self._waits(eng, self._collect(reads, writes))
        self.cnt[eng] += 1
        ev = (self.sem[eng], self.cnt[eng])
        self.prog[eng].append((waits, fn, ev[0], 1))
        self._commit(ev, reads, writes)

    def dma(self, q, fn, reads=(), writes=()):
        if self.plan:
            return
        i = self.didx[q]
        self.didx[q] = (i + 1) % self.nd
        sem = self.dsem[q][i]
        prev = self.dcnt[q][i]
        deps = self._collect(reads, writes)
        if prev > 0 and deps.get(sem, 0) < prev:
            deps[sem] = prev
        waits = self._waits(q, deps)
        self.dcnt[q][i] = prev + 16
        ev = (sem, prev + 16)
        self.prog[q].append((waits, fn, sem, 16))
        self._commit(ev, reads, writes)

    def barrier(self):
        if self.plan:
            return
        for e in self.ENG:
            deps = {}
            for e2 in self.ENG:
                if e2 != e and self.cnt[e2] > 0:
                    deps[self.sem[e2]] = self.cnt[e2]
            for i, s in enumerate(self.dsem["sp"]):
                if self.dcnt["sp"][i] > 0:
                    deps[s] = self.dcnt["sp"][i]
            waits = self._waits(e, deps)
            if waits:
                self.prog[e].append((waits, None, None, 0))

    def finish(self):
        waits = []
        for q in self.dsem:
            for i, s in enumerate(self.dsem[q]):
                if self.dcnt[q][i] > 0:
                    waits.append((s, self.dcnt[q][i]))
        for e in self.ENG:
            if e != "sp" and self.cnt[e] > 0:
                waits.append((self.sem[e], self.cnt[e]))
        self.prog["sp"].append((waits, None, None, 0))

    def emit(self):
        prog = self.prog

        def replay(name, eng):
            for waits, fn, sem, inc in prog[name]:
                for s, v in waits:
                    eng.wait_ge(s, v)
                if fn is not None:
                    fn(eng).then_inc(sem, inc)

        with self.nc.Block() as block:
            @block.tensor
            def _(e):
                replay("pe", e)

            @block.scalar
            def _(e):
                replay("act", e)

            @block.vector
            def _(e):
                replay("dve", e)

            @block.gpsimd
            def _(e):
                replay("pool", e)

            @block.sync
            def _(e):
                replay("sp", e)


def build_program():
    nc = bass.Bass("TRN2", target_bir_lowering=False)
    S = Sched(nc)

    def din(name, shape, dt=F32):
        return nc.dram_tensor(name, list(shape), dt, kind="ExternalInput").ap()

    def dout(name, shape):
        return nc.dram_tensor(name, list(shape), F32, kind="ExternalOutput").ap()

    D = dict(
        xin=din("xin", [NTOK, 1024]), cvT=din("cvT", [128, 8, 2]),
        cckv=din("cckv", [2, 512, 256]), ckr=din("ckr", [2, 512, 32]), s0=din("s0", [2, 2, 4, 64, 128]),
        ngT=din("ngT", [128, 2, 8]), bmT=din("bmT", [128, 2, 24]), fgT=din("fgT", [128, 8]),
        gqT=din("gqT", [128, 2, 3]), gkv=din("gkv", [512]), rdl=din("rdl", [16]),
        w_mod=din("w_mod", [2, 1024, 3072]), w_in=din("w_in", [2, 1024, 6816]),
        w_rqd=din("w_rqd", [2, 1024, 512]), w_krp=din("w_krp", [2, 1024, 192]),
        w_qu2=din("w_qu2", [2, 384, 1536]), w_kvu=din("w_kvu", [2, 256, 1024]),
        w_br=din("w_br", [2, 3, 512, 1024]), w_out=din("w_out", [2, 1024, 1024]),
        ident=din("ident", [128, 128]), cwsw=din("cwsw", [128, 256], BF16),
        dftc0=din("dftc0", [1024, 1024], BF16), dfts0=din("dfts0", [1024, 1024], BF16),
        dftc1=din("dftc1", [512, 512], BF16), dfts1=din("dfts1", [512, 512], BF16),
        rope0=din("rope0", [32, 2, 1024]), rope1=din("rope1", [32, 2, 512]),
        mku0=din("mku0", [5, 1536], BF16), mkw0=din("mkw0", [5, 1024], BF16),
        mku1=din("mku1", [5, 512], BF16), mkw1=din("mkw1", [5, 512], BF16),
        keep=din("keep", [128, 32]), rc=din("rc", [128, 4, 128]), ez=din("ez", [128, 2]),
        y=dout("y", [NTOK, 1024]), o_ckv=dout("o_ckv", [2, NTOK, 256]), o_kr=dout("o_kr", [2, NTOK, 32]),
        o_ret=dout("o_ret", [2, 6, 4, 128, 128]),
    )

    def T(name, shape, dt):
        return nc.alloc_sbuf_tensor("sb_" + name, list(shape), dt), Buf(name)

    xT, XT = T("xT", [128, 8, 1024], F32)
    hT, HT = T("hT", [128, 8, 1024], BF16)
    brT, BRT = T("brT", [128, 12, 1024], BF16)
    slots = [T(f"ws{i}", [128, 2048], BF16) for i in range(NSLOT)]
    ident, IDENT = T("ident", [128, 128], F32)
    onesb, ONESB = T("onesb", [128, 128], BF16)
    cwsw, CWSW = T("cwsw", [128, 256], BF16)
    rc, RC = T("rc", [128, 4, 128], F32)
    ez, EZ = T("ez", [128, 2], F32)
    lg, LG = T("lg", [128, 16], F32)
    lgcol, LGCOL = T("lgcol", [128, 8], F32)
    Dm, DM = T("Dm", [128, 8, 128], F32)
    Xi, XI = T("Xi", [128, 8, 128], F32)
    Zt, ZT = T("Zt", [128, 2, 4, 2], F32)
    gC, GC = T("gC", [128, 16], F32)
    keep, KEEP = T("keep", [128, 2, 2, 8], F32)
    modT, MODT = T("modT", [128, 2, 24, 2], F32)
    g1, G1 = T("g1", [128, 2, 2, 8], F32)
    ngT, NGT = T("ngT", [128, 2, 8], F32)
    bmT, BMT = T("bmT", [128, 2, 24], F32)
    fgT, FGT = T("fgT", [128, 8], F32)
    gqT, GQT = T("gqT", [128, 2, 3], F32)
    gkv, GKV = T("gkv", [128, 2, 256], F32)
    cvT, CVT = T("cvT", [128, 8, 2], F32)
    scv, SCV = T("scv", [128, 8, 2], BF16)
    epsc, EPSC = T("epsc", [128, 1], F32)
    tmp16, TMP16 = T("tmp16", [128, 16], F32)
    arena = nc.alloc_sbuf_tensor("arena", [128, ARENA // 2], BF16)
    ps = nc.alloc_psum_tensor("ps", [128, 4096], F32)
    PSB = [Buf(f"ps{i}", excl=True) for i in range(8)]
    st = {"rot": 0, "aoff": 0}

    def bank(i):
        return ps[:, i * 512:(i + 1) * 512]

    def nb():
        i = st["rot"] % st.get("nrot", 6)
        st["rot"] = i + 1
        return i

    def areset():
        inh = st.setdefault("inh", {})
        for b in st.setdefault("pbufs", []):
            if b.w is not None and inh.get(b.w[0], 0) < b.w[1]:
                inh[b.w[0]] = b.w[1]
            for s_, v_ in b.r.items():
                if inh.get(s_, 0) < v_:
                    inh[s_] = v_
        st["pbufs"] = []
        st["aoff"] = 0

    def aget(shape, dt, name=""):
        assert shape[0] == 128
        fs = list(shape[1:])
        n = int(np.prod(fs))
        nbytes = n * (4 if dt == F32 else 2)
        off = st["aoff"]
        st["aoff"] = off + ((nbytes + 31) // 32) * 32
        assert st["aoff"] <= ARENA, (name, st["aoff"])
        a = arena[:, off // 2:(off + nbytes) // 2]
        if dt == F32:
            a = a.bitcast(F32)
        if len(fs) == 2:
            a = a.rearrange("p (a b) -> p a b", b=fs[1])
        elif len(fs) == 3:
            a = a.rearrange("p (a b c) -> p a b c", b=fs[1], c=fs[2])
        nbuf = Buf(name)
        nbuf.r = dict(st.get("inh", {}))
        st.setdefault("pbufs", []).append(nbuf)
        return a, nbuf

    def mm(out, lhsT, rhs, start, stop, rd, wr):
        S.op("pe", lambda e: e.matmul(out, lhsT=lhsT, rhs=rhs, start=start, stop=stop), rd, wr)

    def tr(out, in_, rd, wr):
        S.op("pe", lambda e: e.transpose(out, in_, ident[:]), list(rd) + [IDENT], wr)

    def act(out, in_, func, rd, wr, **kw):
        S.op("act", lambda e: e.activation(out=out, in_=in_, func=func, **kw), rd, wr)

    def tt(out, in0, in1, op, rd, wr):
        S.op("dve", lambda e: e.tensor_tensor(out=out, in0=in0, in1=in1, op=op), rd, wr)

    def ts(out, in0, s1, op0, rd, wr, s2=None, op1=None):
        if op1 is None:
            S.op("dve", lambda e: e.tensor_scalar(out=out, in0=in0, scalar1=s1, scalar2=None, op0=op0), rd, wr)
        else:
            S.op("dve", lambda e: e.tensor_scalar(out=out, in0=in0, scalar1=s1, scalar2=s2, op0=op0, op1=op1), rd, wr)

    def stt(out, in0, scalar, in1, op0, op1, rd, wr):
        S.op("dve", lambda e: e.scalar_tensor_tensor(out=out, in0=in0, scalar=scalar, in1=in1, op0=op0, op1=op1), rd, wr)

    def pool_ts(out, in0, s1, op0, rd, wr):
        S.op("pool", lambda e: e.tensor_scalar(out=out, in0=in0, scalar1=s1, scalar2=None, op0=op0), rd, wr)

    def pool_tt(out, in0, in1, op, rd, wr):
        S.op("pool", lambda e: e.tensor_tensor(out=out, in0=in0, in1=in1, op=op), rd, wr)

    def cp(out, in_, rd, wr):
        S.op("dve", lambda e: e.tensor_copy(out=out, in_=in_), rd, wr)

    def recip(out, in_, rd, wr):
        S.op("dve", lambda e: e.reciprocal(out=out, in_=in_), rd, wr)

    def rsum(out, in_, rd, wr):
        S.op("dve", lambda e: e.tensor_reduce(out=out, in_=in_, axis=AX.X, op=ALU.add), rd, wr)

    def mset(ap, val, wr):
        S.op("dve", lambda e: e.memset(ap, val), (), wr)

    def dma(out, in_, rd, wr, q="sp"):
        S.dma(q, lambda e: e.dma_start(out=out, in_=in_), rd, wr)

    class WS:
        def __init__(self):
            self.specs = []
            self.i = 0
            self.issued = 0

        def get(self, src, kc, ncols):
            if S.plan:
                self.specs.append((src, kc, ncols))
                t, b = slots[0]
                return t[:, 0:kc * ncols].rearrange("p (k c) -> p k c", c=ncols), b
            while self.issued < min(len(self.specs), self.i + PF + 1):
                j = self.issued
                s_src, s_kc, s_nc = self.specs[j]
                t, b = slots[j % NSLOT]
                dma(t[:, 0:s_kc * s_nc].rearrange("p (k c) -> p k c", c=s_nc), s_src, (), [b], q="pool")
                self.issued += 1
            t, b = slots[self.i % NSLOT]
            self.i += 1
            return t[:, 0:kc * ncols].rearrange("p (k c) -> p k c", c=ncols), b

    ws = WS()

    def wsrc(ap2d, c0, ncols):
        return ap2d.rearrange("(k p) c -> p k c", p=128)[:, :, c0:c0 + ncols]

    def body():
        st["rot"] = 0
        st["aoff"] = 0
        st["inh"] = {}
        st["pbufs"] = []
        dma(ident[:], D["ident"], (), [IDENT])
        dma(cwsw[:], D["cwsw"], (), [CWSW])
        dma(rc[:], D["rc"], (), [RC])
        dma(ez[:], D["ez"], (), [EZ])
        dma(keep[:], D["keep"].rearrange("p (a b c) -> p a b c", a=2, b=2), (), [KEEP])
        dma(ngT[:], D["ngT"], (), [NGT])
        dma(bmT[:], D["bmT"], (), [BMT])
        dma(fgT[:], D["fgT"], (), [FGT])
        dma(gqT[:], D["gqT"], (), [GQT])
        dma(cvT[:], D["cvT"], (), [CVT])
        dma(gkv[:], D["gkv"].partition_broadcast(128).rearrange("p (a b) -> p a b", a=2), (), [GKV])
        dma(lg[:], D["rdl"].partition_broadcast(128), (), [LG])
        ck(0.2)
        mset(onesb[:], 1.0, [ONESB])
        mset(epsc[:], EPS, [EPSC])
        act(lg[:], lg[:], AF.Exp, [LG], [LG], scale=-1.0)
        act(lg[:], lg[:], AF.Ln, [LG], [LG], bias=1.0)
        ts(lg[:], lg[:], -1.0, ALU.mult, [LG], [LG])
        ck(0.4)
        lgv = lg[:].rearrange("p (l d h) -> p l d h", l=2, d=2)
        lgc = lgcol[:].rearrange("p (l h) -> p l h", l=2)
        cp(lgc[0:64], lgv[0:64, :, 0, :], [LG], [LGCOL])
        cp(lgc[64:128], lgv[64:128, :, 1, :], [LG], [LGCOL])
        act(gC[:], lg[:], AF.Exp, [LG], [GC], scale=128.0)
        ck(0.6)
        for l in range(2):
            for h in range(4):
                jf, jb, j = l * 8 + h, l * 8 + 4 + h, l * 4 + h
                ts(Dm[:, j, :], rc[:, 0, :], lg[:, jf:jf + 1], ALU.mult, [RC, LG], [DM])
                stt(Dm[:, j, :], rc[:, 1, :], lg[:, jb:jb + 1], Dm[:, j, :], ALU.mult, ALU.add, [RC, LG, DM], [DM])
                act(Dm[:, j, :], Dm[:, j, :], AF.Exp, [DM], [DM])
                tt(Dm[:, j, :], Dm[:, j, :], rc[:, 2, :], ALU.mult, [DM, RC], [DM])
                ck(0.7)
                act(Xi[:, j, :], rc[:, 3, :], AF.Exp, [RC, LGCOL], [XI], scale=lgcol[:, j:j + 1])
                ck(0.8)
                for d in range(2):
                    jj = l * 8 + d * 4 + h
                    act(Zt[:, l, h, d:d + 1], ez[:, d:d + 1], AF.Exp, [EZ, LG], [ZT], scale=lg[:, jj:jj + 1])
                ck(0.9)
        ck(0.95)
        ts(Zt[:], Zt[:], 0.125, ALU.mult, [ZT], [ZT])
        ck(0.97)
        ck(1)
        act(scv[:], cvT[:], AF.Silu, [CVT], [SCV])

        def emit_mod(l, b0=0, b1=12, with_g1=True):
            for blk in range(b0, b1):
                w, WB = ws.get(wsrc(D["w_mod"][l], blk * 256, 256), 8, 256)
                for fc in range(2):
                    f = blk * 2 + fc
                    for k in range(8):
                        mm(bank(7)[:, f * 2:f * 2 + 2], w[:, k, fc * 128:fc * 128 + 128], scv[:, k, :], k == 0, k == 7,
                           [WB, SCV], [PSB[7]])
            f0, f1 = 2 * b0, 2 * b1
            tt(modT[:, l, f0:f1, :], bank(7)[:, f0 * 2:f1 * 2].rearrange("p (f j) -> p f j", j=2),
               bmT[:, l, f0:f1].unsqueeze(2).broadcast_to([128, f1 - f0, 2]), ALU.add, [PSB[7], BMT], [MODT])
            if with_g1:
                for j in range(2):
                    ts(g1[:, l, j, :], modT[:, l, 8:16, j], 1.0, ALU.add, [MODT], [G1])
                    tt(g1[:, l, j, :], g1[:, l, j, :], ngT[:, l, :], ALU.mult, [G1, NGT], [G1])

        emit_mod(0, 0, 8)
        ck(2)
        for j, (tok0, TT, NP) in enumerate(JOBS):
            NC, NTG, NK = TT // 128, TT // 512, NP + TT
            NKT = NK // 128
            areset()
            xs = [aget([128, 1024], F32, f"xs{i}") for i in range(2)]
            for n in range(NC):
                xa, XA = xs[n % 2]
                dma(xa, D["xin"][tok0 + n * 128:tok0 + (n + 1) * 128, :], (), [XA])
                for kq in range(2):
                    b = nb()
                    for c in range(4):
                        k = kq * 4 + c
                        tr(bank(b)[:, c * 128:(c + 1) * 128], xa[:, k * 128:(k + 1) * 128], [XA], [PSB[b]])
                    o = xT[:, kq * 4:kq * 4 + 4, n * 128:(n + 1) * 128]
                    i_ = bank(b).rearrange("p (c t) -> p c t", c=4)
                    if kq == 0:
                        cp(o, i_, [PSB[b]], [XT])
                    else:
                        act(o, i_, AF.Copy, [PSB[b]], [XT])

            ck(3)

            def norm_stats(rs, RS, sq, tgsl):
                b = nb()
                for k in range(8):
                    sa, SA = sq[k % 2]
                    act(sa, xT[:, k, tgsl], AF.Square, [XT], [SA])
                    mm(bank(b), onesb[:], sa, k == 0, k == 7, [ONESB, SA], [PSB[b]])
                act(rs, bank(b), AF.Ln, [PSB[b], EPSC], [RS], scale=1.0 / 1024, bias=epsc[:, 0:1])
                act(rs, rs, AF.Exp, [RS], [RS], scale=-0.5)

            for l in range(2):
                W_in = D["w_in"][l]
                areset()
                sq = [aget([128, 512], BF16, f"sq{i}") for i in range(2)]
                rsb = [aget([128, 512], F32, f"rs{i}") for i in range(NTG)]
                tf = [aget([128, 512], F32, f"tf{i}") for i in range(2)]
                for tg in range(NTG):
                    norm_stats(rsb[tg][0], rsb[tg][1], sq, slice(tg * 512, (tg + 1) * 512))
                for tg in range(NTG):
                    tgsl = slice(tg * 512, (tg + 1) * 512)
                    rs, RS = rsb[tg]
                    for k in range(8):
                        ta, TA = tf[k % 2]
                        stt(ta, xT[:, k, tgsl], g1[:, l, j, k:k + 1], rs, ALU.mult, ALU.mult, [XT, G1, RS], [TA])
                        act(hT[:, k, tgsl], ta, AF.Identity, [TA, MODT], [HT], bias=modT[:, l, k, j:j + 1], scale=1.0)
                if j == 0 and l == 0:
                    emit_mod(0, 8, 12, with_g1=False)

                ck(4)
                areset()
                Kz, KZ = aget([128, NC, 4, 128], BF16, "Kz")
                Vr, VR = aget([128, NC, 512], BF16, "Vr")
                Vc, VC = aget([128, NC, 512], BF16, "Vc")
                vs4, VS4 = aget([128, 4], F32, "vs4")
                QT = [aget([128, TT], BF16, f"QT{i}") for i in range(2)]
                QTs = [aget([128, TT], BF16, f"QTs{i}") for i in range(2)]
                KT = [aget([128, TT], BF16, f"KT{i}") for i in range(2)]
                gR = [aget([128, TT], BF16, f"gR{i}") for i in range(2)]
                R, RB = aget([128, NC + 2, 128], F32, "R")
                RBB = Buf("Rb")
                RBB.r = dict(RB.r)
                st["pbufs"].append(RBB)
                Sin, SIN = aget([128, NC, 128], F32, "Sin")
                SINB = Buf("Sinb")
                SINB.r = dict(SIN.r)
                st["pbufs"].append(SINB)
                Scs = [aget([128, NC, 128], BF16, f"Sc{i}") for i in range(2)]
                srs, SRS = aget([128, NC], F32, "srs")
                dk, DKB = aget([128, NC], F32, "dk")
                AT = [aget([128, 512], BF16, f"AT{i}") for i in range(2)]
                sqo2 = [aget([128, 512], BF16, f"sqo{i}") for i in range(2)]
                rso2 = [aget([128, 512], F32, f"rso{i}") for i in range(2)]
                to2 = [aget([128, 512], F32, f"to{i}") for i in range(2)]
                wq = [ws.get(wsrc(D["w_rqd"][l], i * 256, 256), 8, 256) for i in range(2)]
                wk, WK = ws.get(wsrc(W_in, 256, 256), 8, 256)
                wv = [ws.get(wsrc(W_in, 512 + i * 256, 256), 8, 256) for i in range(2)]
                wz = [ws.get(wsrc(W_in, 1024 + i * 256, 256), 8, 256) for i in range(2)]
                for n in range(NC):
                    nsl = slice(n * 128, (n + 1) * 128)
                    bK = nb()
                    for k in range(8):
                        mm(bank(bK)[:, 0:256], hT[:, k, nsl], wk[:, k, :], k == 0, k == 7, [HT, WK], [PSB[bK]])
                    bV = nb()
                    for i in range(2):
                        for k in range(8):
                            mm(bank(bV)[:, i * 256:(i + 1) * 256], hT[:, k, nsl], wv[i][0][:, k, :], k == 0, k == 7,
                               [HT, wv[i][1]], [PSB[bV]])
                    tt(Kz[:, n].rearrange("p h (d e) -> p h d e", d=2),
                       bank(bK)[:, 0:256].rearrange("p (h e) -> p h e", h=4).unsqueeze(2).broadcast_to([128, 4, 2, 64]),
                       Zt[:, l].unsqueeze(3).broadcast_to([128, 4, 2, 64]), ALU.mult, [PSB[bK], ZT], [KZ])
                    act(Vr[:, n, :], bank(bV), AF.Copy, [PSB[bV]], [VR])
                    rsum(vs4, bank(bV).rearrange("p (h e) -> p h e", h=4), [PSB[bV]], [VS4])
                    ts(vs4, vs4, -1.0 / 128, ALU.mult, [VS4], [VS4])
                    tt(Vc[:, n, :].rearrange("p (h e) -> p h e", h=4), bank(bV).rearrange("p (h e) -> p h e", h=4),
                       vs4.unsqueeze(2).broadcast_to([128, 4, 128]), ALU.add, [PSB[bV], VS4], [VC])
                def head_vars(h):
                    bi = h % 2
                    return (bi,) + QT[bi] + QTs[bi] + KT[bi] + gR[bi] + wq[h // 2] + wz[h // 2] + ((h % 2) * 128,) + Scs[bi]

                def stage_a1(h):
                    bi, qt, QTB, qs, QSB, kt_, KTB, gr, GRB, wqh, WQH, wzh, WZH, co, Sc, SCB = head_vars(h)
                    for tg in range(NTG):
                        tgsl = slice(tg * 512, (tg + 1) * 512)
                        b = nb()
                        for k in range(8):
                            mm(bank(b), wqh[:, k, co:co + 128], hT[:, k, tgsl], k == 0, k == 7, [WQH, HT], [PSB[b]])
                        act(qt[0:64, tgsl], bank(b)[0:64, :], AF.Copy, [PSB[b]], [QTB])
                        tt(qs[:, tgsl].rearrange("p (c t) -> p c t", c=4), bank(b).rearrange("p (c t) -> p c t", c=4),
                           Xi[:, l * 4 + h, :].unsqueeze(1).broadcast_to([128, 4, 128]), ALU.mult, [PSB[b], XI], [QSB])
                        b = nb()
                        for k in range(8):
                            mm(bank(b)[0:64, :], wk[:, k, h * 64:(h + 1) * 64], hT[:, k, tgsl], k == 0, k == 7, [WK, HT], [PSB[b]])
                        act(kt_[0:64, tgsl], bank(b)[0:64, :], AF.Copy, [PSB[b]], [KTB])

                def stage_a2(h):
                    bi, qt, QTB, qs, QSB, kt_, KTB, gr, GRB, wqh, WQH, wzh, WZH, co, Sc, SCB = head_vars(h)
                    dma(R[0:64, 0, :], D["s0"][l, 0, h], (), [RB])
                    dma(R[64:128, NC, :], D["s0"][l, 1, h], (), [RBB])
                    kvb = []
                    for n in range(NC):
                        if n % 4 == 0:
                            kvb.append(6 + (n // 4))
                        b = kvb[-1]
                        mm(bank(b)[:, (n % 4) * 128:(n % 4 + 1) * 128], Kz[:, n, h, :], Vr[:, n, h * 128:(h + 1) * 128],
                           True, True, [KZ, VR], [PSB[b]])
                    jf, jb = l * 8 + h, l * 8 + 4 + h
                    ts(dk[0:64, :], keep[0:64, j, 0, 0:NC], gC[0:64, jf:jf + 1], ALU.mult, [KEEP, GC], [DKB])
                    ts(dk[64:128, :], keep[64:128, j, 1, 0:NC], gC[64:128, jb:jb + 1], ALU.mult, [KEEP, GC], [DKB])
                    for n in range(NC):
                        b = kvb[n // 4]
                        stt(R[0:64, n + 1, :], R[0:64, n, :], dk[0:64, n:n + 1], bank(b)[0:64, (n % 4) * 128:(n % 4 + 1) * 128],
                            ALU.mult, ALU.add, [RB, DKB, PSB[b]], [RB])
                    for n in range(NC - 1, -1, -1):
                        b = kvb[n // 4]
                        stt(R[64:128, n, :], R[64:128, n + 1, :], dk[64:128, n:n + 1], bank(b)[64:128, (n % 4) * 128:(n % 4 + 1) * 128],
                            ALU.mult, ALU.add, [RBB, DKB, PSB[b]], [RBB])
                    tt(Sin[0:64], R[0:64, 0:NC, :], keep[0:64, j, 0, 0:NC].unsqueeze(2).broadcast_to([64, NC, 128]), ALU.mult,
                       [RB, KEEP], [SIN])
                    tt(Sin[64:128], R[64:128, 1:NC + 1, :], keep[64:128, j, 1, 0:NC].unsqueeze(2).broadcast_to([64, NC, 128]), ALU.mult,
                       [RBB, KEEP], [SINB])
                    rsum(srs, Sin, [SIN, SINB], [SRS])
                    ts(srs, srs, -1.0 / 128, ALU.mult, [SRS], [SRS])
                    tt(Sc, Sin, srs.unsqueeze(2).broadcast_to([128, NC, 128]), ALU.add, [SIN, SINB, SRS], [SCB])
                    ns = NC // 2
                    dma(D["o_ret"][l, SLOT0[j]:SLOT0[j] + ns, h, 0:64, :].rearrange("s p e -> p s e"),
                        R[0:64, 2:NC + 2, :].rearrange("p (s two) e -> p s two e", two=2)[:, :, 0, :], [RB], ())
                    dma(D["o_ret"][l, SLOT0[j]:SLOT0[j] + ns, h, 64:128, :].rearrange("s p e -> p s e"),
                        R[64:128, 0:NC, :].rearrange("p (s two) e -> p s two e", two=2)[:, :, 0, :], [RBB], ())

                def stage_b(h):
                    bi, qt, QTB, qs, QSB, kt_, KTB, gr, GRB, wqh, WQH, wzh, WZH, co, Sc, SCB = head_vars(h)
                    tgs = list(range(NTG))
                    bS, bO, bv = {}, {}, {}
                    for tg in tgs:
                        bS[tg] = nb()
                        for c4 in range(4):
                            n = tg * 4 + c4
                            nsl = slice(n * 128, (n + 1) * 128)
                            mm(bank(bS[tg])[:, c4 * 128:(c4 + 1) * 128], kt_[0:64, nsl], qt[0:64, nsl], True, True, [KTB, QTB], [PSB[bS[tg]]])
                    for tg in tgs:
                        at, ATB = AT[tg % 2]
                        tt(at.rearrange("p (c t) -> p c t", c=4), bank(bS[tg]).rearrange("p (c t) -> p c t", c=4),
                           Dm[:, l * 4 + h, :].unsqueeze(1).broadcast_to([128, 4, 128]), ALU.mult, [PSB[bS[tg]], DM], [ATB])
                    for tg in tgs:
                        at, ATB = AT[tg % 2]
                        bO[tg] = nb()
                        for c4 in range(4):
                            n = tg * 4 + c4
                            nsl = slice(n * 128, (n + 1) * 128)
                            osl = bank(bO[tg])[:, c4 * 128:(c4 + 1) * 128]
                            mm(osl, Vc[:, n, h * 128:(h + 1) * 128], at[:, c4 * 128:(c4 + 1) * 128], True, False, [VC, ATB], [PSB[bO[tg]]])
                            mm(osl, Sc[:, n, :], qs[:, nsl], False, True, [SCB, QSB], [PSB[bO[tg]]])
                    for tg in tgs:
                        act(sqo2[tg % 2][0], bank(bO[tg]), AF.Square, [PSB[bO[tg]]], [sqo2[tg % 2][1]])
                    for tg in tgs:
                        bv[tg] = nb()
                        mm(bank(bv[tg]), onesb[:], sqo2[tg % 2][0], True, True, [ONESB, sqo2[tg % 2][1]], [PSB[bv[tg]]])
                    for tg in tgs:
                        r_, RSB_ = rso2[tg % 2]
                        act(r_, bank(bv[tg]), AF.Ln, [PSB[bv[tg]], EPSC], [RSB_], scale=1.0 / 128, bias=epsc[:, 0:1])
                    for tg in tgs:
                        r_, RSB_ = rso2[tg % 2]
                        act(r_, r_, AF.Exp, [RSB_], [RSB_], scale=-0.5)
                    for tg in tgs:
                        r_, RSB_ = rso2[tg % 2]
                        t_, TOB_ = to2[tg % 2]
                        tt(t_, bank(bO[tg]), r_, ALU.mult, [PSB[bO[tg]], RSB_], [TOB_])
                    for tg in tgs:
                        tgsl = slice(tg * 512, (tg + 1) * 512)
                        t_, TOB_ = to2[tg % 2]
                        tt(brT[:, h, tgsl], t_, brT[:, 4 + h, tgsl], ALU.mult, [TOB_, BRT], [BRT])

                for h in range(4):
                    wzh, WZH = wz[h // 2]
                    co = (h % 2) * 128
                    for tg in range(NTG):
                        tgsl = slice(tg * 512, (tg + 1) * 512)
                        b = nb()
                        for k in range(8):
                            mm(bank(b), wzh[:, k, co:co + 128], hT[:, k, tgsl], k == 0, k == 7, [WZH, HT], [PSB[b]])
                        act(brT[:, 4 + h, tgsl], bank(b), AF.Silu, [PSB[b]], [BRT])
                stage_a1(0)
                stage_a2(0)
                stage_a1(1)
                for h in range(4):
                    stage_b(h)
                    if h + 1 < 4:
                        stage_a2(h + 1)
                    if h + 2 < 4:
                        stage_a1(h + 2)

                ck(5)
                if j == 0 and l == 0:
                    emit_mod(1)
                areset()
                kvu, KVU = aget([128, 2, 1024], BF16, "kvu")
                ckvT, CKVT = aget([128, 2, NK], BF16, "ckvT")
                KRM, KRMB = aget([128, NK], BF16, "KRM")
                Kaug = [aget([128, NK], BF16, f"Kaug{i}") for i in range(2)]
                Qaug = [aget([128, 512], BF16, f"Qaug{i}") for i in range(2)]
                qlT, QLT = aget([128, 3, TT], BF16, "qlT")
                gM = [aget([128, TT], BF16, f"gM{i}") for i in range(2)]
                Vx, VX = aget([128, NKT, 4, 128], BF16, "Vx")
                PT = [aget([128, 512], BF16, f"PT{i}") for i in range(3)]
                stg = [aget([128, 288], F32, f"stg{i}") for i in range(4)]
                kst, KST = aget([128, 96], F32, "kst")
                t1, T1 = aget([128, 512], F32, "t1")
                t2, T2 = aget([128, 512], F32, "t2")
                t3, T3 = aget([128, 512], F32, "t3")
                rq, RQ = aget([128, 512], F32, "rq")
                sqm = [aget([128, 512], BF16, f"sqm{i}") for i in range(4)]
                rden, RDEN = rq, RQ
                rope, ROPE = aget([128, 2, TT], F32, "rope")
                ssq, SSQ = aget([128, NC], F32, "ssq")
                rsd, RSD = aget([128, NC], F32, "rsd")
                dma(kvu, D["w_kvu"][l].rearrange("(k p) c -> p k c", p=128), (), [KVU], q="pool")
                dma(rope[64:96], D[f"rope{j}"], (), [ROPE])
                dma(KRM[96:101, :], D[f"mku{j}"], (), [KRMB])
                for qi in range(2):
                    tg = qi % NTG
                    dma(Qaug[qi][0][96:101, :], D[f"mkw{j}"][:, tg * 512:(tg + 1) * 512], (), [Qaug[qi][1]])
                mset(kst, 0.0, [KST])
                mset(ssq, 0.0, [SSQ])
                vx5 = Vx.rearrange("p k (a two) e -> p k a two e", two=2)
                mset(vx5[:, :, :, 0, 64:128], 1.0, [VX])
                mset(vx5[:, :, :, 1, 0:64], 1.0, [VX])
                wkl, WKL = ws.get(wsrc(W_in, 1920, 256), 8, 256)
                wkr, WKR = ws.get(wsrc(D["w_krp"][l], 0, 192), 8, 192)
                pairs = [[n0, n0 + 1] for n0 in range(0, NC, 2)]
                bk, btk = {}, {}

                def m1_proj(pair):
                    for n in pair:
                        nsl = slice(n * 128, (n + 1) * 128)
                        bk[n] = nb()
                        for k in range(8):
                            mm(bank(bk[n])[:, 0:256], hT[:, k, nsl], wkl[:, k, :], k == 0, k == 7, [HT, WKL], [PSB[bk[n]]])
                        for k in range(8):
                            mm(bank(bk[n])[:, 256:288], hT[:, k, nsl], wkr[:, k, 64:96], k == 0, k == 7, [HT, WKR], [PSB[bk[n]]])

                def m1_norm(pair):
                    for n in pair:
                        sa, SA = sqm[n % 4]
                        act(sa[:, 0:256], bank(bk[n])[:, 0:256], AF.Square, [PSB[bk[n]]], [SA, SSQ], accum_out=ssq[:, n:n + 1])
                    for n in pair:
                        act(rsd[:, n:n + 1], ssq[:, n:n + 1], AF.Ln, [SSQ, EPSC], [RSD], scale=1.0 / 256, bias=epsc[:, 0:1])
                    for n in pair:
                        act(rsd[:, n:n + 1], rsd[:, n:n + 1], AF.Exp, [RSD], [RSD], scale=-0.5)
                    for n in pair:
                        sg, SG = stg[n % 4]
                        stt(sg[:, 0:256], bank(bk[n])[:, 0:256], rsd[:, n:n + 1], gkv[:, l, :], ALU.mult, ALU.mult, [PSB[bk[n]], RSD, GKV], [SG])
                        cp(sg[:, 256:288], bank(bk[n])[:, 256:288], [PSB[bk[n]]], [SG])
                    for n in pair:
                        sg, SG = stg[n % 4]
                        dma(D["o_ckv"][l, tok0 + n * 128:tok0 + (n + 1) * 128, :], sg[:, 0:256], [SG], ())
                        dma(D["o_kr"][l, tok0 + n * 128:tok0 + (n + 1) * 128, :], sg[:, 256:288], [SG], ())

                def m1_tr(pair):
                    for n in pair:
                        sg, SG = stg[n % 4]
                        btk[n] = nb()
                        for c2 in range(2):
                            tr(bank(btk[n])[:, c2 * 128:(c2 + 1) * 128], sg[:, c2 * 128:(c2 + 1) * 128], [SG], [PSB[btk[n]]])
                    for n in pair:
                        act(ckvT[:, :, NP + n * 128:NP + (n + 1) * 128], bank(btk[n])[:, 0:256].rearrange("p (c t) -> p c t", c=2),
                            AF.Copy, [PSB[btk[n]]], [CKVT])

                m1_proj(pairs[0])
                for pi, pair in enumerate(pairs):
                    m1_norm(pair)
                    if pi + 1 < len(pairs):
                        m1_proj(pairs[pi + 1])
                    m1_tr(pair)
                for pt in range(NP // 128):
                    sg, SG = stg[pt % 4]
                    dma(sg[:, 0:256], D["cckv"][l, pt * 128:(pt + 1) * 128, :], (), [SG])
                    dma(kst[:, 64:96], D["ckr"][l, pt * 128:(pt + 1) * 128, :], (), [KST])
                    bt = nb()
                    for c2 in range(2):
                        tr(bank(bt)[:, c2 * 128:(c2 + 1) * 128], sg[:, c2 * 128:(c2 + 1) * 128], [SG], [PSB[bt]])
                    tr(bank(bt)[0:96, 256:384], kst[:, 0:96], [KST], [PSB[bt]])
                    act(ckvT[:, :, pt * 128:(pt + 1) * 128], bank(bt)[:, 0:256].rearrange("p (c t) -> p c t", c=2),
                        AF.Copy, [PSB[bt]], [CKVT])
                    cp(KRM[64:96, pt * 128:(pt + 1) * 128], bank(bt)[64:96, 256:384], [PSB[bt]], [KRMB])
                for tg in range(NTG):
                    tgsl = slice(tg * 512, (tg + 1) * 512)
                    bA = nb()
                    for k in range(8):
                        mm(bank(bA)[0:96, :], wkr[:, k, 0:96], hT[:, k, tgsl], k == 0, k == 7, [WKR, HT], [PSB[bA]])
                    bB = nb()
                    for k in range(8):
                        mm(bank(bB)[0:96, :], wkr[:, k, 96:192], hT[:, k, tgsl], k == 0, k == 7, [WKR, HT], [PSB[bB]])
                    tt(t1[64:96], bank(bA)[64:96, :], rope[64:96, 0, tgsl], ALU.mult, [PSB[bA], ROPE], [T1])
                    tt(t2[64:96], bank(bB)[64:96, :], rope[64:96, 1, tgsl], ALU.mult, [PSB[bB], ROPE], [T2])
                    tt(KRM[64:96, NP + tg * 512:NP + (tg + 1) * 512], t1[64:96], t2[64:96], ALU.add, [T1, T2], [KRMB])
                act(Kaug[0][0][64:101, :], KRM[64:101, :], AF.Copy, [KRMB], [Kaug[0][1]])
                cp(Kaug[1][0][64:101, :], KRM[64:101, :], [KRMB], [Kaug[1][1]])
                wq0, WQ0 = ws.get(wsrc(W_in, 1536, 256), 8, 256)
                wq1, WQ1 = ws.get(wsrc(W_in, 1792, 128), 8, 128)
                for tg in range(NTG):
                    tgsl = slice(tg * 512, (tg + 1) * 512)
                    bc = [nb() for _ in range(3)]
                    for c in range(3):
                        for k in range(8):
                            if c < 2:
                                mm(bank(bc[c]), wq0[:, k, c * 128:(c + 1) * 128], hT[:, k, tgsl], k == 0, k == 7, [WQ0, HT], [PSB[bc[c]]])
                            else:
                                mm(bank(bc[c]), wq1[:, k, 0:128], hT[:, k, tgsl], k == 0, k == 7, [WQ1, HT], [PSB[bc[c]]])
                    bs = nb()
                    for c in range(3):
                        sa, SA = sqm[c % 2]
                        act(sa, bank(bc[c]), AF.Square, [PSB[bc[c]]], [SA])
                        mm(bank(bs), onesb[:], sa, c == 0, c == 2, [ONESB, SA], [PSB[bs]])
                    act(rq, bank(bs), AF.Ln, [PSB[bs], EPSC], [RQ], scale=1.0 / 384, bias=epsc[:, 0:1])
                    act(rq, rq, AF.Exp, [RQ], [RQ], scale=-0.5)
                    for c in range(3):
                        stt(qlT[:, c, tgsl], bank(bc[c]), gqT[:, l, c:c + 1], rq, ALU.mult, ALU.mult, [PSB[bc[c]], GQT, RQ], [QLT])
                kvu3 = kvu.rearrange("p k (h x) -> p k h x", h=8)
                wst = {}

                def prep_head(h):
                    c = h // 2
                    ka, KA = Kaug[h % 2]
                    gm, GMB = gM[c % 2]
                    if h % 2 == 0:
                        wst["wqu"] = ws.get(D["w_qu2"][l].rearrange("(c p) x -> p c x", p=128)[:, :, c * 384:(c + 1) * 384], 3, 384)
                    wst[("wqu", h)] = wst["wqu"]
                    for kg in range(NK // 512):
                        b = nb()
                        for c2 in range(2):
                            mm(bank(b)[0:64, :], kvu[:, c2, h * 128:h * 128 + 64], ckvT[:, c2, kg * 512:(kg + 1) * 512],
                               c2 == 0, c2 == 1, [KVU, CKVT], [PSB[b]])
                        act(ka[0:64, kg * 512:(kg + 1) * 512], bank(b)[0:64, :], AF.Copy, [PSB[b]], [KA])

                def prep_q(h, tg, bc, part="all"):
                    tgsl = slice(tg * 512, (tg + 1) * 512)
                    qa, QA = Qaug[bc % 2]
                    wqu, WQU = wst[("wqu", h)]
                    qo = (h % 2) * 192
                    bA = 4
                    bB = 5
                    if part in ("all", "pe"):
                        for c3 in range(3):
                            mm(bank(bA)[0:96, :], wqu[:, c3, qo:qo + 96], qlT[:, c3, tgsl], c3 == 0, c3 == 2, [WQU, QLT], [PSB[bA]])
                        for c3 in range(3):
                            mm(bank(bB)[0:96, :], wqu[:, c3, qo + 96:qo + 192], qlT[:, c3, tgsl], c3 == 0, c3 == 2, [WQU, QLT], [PSB[bB]])
                    if part == "pe":
                        return
                    act(qa[0:64, :], bank(bA)[0:64, :], AF.Copy, [PSB[bA]], [QA])
                    tt(t1[64:96], bank(bA)[64:96, :], rope[64:96, 0, tgsl], ALU.mult, [PSB[bA], ROPE], [T1])
                    tt(t2[64:96], bank(bB)[64:96, :], rope[64:96, 1, tgsl], ALU.mult, [PSB[bB], ROPE], [T2])
                    tt(qa[64:96, :], t1[64:96], t2[64:96], ALU.add, [T1, T2], [QA])

                def attn(h, tg, bc, nxt=None):
                    tgsl = slice(tg * 512, (tg + 1) * 512)
                    hh = h % 4
                    par = h % 2
                    po, pd = par * 64, (1 - par) * 64
                    c = h // 2
                    ka, KA = Kaug[h % 2]
                    gm, GMB = gM[c % 2]
                    qa, QA = Qaug[bc % 2]
                    bo = 6 + (bc % 2)
                    LA = 2
                    sb_ = {}

                    def score(kt):
                        sb_[kt] = nb()
                        mm(bank(sb_[kt]), ka[0:101, kt * 128:(kt + 1) * 128], qa[0:101, :], True, True, [KA, QA], [PSB[sb_[kt]]])
                    for kt in range(min(LA, NKT)):
                        score(kt)
                    if nxt is not None:
                        prep_q(nxt[0], nxt[1], nxt[2], "pe")
                    for kt in range(NKT):
                        pt_, PTB = PT[kt % 3]
                        act(pt_, bank(sb_[kt]), AF.Exp, [PSB[sb_[kt]]], [PTB], scale=SC)
                        if nxt is not None and kt == min(2, NKT - 1):
                            prep_q(nxt[0], nxt[1], nxt[2], "ev")
                        if kt + LA < NKT:
                            score(kt + LA)
                        mm(bank(bo), Vx[:, kt, hh, :], pt_, kt == 0, kt == NKT - 1, [VX, PTB], [PSB[bo]])
                    if j == 1:
                        act(rden[po:po + 64], bank(bo)[pd:pd + 64, :], AF.Ln, [PSB[bo]], [RDEN])
                        act(rden[po:po + 64], rden[po:po + 64], AF.Exp, [RDEN], [RDEN], scale=-1.0)
                    else:
                        recip(rden[po:po + 64], bank(bo)[pd:pd + 64, :], [PSB[bo]], [RDEN])
                    tt(t3[po:po + 64], bank(bo)[po:po + 64, :], rden[po:po + 64], ALU.mult, [PSB[bo], RDEN], [T3])
                    tt(brT[po:po + 64, 4 + c, tgsl], t3[po:po + 64], brT[po:po + 64, 8 + c, tgsl], ALU.mult, [T3, BRT], [BRT])

                for c in range(4):
                    if c % 2 == 0:
                        wmz, WMZ = ws.get(wsrc(W_in, 2208 + (c // 2) * 256, 256), 8, 256)
                    for tg in range(NTG):
                        tgsl = slice(tg * 512, (tg + 1) * 512)
                        b = nb()
                        for k in range(8):
                            mm(bank(b), wmz[:, k, (c % 2) * 128:(c % 2 + 1) * 128], hT[:, k, tgsl], k == 0, k == 7, [WMZ, HT], [PSB[b]])
                        act(brT[:, 8 + c, tgsl], bank(b), AF.Silu, [PSB[b]], [BRT])
                bc = 0
                st["nrot"] = 4
                for hg in range(2):
                    for kt in range(NKT):
                        b = nb()
                        for c2 in range(2):
                            mm(bank(b)[:, 0:256], ckvT[:, c2, kt * 128:(kt + 1) * 128], kvu3[:, c2, hg * 4:hg * 4 + 4, 64:128],
                               c2 == 0, c2 == 1, [CKVT, KVU], [PSB[b]])
                        bv = bank(b)[:, 0:256].rearrange("p (a two e) -> p a two e", two=2, e=64)
                        act(vx5[:, kt, :, 0, 0:64], bv[:, :, 0, :], AF.Copy, [PSB[b]], [VX])
                        cp(vx5[:, kt, :, 1, 64:128], bv[:, :, 1, :], [PSB[b]], [VX])
                    hblocks = [(h, tg) for h in range(hg * 4, hg * 4 + 4) for tg in range(NTG)]
                    prep_head(hblocks[0][0])
                    prep_q(hblocks[0][0], hblocks[0][1], bc)
                    for idx, (h, tg) in enumerate(hblocks):
                        nxt = None
                        if idx + 1 < len(hblocks):
                            h2, tg2 = hblocks[idx + 1]
                            if h2 != h:
                                prep_head(h2)
                            nxt = (h2, tg2, bc + 1)
                        attn(h, tg, bc, nxt)
                        bc += 1
                st["nrot"] = 6

                areset()
                fuT, FUT = aget([128, 4, TT], BF16, "fuT")
                gF, GFB = aget([128, 4, TT], BF16, "gF")
                AB, ABB = aget([128, NC, 4, 256], BF16, "AB")
                wf = [ws.get(wsrc(W_in, 2720 + i * 256, 256), 8, 256) for i in range(2)]
                for tg in range(NTG):
                    tgsl = slice(tg * 512, (tg + 1) * 512)
                    for g in range(4):
                        b = nb()
                        for k in range(8):
                            mm(bank(b), wf[g // 2][0][:, k, (g % 2) * 128:(g % 2 + 1) * 128], hT[:, k, tgsl], k == 0, k == 7,
                               [wf[g // 2][1], HT], [PSB[b]])
                        act(fuT[:, g, tgsl], bank(b), AF.Copy, [PSB[b]], [FUT])
                wfz = [ws.get(wsrc(W_in, 3232 + i * 256, 256), 8, 256) for i in range(2)]
                for tg in range(NTG):
                    tgsl = slice(tg * 512, (tg + 1) * 512)
                    for g in range(4):
                        b = nb()
                        for k in range(8):
                            mm(bank(b), wfz[g // 2][0][:, k, (g % 2) * 128:(g % 2 + 1) * 128], hT[:, k, tgsl], k == 0, k == 7,
                               [wfz[g // 2][1], HT], [PSB[b]])
                        act(gF[:, g, tgsl], bank(b), AF.Silu, [PSB[b]], [GFB])
                for n in range(NC):
                    nsl = slice(n * 128, (n + 1) * 128)
                    bb = [nb(), nb()]
                    for g in range(4):
                        mm(bank(bb[g // 2])[:, (g % 2) * 256:(g % 2 + 1) * 256], fuT[:, g, nsl], cwsw[:], True, True, [FUT, CWSW], [PSB[bb[g // 2]]])
                    act(AB[:, n, 0:2, :], bank(bb[0]).rearrange("p (g x) -> p g x", g=2), AF.Copy, [PSB[bb[0]]], [ABB])
                    cp(AB[:, n, 2:4, :], bank(bb[1]).rearrange("p (g x) -> p g x", g=2), [PSB[bb[1]]], [ABB])
                for kb in range(TT // 256):
                    wc, WC = ws.get(wsrc(D[f"dftc{j}"], kb * 256, 256), NC, 256)
                    wsn, WSN = ws.get(wsrc(D[f"dfts{j}"], kb * 256, 256), NC, 256)
                    ksl = slice(kb * 256, (kb + 1) * 256)
                    for g in range(4):
                        b = nb()
                        for n in range(NC):
                            mm(bank(b)[:, 0:256], AB[:, n, g, 0:128], wc[:, n, :], n == 0, False, [ABB, WC], [PSB[b]])
                            mm(bank(b)[:, 0:256], AB[:, n, g, 128:256], wsn[:, n, :], False, n == NC - 1, [ABB, WSN], [PSB[b]])
                        tt(brT[:, 8 + g, ksl], bank(b)[:, 0:256], gF[:, g, ksl], ALU.mult, [PSB[b], GFB], [BRT])

                ck(7)
                areset()
                mT, MT = aget([128, 8, TT], BF16, "mT")
                sig = [aget([128, 512], F32, f"sig{i}") for i in range(3)]
                tm = [aget([128, 512], F32, f"tm{i}") for i in range(3)]
                for cpair in range(4):
                    gl = [ws.get(wsrc(W_in, 3744 + n * 1024 + cpair * 256, 256), 8, 256) for n in range(3)]
                    wb = [ws.get(wsrc(D["w_br"][l, n], cpair * 256, 256), 4, 256) for n in range(3)]
                    for cc in range(2):
                        c = cpair * 2 + cc
                        csl = slice(cc * 128, (cc + 1) * 128)
                        for tg in range(NTG):
                            tgsl = slice(tg * 512, (tg + 1) * 512)
                            for n in range(3):
                                bg = nb()
                                for k in range(8):
                                    mm(bank(bg), gl[n][0][:, k, csl], hT[:, k, tgsl], k == 0, k == 7, [gl[n][1], HT], [PSB[bg]])
                                act(sig[n][0], bank(bg), AF.Sigmoid, [PSB[bg]], [sig[n][1]])
                                bp = nb()
                                for k4 in range(4):
                                    mm(bank(bp), wb[n][0][:, k4, csl], brT[:, n * 4 + k4, tgsl], k4 == 0, k4 == 3, [wb[n][1], BRT], [PSB[bp]])
                                tt(tm[n][0], bank(bp), sig[n][0], ALU.mult, [PSB[bp], sig[n][1]], [tm[n][1]])
                            tt(tm[0][0], tm[0][0], tm[1][0], ALU.add, [tm[0][1], tm[1][1]], [tm[0][1]])
                            tt(mT[:, c, tgsl], tm[0][0], tm[2][0], ALU.add, [tm[0][1], tm[2][1]], [MT])
                for cpair in range(4):
                    wo, WO = ws.get(wsrc(D["w_out"][l], cpair * 256, 256), 8, 256)
                    for cc in range(2):
                        c = cpair * 2 + cc
                        csl = slice(cc * 128, (cc + 1) * 128)
                        for tg in range(NTG):
                            tgsl = slice(tg * 512, (tg + 1) * 512)
                            b = nb()
                            for k in range(8):
                                mm(bank(b), wo[:, k, csl], mT[:, k, tgsl], k == 0, k == 7, [WO, MT], [PSB[b]])
                            stt(xT[:, c, tgsl], bank(b), modT[:, l, 16 + c, j:j + 1], xT[:, c, tgsl], ALU.mult, ALU.add,
                                [PSB[b], MODT, XT], [XT])

                ck(8 + l)
            areset()
            sq = [aget([128, 512], BF16, f"fsq{i}") for i in range(2)]
            rsb = [aget([128, 512], F32, f"frs{i}") for i in range(NTG)]
            yT, YT = aget([128, 8, 512], F32, "yT")
            yst = [aget([128, 1024], F32, f"yst{i}") for i in range(2)]
            for tg in range(NTG):
                norm_stats(rsb[tg][0], rsb[tg][1], sq, slice(tg * 512, (tg + 1) * 512))
            for tg in range(NTG):
                tgsl = slice(tg * 512, (tg + 1) * 512)
                rs, RS = rsb[tg]
                for k in range(8):
                    stt(yT[:, k, :], xT[:, k, tgsl], fgT[:, k:k + 1], rs, ALU.mult, ALU.mult, [XT, FGT, RS], [YT])
                for c4 in range(4):
                    n = tg * 4 + c4
                    ya, YA = yst[n % 2]
                    bA, bB = nb(), nb()
                    for k in range(8):
                        bk = bA if k < 4 else bB
                        tr(bank(bk)[:, (k % 4) * 128:(k % 4 + 1) * 128], yT[:, k, c4 * 128:(c4 + 1) * 128], [YT], [PSB[bk]])
                    cp(ya[:, 0:512], bank(bA), [PSB[bA]], [YA])
                    act(ya[:, 512:1024], bank(bB), AF.Copy, [PSB[bB]], [YA])
                    dma(D["y"][tok0 + n * 128:tok0 + (n + 1) * 128, :], ya, [YA], ())

    S.plan = True
    try:
        body()
    except _Stop:
        pass
    S.plan = False
    try:
        body()
    except _Stop:
        pass
    print('ops', S.cnt, {q: sum(v) // 16 for q, v in S.dcnt.items()}, flush=True)
    S.finish()
    S.emit()
    return nc


def _consts(is_sample):
    bf = ml_dtypes.bfloat16
    c = {}
    c["ident"] = np.eye(128, dtype=np.float32)
    ci = np.arange(128)
    ang = 2 * np.pi * np.outer(ci, ci) / 128
    c["cwsw"] = np.concatenate([np.cos(ang), np.sin(ang)], 1).astype(bf)

    def dft(T, seq):
        t = np.arange(T)
        same = (t[:, None] // seq) == (t[None, :] // seq)
        a = 2 * np.pi * np.outer(t % seq, t % seq) / seq
        nrm = (seq * 128) ** -0.5
        return (np.cos(a) * same * nrm).astype(bf), (-np.sin(a) * same * nrm).astype(bf)
    c["dftc0"], c["dfts0"] = dft(1024, 1024 if is_sample else 256)
    c["dftc1"], c["dfts1"] = dft(512, 256)

    def rope(T, real):
        out = np.zeros((32, 2, T), np.float32)
        out[:, 0, :] = 1.0
        if real:
            t = np.arange(T)
            pos = [t // 64, t % 64]
            inv = 10000.0 ** (-np.arange(8, dtype=np.float32) / 8)
            for d in range(32):
                hh, e = d // 16, d % 16
                f, second = e % 8, e // 8
                a = pos[hh].astype(np.float32) * inv[f]
                out[d, 0] = np.cos(a)
                out[d, 1] = np.sin(a) * (1.0 if second else -1.0)
        return out
    c["rope0"] = rope(1024, is_sample)
    c["rope1"] = rope(512, False)

    def masks(T, NP, seq, past_ok):
        NK = NP + T
        U = np.zeros((5, NK), np.float32)
        W = np.full((5, T), -BIG, np.float32)
        U[0, :NP] = 1
        if past_ok:
            W[0, :] = 0
        kj = np.arange(T)
        for s in range(T // seq):
            U[1 + s, NP + s * seq:NP + (s + 1) * seq] = 1
            W[1 + s, s * seq:(s + 1) * seq] = 0
        return U.astype(bf), W.astype(bf)
    c["mku0"], c["mkw0"] = masks(1024, 512, 1024 if is_sample else 256, is_sample)
    c["mku1"], c["mkw1"] = masks(512, 0, 256, False)
    keep = np.ones((2, 2, 8), np.float32)

    def packed(NC):
        kf = np.array([0.0 if n % 2 == 0 else 1.0 for n in range(8)], np.float32)
        kb = np.array([0.0 if n % 2 == 1 else 1.0 for n in range(8)], np.float32)
        return kf, kb
    if not is_sample:
        keep[0, 0], keep[0, 1] = packed(8)
    keep[1, 0], keep[1, 1] = packed(4)
    c["keep"] = np.broadcast_to(keep.reshape(1, 32), (128, 32)).copy()
    jj, ii = np.meshgrid(np.arange(128), np.arange(128), indexing="ij")
    rc = np.zeros((128, 4, 128), np.float32)
    rc[:, 0] = np.maximum(ii - jj, 0)
    rc[:, 1] = np.maximum(jj - ii, 0)
    rc[:, 2] = np.where(ii == jj, 2.0, 1.0) * 0.125
    rc[:64, 3] = np.arange(128)[None, :] + 1
    rc[64:, 3] = 128 - np.arange(128)[None, :]
    c["rc"] = rc
    c["ez"] = np.stack([127 - np.arange(128), np.arange(128)], 1).astype(np.float32)
    return c


_NC_CACHE = {}


def kernel(x_prompt, x_sample, cache_ckv, cache_krope, state_ret, c, c_ctx, norm_g, w_mod, b_mod, w_in,
           ret_decay_logit, q_norm_g, w_q_up, kv_norm_g, w_kv_up, w_branch, w_out, final_norm_g):
    f32 = np.float32
    A = lambda a: np.ascontiguousarray(np.asarray(a, dtype=f32))
    x_prompt, x_sample, cache_ckv, cache_krope, state_ret = map(A, (x_prompt, x_sample, cache_ckv, cache_krope, state_ret))
    c, c_ctx, norm_g, w_mod, b_mod, w_in = map(A, (c, c_ctx, norm_g, w_mod, b_mod, w_in))
    ret_decay_logit, q_norm_g, w_q_up, kv_norm_g, w_kv_up, w_branch, w_out, final_norm_g = map(
        A, (ret_decay_logit, q_norm_g, w_q_up, kv_norm_g, w_kv_up, w_branch, w_out, final_norm_g))
    if "nc" not in _NC_CACHE:
        _NC_CACHE["nc"] = build_program()
    nc = _NC_CACHE["nc"]

    rq = w_in[:, :, 0:256].reshape(2, 1024, 4, 1, 64)
    w_rqd = A(np.broadcast_to(rq, (2, 1024, 4, 2, 64)).reshape(2, 1024, 512))
    swap = np.array([(d // 16) * 16 + ((d % 16) + 8) % 16 for d in range(32)])
    kr = w_in[:, :, 2176:2208]
    z64 = np.zeros((2, 1024, 64), f32)
    w_krp = A(np.concatenate([z64, kr, z64, kr[:, :, swap]], 2))
    qu = w_q_up.reshape(2, 384, 8, 96)
    qsw = np.concatenate([np.zeros((2, 384, 8, 64), f32), qu[:, :, :, 64:][:, :, :, swap]], 3)
    w_qu2 = A(np.concatenate([qu, qsw], 3).reshape(2, 384, 1536))
    fm = lambda v, k: A(v.reshape(k, 128).T)
    shared = dict(
        ngT=A(np.stack([fm(norm_g[l], 8) for l in range(2)], 1)),
        bmT=A(np.stack([fm(b_mod[l], 24) for l in range(2)], 1)),
        fgT=fm(final_norm_g, 8),
        gqT=A(np.stack([fm(q_norm_g[l], 3) for l in range(2)], 1)),
        gkv=A(kv_norm_g.reshape(512)), rdl=A(ret_decay_logit.reshape(16)),
        w_mod=w_mod, w_in=w_in, w_rqd=w_rqd, w_krp=w_krp, w_qu2=w_qu2, w_kvu=w_kv_up, w_br=w_branch, w_out=w_out,
    )
    cs = {True: _consts(True), False: _consts(False)}
    in_maps = []
    for core in range(8):
        samp = core < 4
        sp = [2 * core, 2 * core + 1]
        if samp:
            xl = x_sample[core]
            cv_l = c[core]
            cck, ckr_, s0 = cache_ckv[core], cache_krope[core], state_ret[core]
        else:
            lp = [16 + 4 * (core - 4) + s for s in range(4)]
            xl = x_prompt[lp].reshape(1024, 1024)
            cv_l = c_ctx
            cck, ckr_, s0 = np.zeros((2, 512, 256), f32), np.zeros((2, 512, 32), f32), np.zeros((2, 2, 4, 64, 128), f32)
        xin = A(np.concatenate([xl, x_prompt[sp].reshape(512, 1024)], 0))
        cvT = A(np.stack([fm(cv_l, 8), fm(c_ctx, 8)], 2))
        m = dict(shared)
        m.update(cs[samp])
        m.update(xin=xin, cvT=cvT, cckv=A(cck), ckr=A(ckr_), s0=A(s0))
        in_maps.append(m)
    res = run_bass_kernel_spmd(nc, in_maps, core_ids=list(range(8)))
    R = res.results
    y_prompt = np.zeros((32, 256, 1024), f32)
    y_sample = np.zeros((4, 1024, 1024), f32)
    new_ckv = np.zeros((32, 2, 256, 256), f32)
    new_kr = np.zeros((32, 2, 256, 32), f32)
    new_ret = np.zeros((32, 2, 2, 4, 64, 128), f32)
    for b in range(32):
        if b < 16:
            core, off, slot = b // 2, 1024 + (b % 2) * 256, 4 + (b % 2)
        else:
            core, off, slot = 4 + (b - 16) // 4, ((b - 16) % 4) * 256, (b - 16) % 4
        r = R[core]
        y_prompt[b] = r["y"][off:off + 256]
        new_ckv[b] = r["o_ckv"][:, off:off + 256]
        new_kr[b] = r["o_kr"][:, off:off + 256]
        st_ = r["o_ret"][:, slot]
        new_ret[b] = st_.reshape(2, 4, 2, 64, 128).transpose(0, 2, 1, 3, 4)
    for b in range(4):
        y_sample[b] = R[b]["y"][0:1024]
    return (y_prompt, y_sample, new_ckv, new_kr, new_ret)
```

```python
import os
import numpy as np
import ml_dtypes
import concourse.bass as bass
import concourse.mybir as mybir
from concourse.bass_utils import run_bass_kernel_spmd

F32, BF16 = mybir.dt.float32, mybir.dt.bfloat16
AF = mybir.ActivationFunctionType
ALU = mybir.AluOpType
AX = mybir.AxisListType
EPS = 1e-6
NSLOT = 12
PF = 5
JOBS = [(0, 1024, 512), (1024, 512, 0)]
SLOT0 = [0, 4]
NTOK = 1536
SC = float(96 ** -0.5)
BIG = 30000.0
ARENA = 73728
STAGE = float(os.environ.get('KSTAGE', '999'))


class _Stop(Exception):
    pass


def ck(n):
    if n >= STAGE:
        raise _Stop()


class Buf:
    __slots__ = ("w", "r", "name", "excl")

    def __init__(self, name="", excl=False):
        self.w = None
        self.r = {}
        self.name = name
        self.excl = excl


class Sched:
    ENG = ("pe", "act", "dve", "pool", "sp")

    def __init__(self, nc, n_dma_sems=10):
        self.nc = nc
        self.plan = False
        self.prog = {e: [] for e in self.ENG}
        self.sem = {e: nc.alloc_semaphore(f"c_{e}") for e in self.ENG}
        self.cnt = {e: 0 for e in self.ENG}
        self.waited = {e: {} for e in self.ENG}
        self.nd = n_dma_sems
        self.dsem = {q: [nc.alloc_semaphore(f"d_{q}{i}") for i in range(n_dma_sems)] for q in ("sp", "pool")}
        self.dcnt = {q: [0] * n_dma_sems for q in self.dsem}
        self.didx = {q: 0 for q in self.dsem}

    def _collect(self, reads, writes):
        deps = {}
        for b in reads:
            if b.w is not None and deps.get(b.w[0], 0) < b.w[1]:
                deps[b.w[0]] = b.w[1]
            if b.excl:
                for s, v in b.r.items():
                    if deps.get(s, 0) < v:
                        deps[s] = v
        for b in writes:
            if b.w is not None and deps.get(b.w[0], 0) < b.w[1]:
                deps[b.w[0]] = b.w[1]
            for s, v in b.r.items():
                if deps.get(s, 0) < v:
                    deps[s] = v
        return deps

    def _waits(self, eng, deps):
        waits = []
        wd = self.waited[eng]
        for s, v in deps.items():
            if eng == "pe" and s is self.sem["pe"]:
                continue
            if wd.get(s, 0) >= v:
                continue
            wd[s] = v
            waits.append((s, v))
        return waits

    def _commit(self, ev, reads, writes):
        s, v = ev
        for b in reads:
            if b.r.get(s, 0) < v:
                b.r[s] = v
        for b in writes:
            b.w = ev
            b.r = {}

    def op(self, eng, fn, reads=(), writes=()):
        if self.plan:
            return
        waits = self._waits(eng, self._collect(reads, writes))
        self.cnt[eng] += 1
        ev = (self.sem[eng], self.cnt[eng])
        self.prog[eng].append((waits, fn, ev[0], 1))
        self._commit(ev, reads, writes)

    def dma(self, q, fn, reads=(), writes=()):
        if self.plan:
            return
        i = self.didx[q]
        self.didx[q] = (i + 1) % self.nd
        sem = self.dsem[q][i]
        prev = self.dcnt[q][i]
        deps = self._collect(reads, writes)
        if prev > 0 and deps.get(sem, 0) < prev:
            deps[sem] = prev
        waits = self._waits(q, deps)
        self.dcnt[q][i] = prev + 16
        ev = (sem, prev + 16)
        self.prog[q].append((waits, fn, sem, 16))
        self._commit(ev, reads, writes)

    def barrier(self):
        if self.plan:
            return
        for e in self.ENG:
            deps = {}
            for e2 in self.ENG:
                if e2 != e and self.cnt[e2] > 0:
                    deps[self.sem[e2]] = self.cnt[e2]
            for i, s in enumerate(self.dsem["sp"]):
                if self.dcnt["sp"][i] > 0:
                    deps[s] = self.dcnt["sp"][i]
            waits = self._waits(e, deps)
            if waits:
                self.prog[e].append((waits, None, None, 0))

    def finish(self):
        waits = []
        for q in self.dsem:
            for i, s in enumerate(self.dsem[q]):
                if self.dcnt[q][i] > 0:
                    waits.append((s, self.dcnt[q][i]))
        for e in self.ENG:
            if e != "sp" and self.cnt[e] > 0:
                waits.append((self.sem[e], self.cnt[e]))
        self.prog["sp"].append((waits, None, None, 0))

    def emit(self):
        prog = self.prog

        def replay(name, eng):
            for waits, fn, sem, inc in prog[name]:
                for s, v in waits:
                    eng.wait_ge(s, v)
                if fn is not None:
                    fn(eng).then_inc(sem, inc)

        with self.nc.Block() as block:
            @block.tensor
            def _(e):
                replay("pe", e)

            @block.scalar
            def _(e):
                replay("act", e)

            @block.vector
            def _(e):
                replay("dve", e)

            @block.gpsimd
            def _(e):
                replay("pool", e)

            @block.sync
            def _(e):
                replay("sp", e)


def build_program():
    nc = bass.Bass("TRN2", target_bir_lowering=False)
    S = Sched(nc)

    def din(name, shape, dt=F32):
        return nc.dram_tensor(name, list(shape), dt, kind="ExternalInput").ap()

    def dout(name, shape):
        return nc.dram_tensor(name, list(shape), F32, kind="ExternalOutput").ap()

    D = dict(
        xin=din("xin", [NTOK, 1024]), cvT=din("cvT", [128, 8, 2]),
        cckv=din("cckv", [2, 512, 256]), ckr=din("ckr", [2, 512, 32]), s0=din("s0", [2, 2, 4, 64, 128]),
        ngT=din("ngT", [128, 2, 8]), bmT=din("bmT", [128, 2, 24]), fgT=din("fgT", [128, 8]),
        gqT=din("gqT", [128, 2, 3]), gkv=din("gkv", [512]), rdl=din("rdl", [16]),
        w_mod=din("w_mod", [2, 1024, 3072]), w_in=din("w_in", [2, 1024, 6816]),
        w_rqd=din("w_rqd", [2, 1024, 512]), w_krp=din("w_krp", [2, 1024, 192]),
        w_qu2=din("w_qu2", [2, 384, 1536]), w_kvu=din("w_kvu", [2, 256, 1024]),
        w_br=din("w_br", [2, 3, 512, 1024]), w_out=din("w_out", [2, 1024, 1024]),
        ident=din("ident", [128, 128]), cwsw=din("cwsw", [128, 256], BF16),
        dftc0=din("dftc0", [1024, 1024], BF16), dfts0=din("dfts0", [1024, 1024], BF16),
        dftc1=din("dftc1", [512, 512], BF16), dfts1=din("dfts1", [512, 512], BF16),
        rope0=din("rope0", [32, 2, 1024]), rope1=din("rope1", [32, 2, 512]),
        mku0=din("mku0", [5, 1536], BF16), mkw0=din("mkw0", [5, 1024], BF16),
        mku1=din("mku1", [5, 512], BF16), mkw1=din("mkw1", [5, 512], BF16),
        keep=din("keep", [128, 32]), rc=din("rc", [128, 4, 128]), ez=din("ez", [128, 2]),
        y=dout("y", [NTOK, 1024]), o_ckv=dout("o_ckv", [2, NTOK, 256]), o_kr=dout("o_kr", [2, NTOK, 32]),
        o_ret=dout("o_ret", [2, 6, 4, 128, 128]),
    )

    def T(name, shape, dt):
        return nc.alloc_sbuf_tensor("sb_" + name, list(shape), dt), Buf(name)

    xT, XT = T("xT", [128, 8, 1024], F32)
    hT, HT = T("hT", [128, 8, 1024], BF16)
    brT, BRT = T("brT", [128, 12, 1024], BF16)
    slots = [T(f"ws{i}", [128, 2048], BF16) for i in range(NSLOT)]
    ident, IDENT = T("ident", [128, 128], F32)
    onesb, ONESB = T("onesb", [128, 128], BF16)
    cwsw, CWSW = T("cwsw", [128, 256], BF16)
    rc, RC = T("rc", [128, 4, 128], F32)
    ez, EZ = T("ez", [128, 2], F32)
    lg, LG = T("lg", [128, 16], F32)
    lgcol, LGCOL = T("lgcol", [128, 8], F32)
    Dm, DM = T("Dm", [128, 8, 128], F32)
    Xi, XI = T("Xi", [128, 8, 128], F32)
    Zt, ZT = T("Zt", [128, 2, 4, 2], F32)
    gC, GC = T("gC", [128, 16], F32)
    keep, KEEP = T("keep", [128, 2, 2, 8], F32)
    modT, MODT = T("modT", [128, 2, 24, 2], F32)
    g1, G1 = T("g1", [128, 2, 2, 8], F32)
    ngT, NGT = T("ngT", [128, 2, 8], F32)
    bmT, BMT = T("bmT", [128, 2, 24], F32)
    fgT, FGT = T("fgT", [128, 8], F32)
    gqT, GQT = T("gqT", [128, 2, 3], F32)
    gkv, GKV = T("gkv", [128, 2, 256], F32)
    cvT, CVT = T("cvT", [128, 8, 2], F32)
    scv, SCV = T("scv", [128, 8, 2], BF16)
    epsc, EPSC = T("epsc", [128, 1], F32)
    tmp16, TMP16 = T("tmp16", [128, 16], F32)
    arena = nc.alloc_sbuf_tensor("arena", [128, ARENA // 2], BF16)
    ps = nc.alloc_psum_tensor("ps", [128, 4096], F32)
    PSB = [Buf(f"ps{i}", excl=True) for i in range(8)]
    st = {"rot": 0, "aoff": 0}

    def bank(i):
        return ps[:, i * 512:(i + 1) * 512]

    def nb():
        i = st["rot"] % st.get("nrot", 6)
        st["rot"] = i + 1
        return i

    def areset():
        inh = st.setdefault("inh", {})
        for b in st.setdefault("pbufs", []):
            if b.w is not None and inh.get(b.w[0], 0) < b.w[1]:
                inh[b.w[0]] = b.w[1]
            for s_, v_ in b.r.items():
                if inh.get(s_, 0) < v_:
                    inh[s_] = v_
        st["pbufs"] = []
        st["aoff"] = 0

    def aget(shape, dt, name=""):
        assert shape[0] == 128
        fs = list(shape[1:])
        n = int(np.prod(fs))
        nbytes = n * (4 if dt == F32 else 2)
        off = st["aoff"]
        st["aoff"] = off + ((nbytes + 31) // 32) * 32
        assert st["aoff"] <= ARENA, (name, st["aoff"])
        a = arena[:, off // 2:(off + nbytes) // 2]
        if dt == F32:
            a = a.bitcast(F32)
        if len(fs) == 2:
            a = a.rearrange("p (a b) -> p a b", b=fs[1])
        elif len(fs) == 3:
            a = a.rearrange("p (a b c) -> p a b c", b=fs[1], c=fs[2])
        nbuf = Buf(name)
        nbuf.r = dict(st.get("inh", {}))
        st.setdefault("pbufs", []).append(nbuf)
        return a, nbuf

    def mm(out, lhsT, rhs, start, stop, rd, wr):
        S.op("pe", lambda e: e.matmul(out, lhsT=lhsT, rhs=rhs, start=start, stop=stop), rd, wr)

    def tr(out, in_, rd, wr):
        S.op("pe", lambda e: e.transpose(out, in_, ident[:]), list(rd) + [IDENT], wr)

    def act(out, in_, func, rd, wr, **kw):
        S.op("act", lambda e: e.activation(out=out, in_=in_, func=func, **kw), rd, wr)

    def tt(out, in0, in1, op, rd, wr):
        S.op("dve", lambda e: e.tensor_tensor(out=out, in0=in0, in1=in1, op=op), rd, wr)

    def ts(out, in0, s1, op0, rd, wr, s2=None, op1=None):
        if op1 is None:
            S.op("dve", lambda e: e.tensor_scalar(out=out, in0=in0, scalar1=s1, scalar2=None, op0=op0), rd, wr)
        else:
            S.op("dve", lambda e: e.tensor_scalar(out=out, in0=in0, scalar1=s1, scalar2=s2, op0=op0, op1=op1), rd, wr)

    def stt(out, in0, scalar, in1, op0, op1, rd, wr):
        S.op("dve", lambda e: e.scalar_tensor_tensor(out=out, in0=in0, scalar=scalar, in1=in1, op0=op0, op1=op1), rd, wr)

    def pool_ts(out, in0, s1, op0, rd, wr):
        S.op("pool", lambda e: e.tensor_scalar(out=out, in0=in0, scalar1=s1, scalar2=None, op0=op0), rd, wr)

    def pool_tt(out, in0, in1, op, rd, wr):
        S.op("pool", lambda e: e.tensor_tensor(out=out, in0=in0, in1=in1, op=op), rd, wr)

    def cp(out, in_, rd, wr):
        S.op("dve", lambda e: e.tensor_copy(out=out, in_=in_), rd, wr)

    def recip(out, in_, rd, wr):
        S.op("dve", lambda e: e.reciprocal(out=out, in_=in_), rd, wr)

    def rsum(out, in_, rd, wr):
        S.op("dve", lambda e: e.tensor_reduce(out=out, in_=in_, axis=AX.X, op=ALU.add), rd, wr)

    def mset(ap, val, wr):
        S.op("dve", lambda e: e.memset(ap, val), (), wr)

    def dma(out, in_, rd, wr, q="sp"):
        S.dma(q, lambda e: e.dma_start(out=out, in_=in_), rd, wr)

    class WS:
        def __init__(self):
            self.specs = []
            self.i = 0
            self.issued = 0

        def get(self, src, kc, ncols):
            if S.plan:
                self.specs.append((src, kc, ncols))
                t, b = slots[0]
                return t[:, 0:kc * ncols].rearrange("p (k c) -> p k c", c=ncols), b
            while self.issued < min(len(self.specs), self.i + PF + 1):
                j = self.issued
                s_src, s_kc, s_nc = self.specs[j]
                t, b = slots[j % NSLOT]
                dma(t[:, 0:s_kc * s_nc].rearrange("p (k c) -> p k c", c=s_nc), s_src, (), [b], q="pool")
                self.issued += 1
            t, b = slots[self.i % NSLOT]
            self.i += 1
            return t[:, 0:kc * ncols].rearrange("p (k c) -> p k c", c=ncols), b

    ws = WS()

    def wsrc(ap2d, c0, ncols):
        return ap2d.rearrange("(k p) c -> p k c", p=128)[:, :, c0:c0 + ncols]

    def body():
        st["rot"] = 0
        st["aoff"] = 0
        st["inh"] = {}
        st["pbufs"] = []
        dma(ident[:], D["ident"], (), [IDENT])
        dma(cwsw[:], D["cwsw"], (), [CWSW])
        dma(rc[:], D["rc"], (), [RC])
        dma(ez[:], D["ez"], (), [EZ])
        dma(keep[:], D["keep"].rearrange("p (a b c) -> p a b c", a=2, b=2), (), [KEEP])
        dma(ngT[:], D["ngT"], (), [NGT])
        dma(bmT[:], D["bmT"], (), [BMT])
        dma(fgT[:], D["fgT"], (), [FGT])
        dma(gqT[:], D["gqT"], (), [GQT])
        dma(cvT[:], D["cvT"], (), [CVT])
        dma(gkv[:], D["gkv"].partition_broadcast(128).rearrange("p (a b) -> p a b", a=2), (), [GKV])
        dma(lg[:], D["rdl"].partition_broadcast(128), (), [LG])
        ck(0.2)
        mset(onesb[:], 1.0, [ONESB])
        mset(epsc[:], EPS, [EPSC])
        act(lg[:], lg[:], AF.Exp, [LG], [LG], scale=-1.0)
        act(lg[:], lg[:], AF.Ln, [LG], [LG], bias=1.0)
        ts(lg[:], lg[:], -1.0, ALU.mult, [LG], [LG])
        ck(0.4)
        lgv = lg[:].rearrange("p (l d h) -> p l d h", l=2, d=2)
        lgc = lgcol[:].rearrange("p (l h) -> p l h", l=2)
        cp(lgc[0:64], lgv[0:64, :, 0, :], [LG], [LGCOL])
        cp(lgc[64:128], lgv[64:128, :, 1, :], [LG], [LGCOL])
        act(gC[:], lg[:], AF.Exp, [LG], [GC], scale=128.0)
        ck(0.6)
        for l in range(2):
            for h in range(4):
                jf, jb, j = l * 8 + h, l * 8 + 4 + h, l * 4 + h
                ts(Dm[:, j, :], rc[:, 0, :], lg[:, jf:jf + 1], ALU.mult, [RC, LG], [DM])
                stt(Dm[:, j, :], rc[:, 1, :], lg[:, jb:jb + 1], Dm[:, j, :], ALU.mult, ALU.add, [RC, LG, DM], [DM])
                act(Dm[:, j, :], Dm[:, j, :], AF.Exp, [DM], [DM])
                tt(Dm[:, j, :], Dm[:, j, :], rc[:, 2, :], ALU.mult, [DM, RC], [DM])
                ck(0.7)
                act(Xi[:, j, :], rc[:, 3, :], AF.Exp, [RC, LGCOL], [XI], scale=lgcol[:, j:j + 1])
                ck(0.8)
                for d in range(2):
                    jj = l * 8 + d * 4 + h
                    act(Zt[:, l, h, d:d + 1], ez[:, d:d + 1], AF.Exp, [EZ, LG], [ZT], scale=lg[:, jj:jj + 1])
                ck(0.9)
        ck(0.95)
        ts(Zt[:], Zt[:], 0.125, ALU.mult, [ZT], [ZT])
        ck(0.97)
        ck(1)
        act(scv[:], cvT[:], AF.Silu, [CVT], [SCV])

        def emit_mod(l, b0=0, b1=12, with_g1=True):
            for blk in range(b0, b1):
                w, WB = ws.get(wsrc(D["w_mod"][l], blk * 256, 256), 8, 256)
                for fc in range(2):
                    f = blk * 2 + fc
                    for k in range(8):
                        mm(bank(7)[:, f * 2:f * 2 + 2], w[:, k, fc * 128:fc * 128 + 128], scv[:, k, :], k == 0, k == 7,
                           [WB, SCV], [PSB[7]])
            f0, f1 = 2 * b0, 2 * b1
            tt(modT[:, l, f0:f1, :], bank(7)[:, f0 * 2:f1 * 2].rearrange("p (f j) -> p f j", j=2),
               bmT[:, l, f0:f1].unsqueeze(2).broadcast_to([128, f1 - f0, 2]), ALU.add, [PSB[7], BMT], [MODT])
            if with_g1:
                for j in range(2):
                    ts(g1[:, l, j, :], modT[:, l, 8:16, j], 1.0, ALU.add, [MODT], [G1])
                    tt(g1[:, l, j, :], g1[:, l, j, :], ngT[:, l, :], ALU.mult, [G1, NGT], [G1])

        ck(2)
        for j, (tok0, TT, NP) in enumerate(JOBS):
            NC, NTG, NK = TT // 128, TT // 512, NP + TT
            NKT = NK // 128
            areset()
            xs = [aget([128, 1024], F32, f"xs{i}") for i in range(2)]
            for n in range(NC):
                xa, XA = xs[n % 2]
                dma(xa, D["xin"][tok0 + n * 128:tok0 + (n + 1) * 128, :], (), [XA])
                for kq in range(2):
                    b = nb()
                    for c in range(4):
                        k = kq * 4 + c
                        tr(bank(b)[:, c * 128:(c + 1) * 128], xa[:, k * 128:(k + 1) * 128], [XA], [PSB[b]])
                    o = xT[:, kq * 4:kq * 4 + 4, n * 128:(n + 1) * 128]
                    i_ = bank(b).rearrange("p (c t) -> p c t", c=4)
                    if kq == 0:
                        cp(o, i_, [PSB[b]], [XT])
                    else:
                        act(o, i_, AF.Copy, [PSB[b]], [XT])

            if j == 0:
                emit_mod(0, 0, 8)
            ck(3)

            def norm_stats(rs, RS, sq, tgsl):
                b = nb()
                for k in range(8):
                    sa, SA = sq[k % 2]
                    act(sa, xT[:, k, tgsl], AF.Square, [XT], [SA])
                    mm(bank(b), onesb[:], sa, k == 0, k == 7, [ONESB, SA], [PSB[b]])
                act(rs, bank(b), AF.Ln, [PSB[b], EPSC], [RS], scale=1.0 / 1024, bias=epsc[:, 0:1])
                act(rs, rs, AF.Exp, [RS], [RS], scale=-0.5)

            for l in range(2):
                W_in = D["w_in"][l]
                areset()
                sq = [aget([128, 512], BF16, f"sq{i}") for i in range(2)]
                rsb = [aget([128, 512], F32, f"rs{i}") for i in range(NTG)]
                tf = [aget([128, 512], F32, f"tf{i}") for i in range(2)]
                for tg in range(NTG):
                    norm_stats(rsb[tg][0], rsb[tg][1], sq, slice(tg * 512, (tg + 1) * 512))
                for tg in range(NTG):
                    tgsl = slice(tg * 512, (tg + 1) * 512)
                    rs, RS = rsb[tg]
                    for k in range(8):
                        ta, TA = tf[k % 2]
                        stt(ta, xT[:, k, tgsl], g1[:, l, j, k:k + 1], rs, ALU.mult, ALU.mult, [XT, G1, RS], [TA])
                        act(hT[:, k, tgsl], ta, AF.Identity, [TA, MODT], [HT], bias=modT[:, l, k, j:j + 1], scale=1.0)
                if j == 0 and l == 0:
                    emit_mod(0, 8, 12, with_g1=False)

                ck(4)
                areset()
                Kz, KZ = aget([128, NC, 4, 128], BF16, "Kz")
                Vr, VR = aget([128, NC, 512], BF16, "Vr")
                Vc, VC = aget([128, NC, 512], BF16, "Vc")
                vs4, VS4 = aget([128, 4], F32, "vs4")
                QT = [aget([128, TT], BF16, f"QT{i}") for i in range(2)]
                QTs = [aget([128, TT], BF16, f"QTs{i}") for i in range(2)]
                KT = [aget([128, TT], BF16, f"KT{i}") for i in range(2)]
                gR = [aget([128, TT], BF16, f"gR{i}") for i in range(2)]
                R, RB = aget([128, NC + 2, 128], F32, "R")
                RBB = Buf("Rb")
                RBB.r = dict(RB.r)
                st["pbufs"].append(RBB)
                Sin, SIN = aget([128, NC, 128], F32, "Sin")
                SINB = Buf("Sinb")
                SINB.r = dict(SIN.r)
                st["pbufs"].append(SINB)
                Scs = [aget([128, NC, 128], BF16, f"Sc{i}") for i in range(2)]
                srs, SRS = aget([128, NC], F32, "srs")
                dk, DKB = aget([128, NC], F32, "dk")
                AT = [aget([128, 512], BF16, f"AT{i}") for i in range(2)]
                sqo2 = [aget([128, 512], BF16, f"sqo{i}") for i in range(2)]
                rso2 = [aget([128, 512], F32, f"rso{i}") for i in range(2)]
                to2 = [aget([128, 512], F32, f"to{i}") for i in range(2)]
                wq = [ws.get(wsrc(D["w_rqd"][l], i * 256, 256), 8, 256) for i in range(2)]
                wk, WK = ws.get(wsrc(W_in, 256, 256), 8, 256)
                wv = [ws.get(wsrc(W_in, 512 + i * 256, 256), 8, 256) for i in range(2)]
                wz = [ws.get(wsrc(W_in, 1024 + i * 256, 256), 8, 256) for i in range(2)]
                for n in range(NC):
                    nsl = slice(n * 128, (n + 1) * 128)
                    bK = nb()
                    for k in range(8):
                        mm(bank(bK)[:, 0:256], hT[:, k, nsl], wk[:, k, :], k == 0, k == 7, [HT, WK], [PSB[bK]])
                    bV = nb()
                    for i in range(2):
                        for k in range(8):
                            mm(bank(bV)[:, i * 256:(i + 1) * 256], hT[:, k, nsl], wv[i][0][:, k, :], k == 0, k == 7,
                               [HT, wv[i][1]], [PSB[bV]])
                    tt(Kz[:, n].rearrange("p h (d e) -> p h d e", d=2),
                       bank(bK)[:, 0:256].rearrange("p (h e) -> p h e", h=4).unsqueeze(2).broadcast_to([128, 4, 2, 64]),
                       Zt[:, l].unsqueeze(3).broadcast_to([128, 4, 2, 64]), ALU.mult, [PSB[bK], ZT], [KZ])
                    act(Vr[:, n, :], bank(bV), AF.Copy, [PSB[bV]], [VR])
                    rsum(vs4, bank(bV).rearrange("p (h e) -> p h e", h=4), [PSB[bV]], [VS4])
                    ts(vs4, vs4, -1.0 / 128, ALU.mult, [VS4], [VS4])
                    tt(Vc[:, n, :].rearrange("p (h e) -> p h e", h=4), bank(bV).rearrange("p (h e) -> p h e", h=4),
                       vs4.unsqueeze(2).broadcast_to([128, 4, 128]), ALU.add, [PSB[bV], VS4], [VC])
                def head_vars(h):
                    bi = h % 2
                    return (bi,) + QT[bi] + QTs[bi] + KT[bi] + gR[bi] + wq[h // 2] + wz[h // 2] + ((h % 2) * 128,) + Scs[bi]

                def stage_a1(h):
                    bi, qt, QTB, qs, QSB, kt_, KTB, gr, GRB, wqh, WQH, wzh, WZH, co, Sc, SCB = head_vars(h)
                    for tg in range(NTG):
                        tgsl = slice(tg * 512, (tg + 1) * 512)
                        b = nb()
                        for k in range(8):
                            mm(bank(b), wqh[:, k, co:co + 128], hT[:, k, tgsl], k == 0, k == 7, [WQH, HT], [PSB[b]])
                        act(qt[0:64, tgsl], bank(b)[0:64, :], AF.Copy, [PSB[b]], [QTB])
                        tt(qs[:, tgsl].rearrange("p (c t) -> p c t", c=4), bank(b).rearrange("p (c t) -> p c t", c=4),
                           Xi[:, l * 4 + h, :].unsqueeze(1).broadcast_to([128, 4, 128]), ALU.mult, [PSB[b], XI], [QSB])
                        b = nb()
                        for k in range(8):
                            mm(bank(b)[0:64, :], wk[:, k, h * 64:(h + 1) * 64], hT[:, k, tgsl], k == 0, k == 7, [WK, HT], [PSB[b]])
                        act(kt_[0:64, tgsl], bank(b)[0:64, :], AF.Copy, [PSB[b]], [KTB])

                def stage_a2(h):
                    bi, qt, QTB, qs, QSB, kt_, KTB, gr, GRB, wqh, WQH, wzh, WZH, co, Sc, SCB = head_vars(h)
                    dma(R[0:64, 0, :], D["s0"][l, 0, h], (), [RB])
                    dma(R[64:128, NC, :], D["s0"][l, 1, h], (), [RBB])
                    kvb = []
                    for n in range(NC):
                        if n % 4 == 0:
                            kvb.append(6 + (n // 4))
                        b = kvb[-1]
                        mm(bank(b)[:, (n % 4) * 128:(n % 4 + 1) * 128], Kz[:, n, h, :], Vr[:, n, h * 128:(h + 1) * 128],
                           True, True, [KZ, VR], [PSB[b]])
                    jf, jb = l * 8 + h, l * 8 + 4 + h
                    ts(dk[0:64, :], keep[0:64, j, 0, 0:NC], gC[0:64, jf:jf + 1], ALU.mult, [KEEP, GC], [DKB])
                    ts(dk[64:128, :], keep[64:128, j, 1, 0:NC], gC[64:128, jb:jb + 1], ALU.mult, [KEEP, GC], [DKB])
                    for n in range(NC):
                        b = kvb[n // 4]
                        stt(R[0:64, n + 1, :], R[0:64, n, :], dk[0:64, n:n + 1], bank(b)[0:64, (n % 4) * 128:(n % 4 + 1) * 128],
                            ALU.mult, ALU.add, [RB, DKB, PSB[b]], [RB])
                    for n in range(NC - 1, -1, -1):
                        b = kvb[n // 4]
                        stt(R[64:128, n, :], R[64:128, n + 1, :], dk[64:128, n:n + 1], bank(b)[64:128, (n % 4) * 128:(n % 4 + 1) * 128],
                            ALU.mult, ALU.add, [RBB, DKB, PSB[b]], [RBB])
                    tt(Sin[0:64], R[0:64, 0:NC, :], keep[0:64, j, 0, 0:NC].unsqueeze(2).broadcast_to([64, NC, 128]), ALU.mult,
                       [RB, KEEP], [SIN])
                    tt(Sin[64:128], R[64:128, 1:NC + 1, :], keep[64:128, j, 1, 0:NC].unsqueeze(2).broadcast_to([64, NC, 128]), ALU.mult,
                       [RBB, KEEP], [SINB])
                    rsum(srs, Sin, [SIN, SINB], [SRS])
                    ts(srs, srs, -1.0 / 128, ALU.mult, [SRS], [SRS])
                    tt(Sc, Sin, srs.unsqueeze(2).broadcast_to([128, NC, 128]), ALU.add, [SIN, SINB, SRS], [SCB])
                    ns = NC // 2
                    dma(D["o_ret"][l, SLOT0[j]:SLOT0[j] + ns, h, 0:64, :].rearrange("s p e -> p s e"),
                        R[0:64, 2:NC + 2, :].rearrange("p (s two) e -> p s two e", two=2)[:, :, 0, :], [RB], ())
                    dma(D["o_ret"][l, SLOT0[j]:SLOT0[j] + ns, h, 64:128, :].rearrange("s p e -> p s e"),
                        R[64:128, 0:NC, :].rearrange("p (s two) e -> p s two e", two=2)[:, :, 0, :], [RBB], ())

                def stage_b(h):
                    bi, qt, QTB, qs, QSB, kt_, KTB, gr, GRB, wqh, WQH, wzh, WZH, co, Sc, SCB = head_vars(h)
                    tgs = list(range(NTG))
                    bS, bO, bv = {}, {}, {}
                    for tg in tgs:
                        bS[tg] = nb()
                        for c4 in range(4):
                            n = tg * 4 + c4
                            nsl = slice(n * 128, (n + 1) * 128)
                            mm(bank(bS[tg])[:, c4 * 128:(c4 + 1) * 128], kt_[0:64, nsl], qt[0:64, nsl], True, True, [KTB, QTB], [PSB[bS[tg]]])
                    for tg in tgs:
                        at, ATB = AT[tg % 2]
                        tt(at.rearrange("p (c t) -> p c t", c=4), bank(bS[tg]).rearrange("p (c t) -> p c t", c=4),
                           Dm[:, l * 4 + h, :].unsqueeze(1).broadcast_to([128, 4, 128]), ALU.mult, [PSB[bS[tg]], DM], [ATB])
                    for tg in tgs:
                        at, ATB = AT[tg % 2]
                        bO[tg] = nb()
                        for c4 in range(4):
                            n = tg * 4 + c4
                            nsl = slice(n * 128, (n + 1) * 128)
                            osl = bank(bO[tg])[:, c4 * 128:(c4 + 1) * 128]
                            mm(osl, Vc[:, n, h * 128:(h + 1) * 128], at[:, c4 * 128:(c4 + 1) * 128], True, False, [VC, ATB], [PSB[bO[tg]]])
                            mm(osl, Sc[:, n, :], qs[:, nsl], False, True, [SCB, QSB], [PSB[bO[tg]]])
                    for tg in tgs:
                        act(sqo2[tg % 2][0], bank(bO[tg]), AF.Square, [PSB[bO[tg]]], [sqo2[tg % 2][1]])
                    for tg in tgs:
                        bv[tg] = nb()
                        mm(bank(bv[tg]), onesb[:], sqo2[tg % 2][0], True, True, [ONESB, sqo2[tg % 2][1]], [PSB[bv[tg]]])
                    for tg in tgs:
                        r_, RSB_ = rso2[tg % 2]
                        act(r_, bank(bv[tg]), AF.Ln, [PSB[bv[tg]], EPSC], [RSB_], scale=1.0 / 128, bias=epsc[:, 0:1])
                    for tg in tgs:
                        r_, RSB_ = rso2[tg % 2]
                        act(r_, r_, AF.Exp, [RSB_], [RSB_], scale=-0.5)
                    for tg in tgs:
                        r_, RSB_ = rso2[tg % 2]
                        t_, TOB_ = to2[tg % 2]
                        tt(t_, bank(bO[tg]), r_, ALU.mult, [PSB[bO[tg]], RSB_], [TOB_])
                    for tg in tgs:
                        tgsl = slice(tg * 512, (tg + 1) * 512)
                        t_, TOB_ = to2[tg % 2]
                        tt(brT[:, h, tgsl], t_, brT[:, 4 + h, tgsl], ALU.mult, [TOB_, BRT], [BRT])

                for h in range(4):
                    wzh, WZH = wz[h // 2]
                    co = (h % 2) * 128
                    for tg in range(NTG):
                        tgsl = slice(tg * 512, (tg + 1) * 512)
                        b = nb()
                        for k in range(8):
                            mm(bank(b), wzh[:, k, co:co + 128], hT[:, k, tgsl], k == 0, k == 7, [WZH, HT], [PSB[b]])
                        act(brT[:, 4 + h, tgsl], bank(b), AF.Silu, [PSB[b]], [BRT])
                stage_a1(0)
                stage_a2(0)
                stage_a1(1)
                for h in range(4):
                    stage_b(h)
                    if h + 1 < 4:
                        stage_a2(h + 1)
                    if h + 2 < 4:
                        stage_a1(h + 2)

                ck(5)
                if j == 0 and l == 0:
                    emit_mod(1)
                areset()
                kvu, KVU = aget([128, 2, 1024], BF16, "kvu")
                ckvT, CKVT = aget([128, 2, NK], BF16, "ckvT")
                KRM, KRMB = aget([128, NK], BF16, "KRM")
                Kaug = [aget([128, NK], BF16, f"Kaug{i}") for i in range(2)]
                Qaug = [aget([128, 512], BF16, f"Qaug{i}") for i in range(2)]
                qlT, QLT = aget([128, 3, TT], BF16, "qlT")
                gM = [aget([128, TT], BF16, f"gM{i}") for i in range(2)]
                Vx, VX = aget([128, NKT, 4, 128], BF16, "Vx")
                PT = [aget([128, 512], BF16, f"PT{i}") for i in range(3)]
                stg = [aget([128, 288], F32, f"stg{i}") for i in range(4)]
                kst, KST = aget([128, 96], F32, "kst")
                t1, T1 = aget([128, 512], F32, "t1")
                t2, T2 = aget([128, 512], F32, "t2")
                t3, T3 = aget([128, 512], F32, "t3")
                rq, RQ = aget([128, 512], F32, "rq")
                sqm = [aget([128, 512], BF16, f"sqm{i}") for i in range(4)]
                rden, RDEN = rq, RQ
                rope, ROPE = aget([128, 2, TT], F32, "rope")
                ssq, SSQ = aget([128, NC], F32, "ssq")
                rsd, RSD = aget([128, NC], F32, "rsd")
                dma(kvu, D["w_kvu"][l].rearrange("(k p) c -> p k c", p=128), (), [KVU], q="pool")
                dma(rope[64:96], D[f"rope{j}"], (), [ROPE])
                dma(KRM[96:101, :], D[f"mku{j}"], (), [KRMB])
                for qi in range(2):
                    tg = qi % NTG
                    dma(Qaug[qi][0][96:101, :], D[f"mkw{j}"][:, tg * 512:(tg + 1) * 512], (), [Qaug[qi][1]])
                mset(kst, 0.0, [KST])
                mset(ssq, 0.0, [SSQ])
                vx5 = Vx.rearrange("p k (a two) e -> p k a two e", two=2)
                mset(vx5[:, :, :, 0, 64:128], 1.0, [VX])
                mset(vx5[:, :, :, 1, 0:64], 1.0, [VX])
                wkl, WKL = ws.get(wsrc(W_in, 1920, 256), 8, 256)
                wkr, WKR = ws.get(wsrc(D["w_krp"][l], 0, 192), 8, 192)
                pairs = [[n0, n0 + 1] for n0 in range(0, NC, 2)]
                bk, btk = {}, {}

                def m1_proj(pair):
                    for n in pair:
                        nsl = slice(n * 128, (n + 1) * 128)
                        bk[n] = nb()
                        for k in range(8):
                            mm(bank(bk[n])[:, 0:256], hT[:, k, nsl], wkl[:, k, :], k == 0, k == 7, [HT, WKL], [PSB[bk[n]]])
                        for k in range(8):
                            mm(bank(bk[n])[:, 256:288], hT[:, k, nsl], wkr[:, k, 64:96], k == 0, k == 7, [HT, WKR], [PSB[bk[n]]])

                def m1_norm(pair):
                    for n in pair:
                        sa, SA = sqm[n % 4]
                        act(sa[:, 0:256], bank(bk[n])[:, 0:256], AF.Square, [PSB[bk[n]]], [SA, SSQ], accum_out=ssq[:, n:n + 1])
                    for n in pair:
                        act(rsd[:, n:n + 1], ssq[:, n:n + 1], AF.Ln, [SSQ, EPSC], [RSD], scale=1.0 / 256, bias=epsc[:, 0:1])
                    for n in pair:
                        act(rsd[:, n:n + 1], rsd[:, n:n + 1], AF.Exp, [RSD], [RSD], scale=-0.5)
                    for n in pair:
                        sg, SG = stg[n % 4]
                        stt(sg[:, 0:256], bank(bk[n])[:, 0:256], rsd[:, n:n + 1], gkv[:, l, :], ALU.mult, ALU.mult, [PSB[bk[n]], RSD, GKV], [SG])
                        cp(sg[:, 256:288], bank(bk[n])[:, 256:288], [PSB[bk[n]]], [SG])
                    for n in pair:
                        sg, SG = stg[n % 4]
                        dma(D["o_ckv"][l, tok0 + n * 128:tok0 + (n + 1) * 128, :], sg[:, 0:256], [SG], ())
                        dma(D["o_kr"][l, tok0 + n * 128:tok0 + (n + 1) * 128, :], sg[:, 256:288], [SG], ())

                def m1_tr(pair):
                    for n in pair:
                        sg, SG = stg[n % 4]
                        btk[n] = nb()
                        for c2 in range(2):
                            tr(bank(btk[n])[:, c2 * 128:(c2 + 1) * 128], sg[:, c2 * 128:(c2 + 1) * 128], [SG], [PSB[btk[n]]])
                    for n in pair:
                        act(ckvT[:, :, NP + n * 128:NP + (n + 1) * 128], bank(btk[n])[:, 0:256].rearrange("p (c t) -> p c t", c=2),
                            AF.Copy, [PSB[btk[n]]], [CKVT])

                m1_proj(pairs[0])
                for pi, pair in enumerate(pairs):
                    m1_norm(pair)
                    if pi + 1 < len(pairs):
                        m1_proj(pairs[pi + 1])
                    m1_tr(pair)
                for pt in range(NP // 128):
                    sg, SG = stg[pt % 4]
                    dma(sg[:, 0:256], D["cckv"][l, pt * 128:(pt + 1) * 128, :], (), [SG])
                    dma(kst[:, 64:96], D["ckr"][l, pt * 128:(pt + 1) * 128, :], (), [KST])
                    bt = nb()
                    for c2 in range(2):
                        tr(bank(bt)[:, c2 * 128:(c2 + 1) * 128], sg[:, c2 * 128:(c2 + 1) * 128], [SG], [PSB[bt]])
                    tr(bank(bt)[0:96, 256:384], kst[:, 0:96], [KST], [PSB[bt]])
                    act(ckvT[:, :, pt * 128:(pt + 1) * 128], bank(bt)[:, 0:256].rearrange("p (c t) -> p c t", c=2),
                        AF.Copy, [PSB[bt]], [CKVT])
                    cp(KRM[64:96, pt * 128:(pt + 1) * 128], bank(bt)[64:96, 256:384], [PSB[bt]], [KRMB])
                for tg in range(NTG):
                    tgsl = slice(tg * 512, (tg + 1) * 512)
                    bA = nb()
                    for k in range(8):
                        mm(bank(bA)[0:96, :], wkr[:, k, 0:96], hT[:, k, tgsl], k == 0, k == 7, [WKR, HT], [PSB[bA]])
                    bB = nb()
                    for k in range(8):
                        mm(bank(bB)[0:96, :], wkr[:, k, 96:192], hT[:, k, tgsl], k == 0, k == 7, [WKR, HT], [PSB[bB]])
                    tt(t1[64:96], bank(bA)[64:96, :], rope[64:96, 0, tgsl], ALU.mult, [PSB[bA], ROPE], [T1])
                    tt(t2[64:96], bank(bB)[64:96, :], rope[64:96, 1, tgsl], ALU.mult, [PSB[bB], ROPE], [T2])
                    tt(KRM[64:96, NP + tg * 512:NP + (tg + 1) * 512], t1[64:96], t2[64:96], ALU.add, [T1, T2], [KRMB])
                act(Kaug[0][0][64:101, :], KRM[64:101, :], AF.Copy, [KRMB], [Kaug[0][1]])
                cp(Kaug[1][0][64:101, :], KRM[64:101, :], [KRMB], [Kaug[1][1]])
                wq0, WQ0 = ws.get(wsrc(W_in, 1536, 256), 8, 256)
                wq1, WQ1 = ws.get(wsrc(W_in, 1792, 128), 8, 128)
                for tg in range(NTG):
                    tgsl = slice(tg * 512, (tg + 1) * 512)
                    bc = [nb() for _ in range(3)]
                    for c in range(3):
                        for k in range(8):
                            if c < 2:
                                mm(bank(bc[c]), wq0[:, k, c * 128:(c + 1) * 128], hT[:, k, tgsl], k == 0, k == 7, [WQ0, HT], [PSB[bc[c]]])
                            else:
                                mm(bank(bc[c]), wq1[:, k, 0:128], hT[:, k, tgsl], k == 0, k == 7, [WQ1, HT], [PSB[bc[c]]])
                    bs = nb()
                    for c in range(3):
                        sa, SA = sqm[c % 2]
                        act(sa, bank(bc[c]), AF.Square, [PSB[bc[c]]], [SA])
                        mm(bank(bs), onesb[:], sa, c == 0, c == 2, [ONESB, SA], [PSB[bs]])
                    act(rq, bank(bs), AF.Ln, [PSB[bs], EPSC], [RQ], scale=1.0 / 384, bias=epsc[:, 0:1])
                    act(rq, rq, AF.Exp, [RQ], [RQ], scale=-0.5)
                    for c in range(3):
                        stt(qlT[:, c, tgsl], bank(bc[c]), gqT[:, l, c:c + 1], rq, ALU.mult, ALU.mult, [PSB[bc[c]], GQT, RQ], [QLT])
                kvu3 = kvu.rearrange("p k (h x) -> p k h x", h=8)
                wst = {}

                def prep_head(h):
                    c = h // 2
                    ka, KA = Kaug[h % 2]
                    gm, GMB = gM[c % 2]
                    if h % 2 == 0:
                        wst["wqu"] = ws.get(D["w_qu2"][l].rearrange("(c p) x -> p c x", p=128)[:, :, c * 384:(c + 1) * 384], 3, 384)
                    wst[("wqu", h)] = wst["wqu"]
                    for kg in range(NK // 512):
                        b = nb()
                        for c2 in range(2):
                            mm(bank(b)[0:64, :], kvu[:, c2, h * 128:h * 128 + 64], ckvT[:, c2, kg * 512:(kg + 1) * 512],
                               c2 == 0, c2 == 1, [KVU, CKVT], [PSB[b]])
                        act(ka[0:64, kg * 512:(kg + 1) * 512], bank(b)[0:64, :], AF.Copy, [PSB[b]], [KA])

                def prep_q(h, tg, bc):
                    tgsl = slice(tg * 512, (tg + 1) * 512)
                    qa, QA = Qaug[bc % 2]
                    wqu, WQU = wst[("wqu", h)]
                    qo = (h % 2) * 192
                    bA = 4
                    for c3 in range(3):
                        mm(bank(bA)[0:96, :], wqu[:, c3, qo:qo + 96], qlT[:, c3, tgsl], c3 == 0, c3 == 2, [WQU, QLT], [PSB[bA]])
                    bB = 5
                    for c3 in range(3):
                        mm(bank(bB)[0:96, :], wqu[:, c3, qo + 96:qo + 192], qlT[:, c3, tgsl], c3 == 0, c3 == 2, [WQU, QLT], [PSB[bB]])
                    act(qa[0:64, :], bank(bA)[0:64, :], AF.Copy, [PSB[bA]], [QA])
                    tt(t1[64:96], bank(bA)[64:96, :], rope[64:96, 0, tgsl], ALU.mult, [PSB[bA], ROPE], [T1])
                    tt(t2[64:96], bank(bB)[64:96, :], rope[64:96, 1, tgsl], ALU.mult, [PSB[bB], ROPE], [T2])
                    tt(qa[64:96, :], t1[64:96], t2[64:96], ALU.add, [T1, T2], [QA])

                def attn(h, tg, bc):
                    tgsl = slice(tg * 512, (tg + 1) * 512)
                    hh = h % 4
                    par = h % 2
                    po, pd = par * 64, (1 - par) * 64
                    c = h // 2
                    ka, KA = Kaug[h % 2]
                    gm, GMB = gM[c % 2]
                    qa, QA = Qaug[bc % 2]
                    bo = 6 + (bc % 2)
                    LA = 2
                    sb_ = {}

                    def score(kt):
                        sb_[kt] = nb()
                        mm(bank(sb_[kt]), ka[0:101, kt * 128:(kt + 1) * 128], qa[0:101, :], True, True, [KA, QA], [PSB[sb_[kt]]])
                    for kt in range(min(LA, NKT)):
                        score(kt)
                    for kt in range(NKT):
                        pt_, PTB = PT[kt % 3]
                        act(pt_, bank(sb_[kt]), AF.Exp, [PSB[sb_[kt]]], [PTB], scale=SC)
                        if kt + LA < NKT:
                            score(kt + LA)
                        mm(bank(bo), Vx[:, kt, hh, :], pt_, kt == 0, kt == NKT - 1, [VX, PTB], [PSB[bo]])
                    if j == 1:
                        act(rden[po:po + 64], bank(bo)[pd:pd + 64, :], AF.Ln, [PSB[bo]], [RDEN])
                        act(rden[po:po + 64], rden[po:po + 64], AF.Exp, [RDEN], [RDEN], scale=-1.0)
                    else:
                        recip(rden[po:po + 64], bank(bo)[pd:pd + 64, :], [PSB[bo]], [RDEN])
                    tt(t3[po:po + 64], bank(bo)[po:po + 64, :], rden[po:po + 64], ALU.mult, [PSB[bo], RDEN], [T3])
                    tt(brT[po:po + 64, 4 + c, tgsl], t3[po:po + 64], brT[po:po + 64, 8 + c, tgsl], ALU.mult, [T3, BRT], [BRT])

                for c in range(4):
                    if c % 2 == 0:
                        wmz, WMZ = ws.get(wsrc(W_in, 2208 + (c // 2) * 256, 256), 8, 256)
                    for tg in range(NTG):
                        tgsl = slice(tg * 512, (tg + 1) * 512)
                        b = nb()
                        for k in range(8):
                            mm(bank(b), wmz[:, k, (c % 2) * 128:(c % 2 + 1) * 128], hT[:, k, tgsl], k == 0, k == 7, [WMZ, HT], [PSB[b]])
                        act(brT[:, 8 + c, tgsl], bank(b), AF.Silu, [PSB[b]], [BRT])
                bc = 0
                st["nrot"] = 4
                for hg in range(2):
                    for kt in range(NKT):
                        b = nb()
                        for c2 in range(2):
                            mm(bank(b)[:, 0:256], ckvT[:, c2, kt * 128:(kt + 1) * 128], kvu3[:, c2, hg * 4:hg * 4 + 4, 64:128],
                               c2 == 0, c2 == 1, [CKVT, KVU], [PSB[b]])
                        bv = bank(b)[:, 0:256].rearrange("p (a two e) -> p a two e", two=2, e=64)
                        act(vx5[:, kt, :, 0, 0:64], bv[:, :, 0, :], AF.Copy, [PSB[b]], [VX])
                        cp(vx5[:, kt, :, 1, 64:128], bv[:, :, 1, :], [PSB[b]], [VX])
                    hblocks = [(h, tg) for h in range(hg * 4, hg * 4 + 4) for tg in range(NTG)]
                    prep_head(hblocks[0][0])
                    prep_q(hblocks[0][0], hblocks[0][1], bc)
                    for idx, (h, tg) in enumerate(hblocks):
                        if idx + 1 < len(hblocks):
                            h2, tg2 = hblocks[idx + 1]
                            if h2 != h:
                                prep_head(h2)
                            prep_q(h2, tg2, bc + 1)
                        attn(h, tg, bc)
                        bc += 1
                st["nrot"] = 6

                areset()
                fuT, FUT = aget([128, 4, TT], BF16, "fuT")
                gF, GFB = aget([128, 4, TT], BF16, "gF")
                AB, ABB = aget([128, NC, 4, 256], BF16, "AB")
                wf = [ws.get(wsrc(W_in, 2720 + i * 256, 256), 8, 256) for i in range(2)]
                for tg in range(NTG):
                    tgsl = slice(tg * 512, (tg + 1) * 512)
                    for g in range(4):
                        b = nb()
                        for k in range(8):
                            mm(bank(b), wf[g // 2][0][:, k, (g % 2) * 128:(g % 2 + 1) * 128], hT[:, k, tgsl], k == 0, k == 7,
                               [wf[g // 2][1], HT], [PSB[b]])
                        act(fuT[:, g, tgsl], bank(b), AF.Copy, [PSB[b]], [FUT])
                wfz = [ws.get(wsrc(W_in, 3232 + i * 256, 256), 8, 256) for i in range(2)]
                for tg in range(NTG):
                    tgsl = slice(tg * 512, (tg + 1) * 512)
                    for g in range(4):
                        b = nb()
                        for k in range(8):
                            mm(bank(b), wfz[g // 2][0][:, k, (g % 2) * 128:(g % 2 + 1) * 128], hT[:, k, tgsl], k == 0, k == 7,
                               [wfz[g // 2][1], HT], [PSB[b]])
                        act(gF[:, g, tgsl], bank(b), AF.Silu, [PSB[b]], [GFB])
                for n in range(NC):
                    nsl = slice(n * 128, (n + 1) * 128)
                    bb = [nb(), nb()]
                    for g in range(4):
                        mm(bank(bb[g // 2])[:, (g % 2) * 256:(g % 2 + 1) * 256], fuT[:, g, nsl], cwsw[:], True, True, [FUT, CWSW], [PSB[bb[g // 2]]])
                    act(AB[:, n, 0:2, :], bank(bb[0]).rearrange("p (g x) -> p g x", g=2), AF.Copy, [PSB[bb[0]]], [ABB])
                    cp(AB[:, n, 2:4, :], bank(bb[1]).rearrange("p (g x) -> p g x", g=2), [PSB[bb[1]]], [ABB])
                for kb in range(TT // 256):
                    wc, WC = ws.get(wsrc(D[f"dftc{j}"], kb * 256, 256), NC, 256)
                    wsn, WSN = ws.get(wsrc(D[f"dfts{j}"], kb * 256, 256), NC, 256)
                    ksl = slice(kb * 256, (kb + 1) * 256)
                    for g in range(4):
                        b = nb()
                        for n in range(NC):
                            mm(bank(b)[:, 0:256], AB[:, n, g, 0:128], wc[:, n, :], n == 0, False, [ABB, WC], [PSB[b]])
                            mm(bank(b)[:, 0:256], AB[:, n, g, 128:256], wsn[:, n, :], False, n == NC - 1, [ABB, WSN], [PSB[b]])
                        tt(brT[:, 8 + g, ksl], bank(b)[:, 0:256], gF[:, g, ksl], ALU.mult, [PSB[b], GFB], [BRT])

                ck(7)
                areset()
                mT, MT = aget([128, 8, TT], BF16, "mT")
                sig = [aget([128, 512], F32, f"sig{i}") for i in range(3)]
                tm = [aget([128, 512], F32, f"tm{i}") for i in range(3)]
                for cpair in range(4):
                    gl = [ws.get(wsrc(W_in, 3744 + n * 1024 + cpair * 256, 256), 8, 256) for n in range(3)]
                    wb = [ws.get(wsrc(D["w_br"][l, n], cpair * 256, 256), 4, 256) for n in range(3)]
                    for cc in range(2):
                        c = cpair * 2 + cc
                        csl = slice(cc * 128, (cc + 1) * 128)
                        for tg in range(NTG):
                            tgsl = slice(tg * 512, (tg + 1) * 512)
                            for n in range(3):
                                bg = nb()
                                for k in range(8):
                                    mm(bank(bg), gl[n][0][:, k, csl], hT[:, k, tgsl], k == 0, k == 7, [gl[n][1], HT], [PSB[bg]])
                                act(sig[n][0], bank(bg), AF.Sigmoid, [PSB[bg]], [sig[n][1]])
                                bp = nb()
                                for k4 in range(4):
                                    mm(bank(bp), wb[n][0][:, k4, csl], brT[:, n * 4 + k4, tgsl], k4 == 0, k4 == 3, [wb[n][1], BRT], [PSB[bp]])
                                tt(tm[n][0], bank(bp), sig[n][0], ALU.mult, [PSB[bp], sig[n][1]], [tm[n][1]])
                            tt(tm[0][0], tm[0][0], tm[1][0], ALU.add, [tm[0][1], tm[1][1]], [tm[0][1]])
                            tt(mT[:, c, tgsl], tm[0][0], tm[2][0], ALU.add, [tm[0][1], tm[2][1]], [MT])
                for cpair in range(4):
                    wo, WO = ws.get(wsrc(D["w_out"][l], cpair * 256, 256), 8, 256)
                    for cc in range(2):
                        c = cpair * 2 + cc
                        csl = slice(cc * 128, (cc + 1) * 128)
                        for tg in range(NTG):
                            tgsl = slice(tg * 512, (tg + 1) * 512)
                            b = nb()
                            for k in range(8):
                                mm(bank(b), wo[:, k, csl], mT[:, k, tgsl], k == 0, k == 7, [WO, MT], [PSB[b]])
                            stt(xT[:, c, tgsl], bank(b), modT[:, l, 16 + c, j:j + 1], xT[:, c, tgsl], ALU.mult, ALU.add,
                                [PSB[b], MODT, XT], [XT])

                ck(8 + l)
            areset()
            sq = [aget([128, 512], BF16, f"fsq{i}") for i in range(2)]
            rsb = [aget([128, 512], F32, f"frs{i}") for i in range(NTG)]
            yT, YT = aget([128, 8, 512], F32, "yT")
            yst = [aget([128, 1024], F32, f"yst{i}") for i in range(2)]
            for tg in range(NTG):
                norm_stats(rsb[tg][0], rsb[tg][1], sq, slice(tg * 512, (tg + 1) * 512))
            for tg in range(NTG):
                tgsl = slice(tg * 512, (tg + 1) * 512)
                rs, RS = rsb[tg]
                for k in range(8):
                    stt(yT[:, k, :], xT[:, k, tgsl], fgT[:, k:k + 1], rs, ALU.mult, ALU.mult, [XT, FGT, RS], [YT])
                for c4 in range(4):
                    n = tg * 4 + c4
                    ya, YA = yst[n % 2]
                    bA, bB = nb(), nb()
                    for k in range(8):
                        bk = bA if k < 4 else bB
                        tr(bank(bk)[:, (k % 4) * 128:(k % 4 + 1) * 128], yT[:, k, c4 * 128:(c4 + 1) * 128], [YT], [PSB[bk]])
                    cp(ya[:, 0:512], bank(bA), [PSB[bA]], [YA])
                    act(ya[:, 512:1024], bank(bB), AF.Copy, [PSB[bB]], [YA])
                    dma(D["y"][tok0 + n * 128:tok0 + (n + 1) * 128, :], ya, [YA], ())

    S.plan = True
    try:
        body()
    except _Stop:
        pass
    S.plan = False
    try:
        body()
    except _Stop:
        pass
    print('ops', S.cnt, {q: sum(v) // 16 for q, v in S.dcnt.items()}, flush=True)
    S.finish()
    S.emit()
    return nc


def _consts(is_sample):
    bf = ml_dtypes.bfloat16
    c = {}
    c["ident"] = np.eye(128, dtype=np.float32)
    ci = np.arange(128)
    ang = 2 * np.pi * np.outer(ci, ci) / 128
    c["cwsw"] = np.concatenate([np.cos(ang), np.sin(ang)], 1).astype(bf)

    def dft(T, seq):
        t = np.arange(T)
        same = (t[:, None] // seq) == (t[None, :] // seq)
        a = 2 * np.pi * np.outer(t % seq, t % seq) / seq
        nrm = (seq * 128) ** -0.5
        return (np.cos(a) * same * nrm).astype(bf), (-np.sin(a) * same * nrm).astype(bf)
    c["dftc0"], c["dfts0"] = dft(1024, 1024 if is_sample else 256)
    c["dftc1"], c["dfts1"] = dft(512, 256)

    def rope(T, real):
        out = np.zeros((32, 2, T), np.float32)
        out[:, 0, :] = 1.0
        if real:
            t = np.arange(T)
            pos = [t // 64, t % 64]
            inv = 10000.0 ** (-np.arange(8, dtype=np.float32) / 8)
            for d in range(32):
                hh, e = d // 16, d % 16
                f, second = e % 8, e // 8
                a = pos[hh].astype(np.float32) * inv[f]
                out[d, 0] = np.cos(a)
                out[d, 1] = np.sin(a) * (1.0 if second else -1.0)
        return out
    c["rope0"] = rope(1024, is_sample)
    c["rope1"] = rope(512, False)

    def masks(T, NP, seq, past_ok):
        NK = NP + T
        U = np.zeros((5, NK), np.float32)
        W = np.full((5, T), -BIG, np.float32)
        U[0, :NP] = 1
        if past_ok:
            W[0, :] = 0
        kj = np.arange(T)
        for s in range(T // seq):
            U[1 + s, NP + s * seq:NP + (s + 1) * seq] = 1
            W[1 + s, s * seq:(s + 1) * seq] = 0
        return U.astype(bf), W.astype(bf)
    c["mku0"], c["mkw0"] = masks(1024, 512, 1024 if is_sample else 256, is_sample)
    c["mku1"], c["mkw1"] = masks(512, 0, 256, False)
    keep = np.ones((2, 2, 8), np.float32)

    def packed(NC):
        kf = np.array([0.0 if n % 2 == 0 else 1.0 for n in range(8)], np.float32)
        kb = np.array([0.0 if n % 2 == 1 else 1.0 for n in range(8)], np.float32)
        return kf, kb
    if not is_sample:
        keep[0, 0], keep[0, 1] = packed(8)
    keep[1, 0], keep[1, 1] = packed(4)
    c["keep"] = np.broadcast_to(keep.reshape(1, 32), (128, 32)).copy()
    jj, ii = np.meshgrid(np.arange(128), np.arange(128), indexing="ij")
    rc = np.zeros((128, 4, 128), np.float32)
    rc[:, 0] = np.maximum(ii - jj, 0)
    rc[:, 1] = np.maximum(jj - ii, 0)
    rc[:, 2] = np.where(ii == jj, 2.0, 1.0) * 0.125
    rc[:64, 3] = np.arange(128)[None, :] + 1
    rc[64:, 3] = 128 - np.arange(128)[None, :]
    c["rc"] = rc
    c["ez"] = np.stack([127 - np.arange(128), np.arange(128)], 1).astype(np.float32)
    return c


_NC_CACHE = {}


def kernel(x_prompt, x_sample, cache_ckv, cache_krope, state_ret, c, c_ctx, norm_g, w_mod, b_mod, w_in,
           ret_decay_logit, q_norm_g, w_q_up, kv_norm_g, w_kv_up, w_branch, w_out, final_norm_g):
    f32 = np.float32
    A = lambda a: np.ascontiguousarray(np.asarray(a, dtype=f32))
    x_prompt, x_sample, cache_ckv, cache_krope, state_ret = map(A, (x_prompt, x_sample, cache_ckv, cache_krope, state_ret))
    c, c_ctx, norm_g, w_mod, b_mod, w_in = map(A, (c, c_ctx, norm_g, w_mod, b_mod, w_in))
    ret_decay_logit, q_norm_g, w_q_up, kv_norm_g, w_kv_up, w_branch, w_out, final_norm_g = map(
        A, (ret_decay_logit, q_norm_g, w_q_up, kv_norm_g, w_kv_up, w_branch, w_out, final_norm_g))
    if "nc" not in _NC_CACHE:
        _NC_CACHE["nc"] = build_program()
    nc = _NC_CACHE["nc"]

    rq = w_in[:, :, 0:256].reshape(2, 1024, 4, 1, 64)
    w_rqd = A(np.broadcast_to(rq, (2, 1024, 4, 2, 64)).reshape(2, 1024, 512))
    swap = np.array([(d // 16) * 16 + ((d % 16) + 8) % 16 for d in range(32)])
    kr = w_in[:, :, 2176:2208]
    z64 = np.zeros((2, 1024, 64), f32)
    w_krp = A(np.concatenate([z64, kr, z64, kr[:, :, swap]], 2))
    qu = w_q_up.reshape(2, 384, 8, 96)
    qsw = np.concatenate([np.zeros((2, 384, 8, 64), f32), qu[:, :, :, 64:][:, :, :, swap]], 3)
    w_qu2 = A(np.concatenate([qu, qsw], 3).reshape(2, 384, 1536))
    fm = lambda v, k: A(v.reshape(k, 128).T)
    shared = dict(
        ngT=A(np.stack([fm(norm_g[l], 8) for l in range(2)], 1)),
        bmT=A(np.stack([fm(b_mod[l], 24) for l in range(2)], 1)),
        fgT=fm(final_norm_g, 8),
        gqT=A(np.stack([fm(q_norm_g[l], 3) for l in range(2)], 1)),
        gkv=A(kv_norm_g.reshape(512)), rdl=A(ret_decay_logit.reshape(16)),
        w_mod=w_mod, w_in=w_in, w_rqd=w_rqd, w_krp=w_krp, w_qu2=w_qu2, w_kvu=w_kv_up, w_br=w_branch, w_out=w_out,
    )
    cs = {True: _consts(True), False: _consts(False)}
    in_maps = []
    for core in range(8):
        samp = core < 4
        sp = [2 * core, 2 * core + 1]
        if samp:
            xl = x_sample[core]
            cv_l = c[core]
            cck, ckr_, s0 = cache_ckv[core], cache_krope[core], state_ret[core]
        else:
            lp = [16 + 4 * (core - 4) + s for s in range(4)]
            xl = x_prompt[lp].reshape(1024, 1024)
            cv_l = c_ctx
            cck, ckr_, s0 = np.zeros((2, 512, 256), f32), np.zeros((2, 512, 32), f32), np.zeros((2, 2, 4, 64, 128), f32)
        xin = A(np.concatenate([xl, x_prompt[sp].reshape(512, 1024)], 0))
        cvT = A(np.stack([fm(cv_l, 8), fm(c_ctx, 8)], 2))
        m = dict(shared)
        m.update(cs[samp])
        m.update(xin=xin, cvT=cvT, cckv=A(cck), ckr=A(ckr_), s0=A(s0))
        in_maps.append(m)
    res = run_bass_kernel_spmd(nc, in_maps, core_ids=list(range(8)))
    R = res.results
    y_prompt = np.zeros((32, 256, 1024), f32)
    y_sample = np.zeros((4, 1024, 1024), f32)
    new_ckv = np.zeros((32, 2, 256, 256), f32)
    new_kr = np.zeros((32, 2, 256, 32), f32)
    new_ret = np.zeros((32, 2, 2, 4, 64, 128), f32)
    for b in range(32):
        if b < 16:
            core, off, slot = b // 2, 1024 + (b % 2) * 256, 4 + (b % 2)
        else:
            core, off, slot = 4 + (b - 16) // 4, ((b - 16) % 4) * 256, (b - 16) % 4
        r = R[core]
        y_prompt[b] = r["y"][off:off + 256]
        new_ckv[b] = r["o_ckv"][:, off:off + 256]
        new_kr[b] = r["o_kr"][:, off:off + 256]
        st_ = r["o_ret"][:, slot]
        new_ret[b] = st_.reshape(2, 4, 2, 64, 128).transpose(0, 2, 1, 3, 4)
    for b in range(4):
        y_sample[b] = R[b]["y"][0:1024]
    return (y_prompt, y_sample, new_ckv, new_kr, new_ret)
```

```python
import os
import numpy as np
import ml_dtypes
import concourse.bass as bass
import concourse.mybir as mybir
from concourse.bass_utils import run_bass_kernel_spmd

F32, BF16 = mybir.dt.float32, mybir.dt.bfloat16
AF = mybir.ActivationFunctionType
ALU = mybir.AluOpType
AX = mybir.AxisListType
EPS = 1e-6
NSLOT = 12
PF = 5
JOBS = [(0, 1024, 512), (1024, 512, 0)]
SLOT0 = [0, 4]
NTOK = 1536
SC = float(96 ** -0.5)
BIG = 30000.0
ARENA = 73728
STAGE = float(os.environ.get('KSTAGE', '999'))


class _Stop(Exception):
    pass


def ck(n):
    if n >= STAGE:
        raise _Stop()


class Buf:
    __slots__ = ("w", "r", "name", "excl")

    def __init__(self, name="", excl=False):
        self.w = None
        self.r = {}
        self.name = name
        self.excl = excl


class Sched:
    ENG = ("pe", "act", "dve", "pool", "sp")

    def __init__(self, nc, n_dma_sems=10):
        self.nc = nc
        self.plan = False
        self.prog = {e: [] for e in self.ENG}
        self.sem = {e: nc.alloc_semaphore(f"c_{e}") for e in self.ENG}
        self.cnt = {e: 0 for e in self.ENG}
        self.waited = {e: {} for e in self.ENG}
        self.nd = n_dma_sems
        self.dsem = {q: [nc.alloc_semaphore(f"d_{q}{i}") for i in range(n_dma_sems)] for q in ("sp", "pool")}
        self.dcnt = {q: [0] * n_dma_sems for q in self.dsem}
        self.didx = {q: 0 for q in self.dsem}

    def _collect(self, reads, writes):
        deps = {}
        for b in reads:
            if b.w is not None and deps.get(b.w[0], 0) < b.w[1]:
                deps[b.w[0]] = b.w[1]
            if b.excl:
                for s, v in b.r.items():
                    if deps.get(s, 0) < v:
                        deps[s] = v
        for b in writes:
            if b.w is not None and deps.get(b.w[0], 0) < b.w[1]:
                deps[b.w[0]] = b.w[1]
            for s, v in b.r.items():
                if deps.get(s, 0) < v:
                    deps[s] = v
        return deps

    def _waits(self, eng, deps):
        waits = []
        wd = self.waited[eng]
        for s, v in deps.items():
            if eng == "pe" and s is self.sem["pe"]:
                continue
            if wd.get(s, 0) >= v:
                continue
            wd[s] = v
            waits.append((s, v))
        return waits

    def _commit(self, ev, reads, writes):
        s, v = ev
        for b in reads:
            if b.r.get(s, 0) < v:
                b.r[s] = v
        for b in writes:
            b.w = ev
            b.r = {}

    def op(self, eng, fn, reads=(), writes=()):
        if self.plan:
            return
        waits = self._waits(eng, self._collect(reads, writes))
        self.cnt[eng] += 1
        ev = (self.sem[eng], self.cnt[eng])
        self.prog[eng].append((waits, fn, ev[0], 1))
        self._commit(ev, reads, writes)

    def dma(self, q, fn, reads=(), writes=()):
        if self.plan:
            return
        i = self.didx[q]
        self.didx[q] = (i + 1) % self.nd
        sem = self.dsem[q][i]
        prev = self.dcnt[q][i]
        deps = self._collect(reads, writes)
        if prev > 0 and deps.get(sem, 0) < prev:
            deps[sem] = prev
        waits = self._waits(q, deps)
        self.dcnt[q][i] = prev + 16
        ev = (sem, prev + 16)
        self.prog[q].append((waits, fn, sem, 16))
        self._commit(ev, reads, writes)

    def barrier(self):
        if self.plan:
            return
        for e in self.ENG:
            deps = {}
            for e2 in self.ENG:
                if e2 != e and self.cnt[e2] > 0:
                    deps[self.sem[e2]] = self.cnt[e2]
            for i, s in enumerate(self.dsem["sp"]):
                if self.dcnt["sp"][i] > 0:
                    deps[s] = self.dcnt["sp"][i]
            waits = self._waits(e, deps)
            if waits:
                self.prog[e].append((waits, None, None, 0))

    def finish(self):
        waits = []
        for q in self.dsem:
            for i, s in enumerate(self.dsem[q]):
                if self.dcnt[q][i] > 0:
                    waits.append((s, self.dcnt[q][i]))
        for e in self.ENG:
            if e != "sp" and self.cnt[e] > 0:
                waits.append((self.sem[e], self.cnt[e]))
        self.prog["sp"].append((waits, None, None, 0))

    def emit(self):
        prog = self.prog

        def replay(name, eng):
            for waits, fn, sem, inc in prog[name]:
                for s, v in waits:
                    eng.wait_ge(s, v)
                if fn is not None:
                    fn(eng).then_inc(sem, inc)

        with self.nc.Block() as block:
            @block.tensor
            def _(e):
                replay("pe", e)

            @block.scalar
            def _(e):
                replay("act", e)

            @block.vector
            def _(e):
                replay("dve", e)

            @block.gpsimd
            def _(e):
                replay("pool", e)

            @block.sync
            def _(e):
                replay("sp", e)


def build_program():
    nc = bass.Bass("TRN2", target_bir_lowering=False)
    S = Sched(nc)

    def din(name, shape, dt=F32):
        return nc.dram_tensor(name, list(shape), dt, kind="ExternalInput").ap()

    def dout(name, shape):
        return nc.dram_tensor(name, list(shape), F32, kind="ExternalOutput").ap()

    D = dict(
        xin=din("xin", [NTOK, 1024]), cvT=din("cvT", [128, 8, 2]),
        cckv=din("cckv", [2, 512, 256]), ckr=din("ckr", [2, 512, 32]), s0=din("s0", [2, 2, 4, 64, 128]),
        ngT=din("ngT", [128, 2, 8]), bmT=din("bmT", [128, 2, 24]), fgT=din("fgT", [128, 8]),
        gqT=din("gqT", [128, 2, 3]), gkv=din("gkv", [512]), rdl=din("rdl", [16]),
        w_mod=din("w_mod", [2, 1024, 3072]), w_in=din("w_in", [2, 1024, 6816]),
        w_rqd=din("w_rqd", [2, 1024, 512]), w_krp=din("w_krp", [2, 1024, 192]),
        w_qu2=din("w_qu2", [2, 384, 1536]), w_kvu=din("w_kvu", [2, 256, 1024]),
        w_br=din("w_br", [2, 3, 512, 1024]), w_out=din("w_out", [2, 1024, 1024]),
        ident=din("ident", [128, 128]), cwsw=din("cwsw", [128, 256], BF16),
        dftc0=din("dftc0", [1024, 1024], BF16), dfts0=din("dfts0", [1024, 1024], BF16),
        dftc1=din("dftc1", [512, 512], BF16), dfts1=din("dfts1", [512, 512], BF16),
        rope0=din("rope0", [32, 2, 1024]), rope1=din("rope1", [32, 2, 512]),
        mku0=din("mku0", [5, 1536], BF16), mkw0=din("mkw0", [5, 1024], BF16),
        mku1=din("mku1", [5, 512], BF16), mkw1=din("mkw1", [5, 512], BF16),
        keep=din("keep", [128, 32]), rc=din("rc", [128, 4, 128]), ez=din("ez", [128, 2]),
        y=dout("y", [NTOK, 1024]), o_ckv=dout("o_ckv", [2, NTOK, 256]), o_kr=dout("o_kr", [2, NTOK, 32]),
        o_ret=dout("o_ret", [2, 6, 4, 128, 128]),
    )

    def T(name, shape, dt):
        return nc.alloc_sbuf_tensor("sb_" + name, list(shape), dt), Buf(name)

    xT, XT = T("xT", [128, 8, 1024], F32)
    hT, HT = T("hT", [128, 8, 1024], BF16)
    brT, BRT = T("brT", [128, 12, 1024], BF16)
    slots = [T(f"ws{i}", [128, 2048], BF16) for i in range(NSLOT)]
    ident, IDENT = T("ident", [128, 128], F32)
    onesb, ONESB = T("onesb", [128, 128], BF16)
    cwsw, CWSW = T("cwsw", [128, 256], BF16)
    rc, RC = T("rc", [128, 4, 128], F32)
    ez, EZ = T("ez", [128, 2], F32)
    lg, LG = T("lg", [128, 16], F32)
    lgcol, LGCOL = T("lgcol", [128, 8], F32)
    Dm, DM = T("Dm", [128, 8, 128], F32)
    Xi, XI = T("Xi", [128, 8, 128], F32)
    Zt, ZT = T("Zt", [128, 2, 4, 2], F32)
    gC, GC = T("gC", [128, 16], F32)
    keep, KEEP = T("keep", [128, 2, 2, 8], F32)
    modT, MODT = T("modT", [128, 2, 24, 2], F32)
    g1, G1 = T("g1", [128, 2, 2, 8], F32)
    ngT, NGT = T("ngT", [128, 2, 8], F32)
    bmT, BMT = T("bmT", [128, 2, 24], F32)
    fgT, FGT = T("fgT", [128, 8], F32)
    gqT, GQT = T("gqT", [128, 2, 3], F32)
    gkv, GKV = T("gkv", [128, 2, 256], F32)
    cvT, CVT = T("cvT", [128, 8, 2], F32)
    scv, SCV = T("scv", [128, 8, 2], BF16)
    epsc, EPSC = T("epsc", [128, 1], F32)
    tmp16, TMP16 = T("tmp16", [128, 16], F32)
    arena = nc.alloc_sbuf_tensor("arena", [128, ARENA // 2], BF16)
    ps = nc.alloc_psum_tensor("ps", [128, 4096], F32)
    PSB = [Buf(f"ps{i}", excl=True) for i in range(8)]
    st = {"rot": 0, "aoff": 0}

    def bank(i):
        return ps[:, i * 512:(i + 1) * 512]

    def nb():
        i = st["rot"] % st.get("nrot", 6)
        st["rot"] = i + 1
        return i

    def areset():
        inh = st.setdefault("inh", {})
        for b in st.setdefault("pbufs", []):
            if b.w is not None and inh.get(b.w[0], 0) < b.w[1]:
                inh[b.w[0]] = b.w[1]
            for s_, v_ in b.r.items():
                if inh.get(s_, 0) < v_:
                    inh[s_] = v_
        st["pbufs"] = []
        st["aoff"] = 0

    def aget(shape, dt, name=""):
        assert shape[0] == 128
        fs = list(shape[1:])
        n = int(np.prod(fs))
        nbytes = n * (4 if dt == F32 else 2)
        off = st["aoff"]
        st["aoff"] = off + ((nbytes + 31) // 32) * 32
        assert st["aoff"] <= ARENA, (name, st["aoff"])
        a = arena[:, off // 2:(off + nbytes) // 2]
        if dt == F32:
            a = a.bitcast(F32)
        if len(fs) == 2:
            a = a.rearrange("p (a b) -> p a b", b=fs[1])
        elif len(fs) == 3:
            a = a.rearrange("p (a b c) -> p a b c", b=fs[1], c=fs[2])
        nbuf = Buf(name)
        nbuf.r = dict(st.get("inh", {}))
        st.setdefault("pbufs", []).append(nbuf)
        return a, nbuf

    def mm(out, lhsT, rhs, start, stop, rd, wr):
        S.op("pe", lambda e: e.matmul(out, lhsT=lhsT, rhs=rhs, start=start, stop=stop), rd, wr)

    def tr(out, in_, rd, wr):
        S.op("pe", lambda e: e.transpose(out, in_, ident[:]), list(rd) + [IDENT], wr)

    def act(out, in_, func, rd, wr, **kw):
        S.op("act", lambda e: e.activation(out=out, in_=in_, func=func, **kw), rd, wr)

    def tt(out, in0, in1, op, rd, wr):
        S.op("dve", lambda e: e.tensor_tensor(out=out, in0=in0, in1=in1, op=op), rd, wr)

    def ts(out, in0, s1, op0, rd, wr, s2=None, op1=None):
        if op1 is None:
            S.op("dve", lambda e: e.tensor_scalar(out=out, in0=in0, scalar1=s1, scalar2=None, op0=op0), rd, wr)
        else:
            S.op("dve", lambda e: e.tensor_scalar(out=out, in0=in0, scalar1=s1, scalar2=s2, op0=op0, op1=op1), rd, wr)

    def stt(out, in0, scalar, in1, op0, op1, rd, wr):
        S.op("dve", lambda e: e.scalar_tensor_tensor(out=out, in0=in0, scalar=scalar, in1=in1, op0=op0, op1=op1), rd, wr)

    def pool_ts(out, in0, s1, op0, rd, wr):
        S.op("pool", lambda e: e.tensor_scalar(out=out, in0=in0, scalar1=s1, scalar2=None, op0=op0), rd, wr)

    def pool_tt(out, in0, in1, op, rd, wr):
        S.op("pool", lambda e: e.tensor_tensor(out=out, in0=in0, in1=in1, op=op), rd, wr)

    def cp(out, in_, rd, wr):
        S.op("dve", lambda e: e.tensor_copy(out=out, in_=in_), rd, wr)

    def recip(out, in_, rd, wr):
        S.op("dve", lambda e: e.reciprocal(out=out, in_=in_), rd, wr)

    def rsum(out, in_, rd, wr):
        S.op("dve", lambda e: e.tensor_reduce(out=out, in_=in_, axis=AX.X, op=ALU.add), rd, wr)

    def mset(ap, val, wr):
        S.op("dve", lambda e: e.memset(ap, val), (), wr)

    def dma(out, in_, rd, wr, q="sp"):
        S.dma(q, lambda e: e.dma_start(out=out, in_=in_), rd, wr)

    class WS:
        def __init__(self):
            self.specs = []
            self.i = 0
            self.issued = 0

        def get(self, src, kc, ncols):
            if S.plan:
                self.specs.append((src, kc, ncols))
                t, b = slots[0]
                return t[:, 0:kc * ncols].rearrange("p (k c) -> p k c", c=ncols), b
            while self.issued < min(len(self.specs), self.i + PF + 1):
                j = self.issued
                s_src, s_kc, s_nc = self.specs[j]
                t, b = slots[j % NSLOT]
                dma(t[:, 0:s_kc * s_nc].rearrange("p (k c) -> p k c", c=s_nc), s_src, (), [b], q="pool")
                self.issued += 1
            t, b = slots[self.i % NSLOT]
            self.i += 1
            return t[:, 0:kc * ncols].rearrange("p (k c) -> p k c", c=ncols), b

    ws = WS()

    def wsrc(ap2d, c0, ncols):
        return ap2d.rearrange("(k p) c -> p k c", p=128)[:, :, c0:c0 + ncols]

    def body():
        st["rot"] = 0
        st["aoff"] = 0
        st["inh"] = {}
        st["pbufs"] = []
        dma(ident[:], D["ident"], (), [IDENT])
        dma(cwsw[:], D["cwsw"], (), [CWSW])
        dma(rc[:], D["rc"], (), [RC])
        dma(ez[:], D["ez"], (), [EZ])
        dma(keep[:], D["keep"].rearrange("p (a b c) -> p a b c", a=2, b=2), (), [KEEP])
        dma(ngT[:], D["ngT"], (), [NGT])
        dma(bmT[:], D["bmT"], (), [BMT])
        dma(fgT[:], D["fgT"], (), [FGT])
        dma(gqT[:], D["gqT"], (), [GQT])
        dma(cvT[:], D["cvT"], (), [CVT])
        dma(gkv[:], D["gkv"].partition_broadcast(128).rearrange("p (a b) -> p a b", a=2), (), [GKV])
        dma(lg[:], D["rdl"].partition_broadcast(128), (), [LG])
        ck(0.2)
        mset(onesb[:], 1.0, [ONESB])
        mset(epsc[:], EPS, [EPSC])
        act(lg[:], lg[:], AF.Exp, [LG], [LG], scale=-1.0)
        act(lg[:], lg[:], AF.Ln, [LG], [LG], bias=1.0)
        ts(lg[:], lg[:], -1.0, ALU.mult, [LG], [LG])
        ck(0.4)
        lgv = lg[:].rearrange("p (l d h) -> p l d h", l=2, d=2)
        lgc = lgcol[:].rearrange("p (l h) -> p l h", l=2)
        cp(lgc[0:64], lgv[0:64, :, 0, :], [LG], [LGCOL])
        cp(lgc[64:128], lgv[64:128, :, 1, :], [LG], [LGCOL])
        act(gC[:], lg[:], AF.Exp, [LG], [GC], scale=128.0)
        ck(0.6)
        for l in range(2):
            for h in range(4):
                jf, jb, j = l * 8 + h, l * 8 + 4 + h, l * 4 + h
                ts(Dm[:, j, :], rc[:, 0, :], lg[:, jf:jf + 1], ALU.mult, [RC, LG], [DM])
                stt(Dm[:, j, :], rc[:, 1, :], lg[:, jb:jb + 1], Dm[:, j, :], ALU.mult, ALU.add, [RC, LG, DM], [DM])
                act(Dm[:, j, :], Dm[:, j, :], AF.Exp, [DM], [DM])
                tt(Dm[:, j, :], Dm[:, j, :], rc[:, 2, :], ALU.mult, [DM, RC], [DM])
                ck(0.7)
                act(Xi[:, j, :], rc[:, 3, :], AF.Exp, [RC, LGCOL], [XI], scale=lgcol[:, j:j + 1])
                ck(0.8)
                for d in range(2):
                    jj = l * 8 + d * 4 + h
                    act(Zt[:, l, h, d:d + 1], ez[:, d:d + 1], AF.Exp, [EZ, LG], [ZT], scale=lg[:, jj:jj + 1])
                ck(0.9)
        ck(0.95)
        ts(Zt[:], Zt[:], 0.125, ALU.mult, [ZT], [ZT])
        ck(0.97)
        ck(1)
        act(scv[:], cvT[:], AF.Silu, [CVT], [SCV])

        def emit_mod(l, b0=0, b1=12, with_g1=True):
            for blk in range(b0, b1):
                w, WB = ws.get(wsrc(D["w_mod"][l], blk * 256, 256), 8, 256)
                for fc in range(2):
                    f = blk * 2 + fc
                    for k in range(8):
                        mm(bank(7)[:, f * 2:f * 2 + 2], w[:, k, fc * 128:fc * 128 + 128], scv[:, k, :], k == 0, k == 7,
                           [WB, SCV], [PSB[7]])
            f0, f1 = 2 * b0, 2 * b1
            tt(modT[:, l, f0:f1, :], bank(7)[:, f0 * 2:f1 * 2].rearrange("p (f j) -> p f j", j=2),
               bmT[:, l, f0:f1].unsqueeze(2).broadcast_to([128, f1 - f0, 2]), ALU.add, [PSB[7], BMT], [MODT])
            if with_g1:
                for j in range(2):
                    ts(g1[:, l, j, :], modT[:, l, 8:16, j], 1.0, ALU.add, [MODT], [G1])
                    tt(g1[:, l, j, :], g1[:, l, j, :], ngT[:, l, :], ALU.mult, [G1, NGT], [G1])

        ck(2)
        for j, (tok0, TT, NP) in enumerate(JOBS):
            NC, NTG, NK = TT // 128, TT // 512, NP + TT
            NKT = NK // 128
            areset()
            xs = [aget([128, 1024], F32, f"xs{i}") for i in range(4)]
            for n in range(NC):
                xa, XA = xs[n % 4]
                dma(xa, D["xin"][tok0 + n * 128:tok0 + (n + 1) * 128, :], (), [XA])
                for kq in range(2):
                    b = nb()
                    for c in range(4):
                        k = kq * 4 + c
                        tr(bank(b)[:, c * 128:(c + 1) * 128], xa[:, k * 128:(k + 1) * 128], [XA], [PSB[b]])
                    o = xT[:, kq * 4:kq * 4 + 4, n * 128:(n + 1) * 128]
                    i_ = bank(b).rearrange("p (c t) -> p c t", c=4)
                    if kq == 0:
                        cp(o, i_, [PSB[b]], [XT])
                    else:
                        act(o, i_, AF.Copy, [PSB[b]], [XT])

            if j == 0:
                emit_mod(0, 0, 8)
            ck(3)

            def norm_stats(rs, RS, sq, tgsl):
                b = nb()
                for k in range(8):
                    sa, SA = sq[k % 2]
                    if k % 2 == 0:
                        act(sa, xT[:, k, tgsl], AF.Square, [XT], [SA])
                    else:
                        tt(sa, xT[:, k, tgsl], xT[:, k, tgsl], ALU.mult, [XT], [SA])
                    mm(bank(b), onesb[:], sa, k == 0, k == 7, [ONESB, SA], [PSB[b]])
                act(rs, bank(b), AF.Ln, [PSB[b], EPSC], [RS], scale=1.0 / 1024, bias=epsc[:, 0:1])
                act(rs, rs, AF.Exp, [RS], [RS], scale=-0.5)

            for l in range(2):
                W_in = D["w_in"][l]
                areset()
                sq = [aget([128, 512], BF16, f"sq{i}") for i in range(2)]
                rsb = [aget([128, 512], F32, f"rs{i}") for i in range(NTG)]
                tf = [aget([128, 512], F32, f"tf{i}") for i in range(2)]
                for tg in range(NTG):
                    norm_stats(rsb[tg][0], rsb[tg][1], sq, slice(tg * 512, (tg + 1) * 512))
                for tg in range(NTG):
                    tgsl = slice(tg * 512, (tg + 1) * 512)
                    rs, RS = rsb[tg]
                    for k in range(8):
                        ta, TA = tf[k % 2]
                        stt(ta, xT[:, k, tgsl], g1[:, l, j, k:k + 1], rs, ALU.mult, ALU.mult, [XT, G1, RS], [TA])
                        act(hT[:, k, tgsl], ta, AF.Identity, [TA, MODT], [HT], bias=modT[:, l, k, j:j + 1], scale=1.0)
                if j == 0 and l == 0:
                    emit_mod(0, 8, 12, with_g1=False)

                ck(4)
                areset()
                Kz, KZ = aget([128, NC, 4, 128], BF16, "Kz")
                Vr, VR = aget([128, NC, 512], BF16, "Vr")
                Vc, VC = aget([128, NC, 512], BF16, "Vc")
                vs4, VS4 = aget([128, 4], F32, "vs4")
                QT = [aget([128, TT], BF16, f"QT{i}") for i in range(2)]
                QTs = [aget([128, TT], BF16, f"QTs{i}") for i in range(2)]
                KT = [aget([128, TT], BF16, f"KT{i}") for i in range(2)]
                gR = [aget([128, TT], BF16, f"gR{i}") for i in range(2)]
                R, RB = aget([128, NC + 2, 128], F32, "R")
                RBB = Buf("Rb")
                RBB.r = dict(RB.r)
                st["pbufs"].append(RBB)
                Sin, SIN = aget([128, NC, 128], F32, "Sin")
                SINB = Buf("Sinb")
                SINB.r = dict(SIN.r)
                st["pbufs"].append(SINB)
                Scs = [aget([128, NC, 128], BF16, f"Sc{i}") for i in range(2)]
                srs, SRS = aget([128, NC], F32, "srs")
                dk, DKB = aget([128, NC], F32, "dk")
                AT = [aget([128, 512], BF16, f"AT{i}") for i in range(2)]
                sqo2 = [aget([128, 512], BF16, f"sqo{i}") for i in range(2)]
                rso2 = [aget([128, 512], F32, f"rso{i}") for i in range(2)]
                to2 = [aget([128, 512], F32, f"to{i}") for i in range(2)]
                wq = [ws.get(wsrc(D["w_rqd"][l], i * 256, 256), 8, 256) for i in range(2)]
                wk, WK = ws.get(wsrc(W_in, 256, 256), 8, 256)
                wv = [ws.get(wsrc(W_in, 512 + i * 256, 256), 8, 256) for i in range(2)]
                wz = [ws.get(wsrc(W_in, 1024 + i * 256, 256), 8, 256) for i in range(2)]
                for n in range(NC):
                    nsl = slice(n * 128, (n + 1) * 128)
                    bK = nb()
                    for k in range(8):
                        mm(bank(bK)[:, 0:256], hT[:, k, nsl], wk[:, k, :], k == 0, k == 7, [HT, WK], [PSB[bK]])
                    bV = nb()
                    for i in range(2):
                        for k in range(8):
                            mm(bank(bV)[:, i * 256:(i + 1) * 256], hT[:, k, nsl], wv[i][0][:, k, :], k == 0, k == 7,
                               [HT, wv[i][1]], [PSB[bV]])
                    tt(Kz[:, n].rearrange("p h (d e) -> p h d e", d=2),
                       bank(bK)[:, 0:256].rearrange("p (h e) -> p h e", h=4).unsqueeze(2).broadcast_to([128, 4, 2, 64]),
                       Zt[:, l].unsqueeze(3).broadcast_to([128, 4, 2, 64]), ALU.mult, [PSB[bK], ZT], [KZ])
                    act(Vr[:, n, :], bank(bV), AF.Copy, [PSB[bV]], [VR])
                    rsum(vs4, bank(bV).rearrange("p (h e) -> p h e", h=4), [PSB[bV]], [VS4])
                    ts(vs4, vs4, -1.0 / 128, ALU.mult, [VS4], [VS4])
                    tt(Vc[:, n, :].rearrange("p (h e) -> p h e", h=4), bank(bV).rearrange("p (h e) -> p h e", h=4),
                       vs4.unsqueeze(2).broadcast_to([128, 4, 128]), ALU.add, [PSB[bV], VS4], [VC])
                def head_vars(h):
                    bi = h % 2
                    return (bi,) + QT[bi] + QTs[bi] + KT[bi] + gR[bi] + wq[h // 2] + wz[h // 2] + ((h % 2) * 128,) + Scs[bi]

                def stage_a1(h):
                    bi, qt, QTB, qs, QSB, kt_, KTB, gr, GRB, wqh, WQH, wzh, WZH, co, Sc, SCB = head_vars(h)
                    for tg in range(NTG):
                        tgsl = slice(tg * 512, (tg + 1) * 512)
                        b = nb()
                        for k in range(8):
                            mm(bank(b), wqh[:, k, co:co + 128], hT[:, k, tgsl], k == 0, k == 7, [WQH, HT], [PSB[b]])
                        act(qt[0:64, tgsl], bank(b)[0:64, :], AF.Copy, [PSB[b]], [QTB])
                        tt(qs[:, tgsl].rearrange("p (c t) -> p c t", c=4), bank(b).rearrange("p (c t) -> p c t", c=4),
                           Xi[:, l * 4 + h, :].unsqueeze(1).broadcast_to([128, 4, 128]), ALU.mult, [PSB[b], XI], [QSB])
                        b = nb()
                        for k in range(8):
                            mm(bank(b)[0:64, :], wk[:, k, h * 64:(h + 1) * 64], hT[:, k, tgsl], k == 0, k == 7, [WK, HT], [PSB[b]])
                        act(kt_[0:64, tgsl], bank(b)[0:64, :], AF.Copy, [PSB[b]], [KTB])

                def stage_a2(h):
                    bi, qt, QTB, qs, QSB, kt_, KTB, gr, GRB, wqh, WQH, wzh, WZH, co, Sc, SCB = head_vars(h)
                    dma(R[0:64, 0, :], D["s0"][l, 0, h], (), [RB])
                    dma(R[64:128, NC, :], D["s0"][l, 1, h], (), [RBB])
                    kvb = []
                    for n in range(NC):
                        if n % 4 == 0:
                            kvb.append(6 + (n // 4))
                        b = kvb[-1]
                        mm(bank(b)[:, (n % 4) * 128:(n % 4 + 1) * 128], Kz[:, n, h, :], Vr[:, n, h * 128:(h + 1) * 128],
                           True, True, [KZ, VR], [PSB[b]])
                    jf, jb = l * 8 + h, l * 8 + 4 + h
                    ts(dk[0:64, :], keep[0:64, j, 0, 0:NC], gC[0:64, jf:jf + 1], ALU.mult, [KEEP, GC], [DKB])
                    ts(dk[64:128, :], keep[64:128, j, 1, 0:NC], gC[64:128, jb:jb + 1], ALU.mult, [KEEP, GC], [DKB])
                    for n in range(NC):
                        b = kvb[n // 4]
                        stt(R[0:64, n + 1, :], R[0:64, n, :], dk[0:64, n:n + 1], bank(b)[0:64, (n % 4) * 128:(n % 4 + 1) * 128],
                            ALU.mult, ALU.add, [RB, DKB, PSB[b]], [RB])
                    for n in range(NC - 1, -1, -1):
                        b = kvb[n // 4]
                        stt(R[64:128, n, :], R[64:128, n + 1, :], dk[64:128, n:n + 1], bank(b)[64:128, (n % 4) * 128:(n % 4 + 1) * 128],
                            ALU.mult, ALU.add, [RBB, DKB, PSB[b]], [RBB])
                    tt(Sin[0:64], R[0:64, 0:NC, :], keep[0:64, j, 0, 0:NC].unsqueeze(2).broadcast_to([64, NC, 128]), ALU.mult,
                       [RB, KEEP], [SIN])
                    tt(Sin[64:128], R[64:128, 1:NC + 1, :], keep[64:128, j, 1, 0:NC].unsqueeze(2).broadcast_to([64, NC, 128]), ALU.mult,
                       [RBB, KEEP], [SINB])
                    rsum(srs, Sin, [SIN, SINB], [SRS])
                    ts(srs, srs, -1.0 / 128, ALU.mult, [SRS], [SRS])
                    tt(Sc, Sin, srs.unsqueeze(2).broadcast_to([128, NC, 128]), ALU.add, [SIN, SINB, SRS], [SCB])
                    ns = NC // 2
                    dma(D["o_ret"][l, SLOT0[j]:SLOT0[j] + ns, h, 0:64, :].rearrange("s p e -> p s e"),
                        R[0:64, 2:NC + 2, :].rearrange("p (s two) e -> p s two e", two=2)[:, :, 0, :], [RB], ())
                    dma(D["o_ret"][l, SLOT0[j]:SLOT0[j] + ns, h, 64:128, :].rearrange("s p e -> p s e"),
                        R[64:128, 0:NC, :].rearrange("p (s two) e -> p s two e", two=2)[:, :, 0, :], [RBB], ())

                def stage_b(h):
                    bi, qt, QTB, qs, QSB, kt_, KTB, gr, GRB, wqh, WQH, wzh, WZH, co, Sc, SCB = head_vars(h)
                    tgs = list(range(NTG))
                    bS, bO, bv = {}, {}, {}
                    for tg in tgs:
                        bS[tg] = nb()
                        for c4 in range(4):
                            n = tg * 4 + c4
                            nsl = slice(n * 128, (n + 1) * 128)
                            mm(bank(bS[tg])[:, c4 * 128:(c4 + 1) * 128], kt_[0:64, nsl], qt[0:64, nsl], True, True, [KTB, QTB], [PSB[bS[tg]]])
                    for tg in tgs:
                        at, ATB = AT[tg % 2]
                        tt(at.rearrange("p (c t) -> p c t", c=4), bank(bS[tg]).rearrange("p (c t) -> p c t", c=4),
                           Dm[:, l * 4 + h, :].unsqueeze(1).broadcast_to([128, 4, 128]), ALU.mult, [PSB[bS[tg]], DM], [ATB])
                    for tg in tgs:
                        at, ATB = AT[tg % 2]
                        bO[tg] = nb()
                        for c4 in range(4):
                            n = tg * 4 + c4
                            nsl = slice(n * 128, (n + 1) * 128)
                            osl = bank(bO[tg])[:, c4 * 128:(c4 + 1) * 128]
                            mm(osl, Vc[:, n, h * 128:(h + 1) * 128], at[:, c4 * 128:(c4 + 1) * 128], True, False, [VC, ATB], [PSB[bO[tg]]])
                            mm(osl, Sc[:, n, :], qs[:, nsl], False, True, [SCB, QSB], [PSB[bO[tg]]])
                    for tg in tgs:
                        act(sqo2[tg % 2][0], bank(bO[tg]), AF.Square, [PSB[bO[tg]]], [sqo2[tg % 2][1]])
                    for tg in tgs:
                        bv[tg] = nb()
                        mm(bank(bv[tg]), onesb[:], sqo2[tg % 2][0], True, True, [ONESB, sqo2[tg % 2][1]], [PSB[bv[tg]]])
                    for tg in tgs:
                        r_, RSB_ = rso2[tg % 2]
                        act(r_, bank(bv[tg]), AF.Ln, [PSB[bv[tg]], EPSC], [RSB_], scale=1.0 / 128, bias=epsc[:, 0:1])
                    for tg in tgs:
                        r_, RSB_ = rso2[tg % 2]
                        act(r_, r_, AF.Exp, [RSB_], [RSB_], scale=-0.5)
                    for tg in tgs:
                        r_, RSB_ = rso2[tg % 2]
                        t_, TOB_ = to2[tg % 2]
                        tt(t_, bank(bO[tg]), r_, ALU.mult, [PSB[bO[tg]], RSB_], [TOB_])
                    for tg in tgs:
                        tgsl = slice(tg * 512, (tg + 1) * 512)
                        t_, TOB_ = to2[tg % 2]
                        tt(brT[:, h, tgsl], t_, brT[:, 4 + h, tgsl], ALU.mult, [TOB_, BRT], [BRT])

                for h in range(4):
                    wzh, WZH = wz[h // 2]
                    co = (h % 2) * 128
                    for tg in range(NTG):
                        tgsl = slice(tg * 512, (tg + 1) * 512)
                        b = nb()
                        for k in range(8):
                            mm(bank(b), wzh[:, k, co:co + 128], hT[:, k, tgsl], k == 0, k == 7, [WZH, HT], [PSB[b]])
                        act(brT[:, 4 + h, tgsl], bank(b), AF.Silu, [PSB[b]], [BRT])
                stage_a1(0)
                stage_a2(0)
                stage_a1(1)
                for h in range(4):
                    stage_b(h)
                    if h + 1 < 4:
                        stage_a2(h + 1)
                    if h + 2 < 4:
                        stage_a1(h + 2)

                ck(5)
                if j == 0 and l == 0:
                    emit_mod(1)
                areset()
                kvu, KVU = aget([128, 2, 1024], BF16, "kvu")
                ckvT, CKVT = aget([128, 2, NK], BF16, "ckvT")
                KRM, KRMB = aget([128, NK], BF16, "KRM")
                Kaug = [aget([128, NK], BF16, f"Kaug{i}") for i in range(2)]
                Qaug = [aget([128, 512], BF16, f"Qaug{i}") for i in range(2)]
                qlT, QLT = aget([128, 3, TT], BF16, "qlT")
                gM = [aget([128, TT], BF16, f"gM{i}") for i in range(2)]
                Vx, VX = aget([128, NKT, 4, 128], BF16, "Vx")
                PT = [aget([128, 512], BF16, f"PT{i}") for i in range(3)]
                stg = [aget([128, 288], F32, f"stg{i}") for i in range(4)]
                kst, KST = aget([128, 96], F32, "kst")
                t1, T1 = aget([128, 512], F32, "t1")
                t2, T2 = aget([128, 512], F32, "t2")
                t3, T3 = aget([128, 512], F32, "t3")
                rq, RQ = aget([128, 512], F32, "rq")
                sqm = [aget([128, 512], BF16, f"sqm{i}") for i in range(4)]
                rden, RDEN = rq, RQ
                rope, ROPE = aget([128, 2, TT], F32, "rope")
                ssq, SSQ = aget([128, NC], F32, "ssq")
                rsd, RSD = aget([128, NC], F32, "rsd")
                dma(kvu, D["w_kvu"][l].rearrange("(k p) c -> p k c", p=128), (), [KVU], q="pool")
                dma(rope[64:96], D[f"rope{j}"], (), [ROPE])
                dma(KRM[96:101, :], D[f"mku{j}"], (), [KRMB])
                for qi in range(2):
                    tg = qi % NTG
                    dma(Qaug[qi][0][96:101, :], D[f"mkw{j}"][:, tg * 512:(tg + 1) * 512], (), [Qaug[qi][1]])
                mset(kst, 0.0, [KST])
                mset(ssq, 0.0, [SSQ])
                vx5 = Vx.rearrange("p k (a two) e -> p k a two e", two=2)
                mset(vx5[:, :, :, 0, 64:128], 1.0, [VX])
                mset(vx5[:, :, :, 1, 0:64], 1.0, [VX])
                wkl, WKL = ws.get(wsrc(W_in, 1920, 256), 8, 256)
                wkr, WKR = ws.get(wsrc(D["w_krp"][l], 0, 192), 8, 192)
                pairs = [[n0, n0 + 1] for n0 in range(0, NC, 2)]
                bk, btk = {}, {}

                def m1_proj(pair):
                    for n in pair:
                        nsl = slice(n * 128, (n + 1) * 128)
                        bk[n] = nb()
                        for k in range(8):
                            mm(bank(bk[n])[:, 0:256], hT[:, k, nsl], wkl[:, k, :], k == 0, k == 7, [HT, WKL], [PSB[bk[n]]])
                        for k in range(8):
                            mm(bank(bk[n])[:, 256:288], hT[:, k, nsl], wkr[:, k, 64:96], k == 0, k == 7, [HT, WKR], [PSB[bk[n]]])

                def m1_norm(pair):
                    for n in pair:
                        sa, SA = sqm[n % 4]
                        act(sa[:, 0:256], bank(bk[n])[:, 0:256], AF.Square, [PSB[bk[n]]], [SA, SSQ], accum_out=ssq[:, n:n + 1])
                    for n in pair:
                        act(rsd[:, n:n + 1], ssq[:, n:n + 1], AF.Ln, [SSQ, EPSC], [RSD], scale=1.0 / 256, bias=epsc[:, 0:1])
                    for n in pair:
                        act(rsd[:, n:n + 1], rsd[:, n:n + 1], AF.Exp, [RSD], [RSD], scale=-0.5)
                    for n in pair:
                        sg, SG = stg[n % 4]
                        stt(sg[:, 0:256], bank(bk[n])[:, 0:256], rsd[:, n:n + 1], gkv[:, l, :], ALU.mult, ALU.mult, [PSB[bk[n]], RSD, GKV], [SG])
                        cp(sg[:, 256:288], bank(bk[n])[:, 256:288], [PSB[bk[n]]], [SG])
                    for n in pair:
                        sg, SG = stg[n % 4]
                        dma(D["o_ckv"][l, tok0 + n * 128:tok0 + (n + 1) * 128, :], sg[:, 0:256], [SG], ())
                        dma(D["o_kr"][l, tok0 + n * 128:tok0 + (n + 1) * 128, :], sg[:, 256:288], [SG], ())

                def m1_tr(pair):
                    for n in pair:
                        sg, SG = stg[n % 4]
                        btk[n] = nb()
                        for c2 in range(2):
                            tr(bank(btk[n])[:, c2 * 128:(c2 + 1) * 128], sg[:, c2 * 128:(c2 + 1) * 128], [SG], [PSB[btk[n]]])
                    for n in pair:
                        act(ckvT[:, :, NP + n * 128:NP + (n + 1) * 128], bank(btk[n])[:, 0:256].rearrange("p (c t) -> p c t", c=2),
                            AF.Copy, [PSB[btk[n]]], [CKVT])

                m1_proj(pairs[0])
                for pi, pair in enumerate(pairs):
                    m1_norm(pair)
                    if pi + 1 < len(pairs):
                        m1_proj(pairs[pi + 1])
                    m1_tr(pair)
                for pt in range(NP // 128):
                    sg, SG = stg[pt % 4]
                    dma(sg[:, 0:256], D["cckv"][l, pt * 128:(pt + 1) * 128, :], (), [SG])
                    dma(kst[:, 64:96], D["ckr"][l, pt * 128:(pt + 1) * 128, :], (), [KST])
                    bt = nb()
                    for c2 in range(2):
                        tr(bank(bt)[:, c2 * 128:(c2 + 1) * 128], sg[:, c2 * 128:(c2 + 1) * 128], [SG], [PSB[bt]])
                    tr(bank(bt)[0:96, 256:384], kst[:, 0:96], [KST], [PSB[bt]])
                    act(ckvT[:, :, pt * 128:(pt + 1) * 128], bank(bt)[:, 0:256].rearrange("p (c t) -> p c t", c=2),
                        AF.Copy, [PSB[bt]], [CKVT])
                    cp(KRM[64:96, pt * 128:(pt + 1) * 128], bank(bt)[64:96, 256:384], [PSB[bt]], [KRMB])
                for tg in range(NTG):
                    tgsl = slice(tg * 512, (tg + 1) * 512)
                    bA = nb()
                    for k in range(8):
                        mm(bank(bA)[0:96, :], wkr[:, k, 0:96], hT[:, k, tgsl], k == 0, k == 7, [WKR, HT], [PSB[bA]])
                    bB = nb()
                    for k in range(8):
                        mm(bank(bB)[0:96, :], wkr[:, k, 96:192], hT[:, k, tgsl], k == 0, k == 7, [WKR, HT], [PSB[bB]])
                    tt(t1[64:96], bank(bA)[64:96, :], rope[64:96, 0, tgsl], ALU.mult, [PSB[bA], ROPE], [T1])
                    tt(t2[64:96], bank(bB)[64:96, :], rope[64:96, 1, tgsl], ALU.mult, [PSB[bB], ROPE], [T2])
                    tt(KRM[64:96, NP + tg * 512:NP + (tg + 1) * 512], t1[64:96], t2[64:96], ALU.add, [T1, T2], [KRMB])
                act(Kaug[0][0][64:101, :], KRM[64:101, :], AF.Copy, [KRMB], [Kaug[0][1]])
                cp(Kaug[1][0][64:101, :], KRM[64:101, :], [KRMB], [Kaug[1][1]])
                wq0, WQ0 = ws.get(wsrc(W_in, 1536, 256), 8, 256)
                wq1, WQ1 = ws.get(wsrc(W_in, 1792, 128), 8, 128)
                for tg in range(NTG):
                    tgsl = slice(tg * 512, (tg + 1) * 512)
                    bc = [nb() for _ in range(3)]
                    for c in range(3):
                        for k in range(8):
                            if c < 2:
                                mm(bank(bc[c]), wq0[:, k, c * 128:(c + 1) * 128], hT[:, k, tgsl], k == 0, k == 7, [WQ0, HT], [PSB[bc[c]]])
                            else:
                                mm(bank(bc[c]), wq1[:, k, 0:128], hT[:, k, tgsl], k == 0, k == 7, [WQ1, HT], [PSB[bc[c]]])
                    bs = nb()
                    for c in range(3):
                        sa, SA = sqm[c % 2]
                        act(sa, bank(bc[c]), AF.Square, [PSB[bc[c]]], [SA])
                        mm(bank(bs), onesb[:], sa, c == 0, c == 2, [ONESB, SA], [PSB[bs]])
                    act(rq, bank(bs), AF.Ln, [PSB[bs], EPSC], [RQ], scale=1.0 / 384, bias=epsc[:, 0:1])
                    act(rq, rq, AF.Exp, [RQ], [RQ], scale=-0.5)
                    for c in range(3):
                        stt(qlT[:, c, tgsl], bank(bc[c]), gqT[:, l, c:c + 1], rq, ALU.mult, ALU.mult, [PSB[bc[c]], GQT, RQ], [QLT])
                kvu3 = kvu.rearrange("p k (h x) -> p k h x", h=8)
                wst = {}

                def prep_head(h):
                    c = h // 2
                    ka, KA = Kaug[h % 2]
                    gm, GMB = gM[c % 2]
                    if h % 2 == 0:
                        wst["wqu"] = ws.get(D["w_qu2"][l].rearrange("(c p) x -> p c x", p=128)[:, :, c * 384:(c + 1) * 384], 3, 384)
                    wst[("wqu", h)] = wst["wqu"]
                    for kg in range(NK // 512):
                        b = nb()
                        for c2 in range(2):
                            mm(bank(b)[0:64, :], kvu[:, c2, h * 128:h * 128 + 64], ckvT[:, c2, kg * 512:(kg + 1) * 512],
                               c2 == 0, c2 == 1, [KVU, CKVT], [PSB[b]])
                        act(ka[0:64, kg * 512:(kg + 1) * 512], bank(b)[0:64, :], AF.Copy, [PSB[b]], [KA])

                def prep_q(h, tg, bc):
                    tgsl = slice(tg * 512, (tg + 1) * 512)
                    qa, QA = Qaug[bc % 2]
                    wqu, WQU = wst[("wqu", h)]
                    qo = (h % 2) * 192
                    bA = 4
                    for c3 in range(3):
                        mm(bank(bA)[0:96, :], wqu[:, c3, qo:qo + 96], qlT[:, c3, tgsl], c3 == 0, c3 == 2, [WQU, QLT], [PSB[bA]])
                    bB = 5
                    for c3 in range(3):
                        mm(bank(bB)[0:96, :], wqu[:, c3, qo + 96:qo + 192], qlT[:, c3, tgsl], c3 == 0, c3 == 2, [WQU, QLT], [PSB[bB]])
                    act(qa[0:64, :], bank(bA)[0:64, :], AF.Copy, [PSB[bA]], [QA])
                    tt(t1[64:96], bank(bA)[64:96, :], rope[64:96, 0, tgsl], ALU.mult, [PSB[bA], ROPE], [T1])
                    tt(t2[64:96], bank(bB)[64:96, :], rope[64:96, 1, tgsl], ALU.mult, [PSB[bB], ROPE], [T2])
                    tt(qa[64:96, :], t1[64:96], t2[64:96], ALU.add, [T1, T2], [QA])

                def attn(h, tg, bc):
                    tgsl = slice(tg * 512, (tg + 1) * 512)
                    hh = h % 4
                    par = h % 2
                    po, pd = par * 64, (1 - par) * 64
                    c = h // 2
                    ka, KA = Kaug[h % 2]
                    gm, GMB = gM[c % 2]
                    qa, QA = Qaug[bc % 2]
                    bo = 6 + (bc % 2)
                    LA = 2
                    sb_ = {}

                    def score(kt):
                        sb_[kt] = nb()
                        mm(bank(sb_[kt]), ka[0:101, kt * 128:(kt + 1) * 128], qa[0:101, :], True, True, [KA, QA], [PSB[sb_[kt]]])
                    for kt in range(min(LA, NKT)):
                        score(kt)
                    for kt in range(NKT):
                        pt_, PTB = PT[kt % 3]
                        act(pt_, bank(sb_[kt]), AF.Exp, [PSB[sb_[kt]]], [PTB], scale=SC)
                        if kt + LA < NKT:
                            score(kt + LA)
                        mm(bank(bo), Vx[:, kt, hh, :], pt_, kt == 0, kt == NKT - 1, [VX, PTB], [PSB[bo]])
                    if j == 1:
                        act(rden[po:po + 64], bank(bo)[pd:pd + 64, :], AF.Ln, [PSB[bo]], [RDEN])
                        act(rden[po:po + 64], rden[po:po + 64], AF.Exp, [RDEN], [RDEN], scale=-1.0)
                    else:
                        recip(rden[po:po + 64], bank(bo)[pd:pd + 64, :], [PSB[bo]], [RDEN])
                    tt(t3[po:po + 64], bank(bo)[po:po + 64, :], rden[po:po + 64], ALU.mult, [PSB[bo], RDEN], [T3])
                    tt(brT[po:po + 64, 4 + c, tgsl], t3[po:po + 64], brT[po:po + 64, 8 + c, tgsl], ALU.mult, [T3, BRT], [BRT])

                for c in range(4):
                    if c % 2 == 0:
                        wmz, WMZ = ws.get(wsrc(W_in, 2208 + (c // 2) * 256, 256), 8, 256)
                    for tg in range(NTG):
                        tgsl = slice(tg * 512, (tg + 1) * 512)
                        b = nb()
                        for k in range(8):
                            mm(bank(b), wmz[:, k, (c % 2) * 128:(c % 2 + 1) * 128], hT[:, k, tgsl], k == 0, k == 7, [WMZ, HT], [PSB[b]])
                        act(brT[:, 8 + c, tgsl], bank(b), AF.Silu, [PSB[b]], [BRT])
                bc = 0
                st["nrot"] = 4
                for hg in range(2):
                    for kt in range(NKT):
                        b = nb()
                        for c2 in range(2):
                            mm(bank(b)[:, 0:256], ckvT[:, c2, kt * 128:(kt + 1) * 128], kvu3[:, c2, hg * 4:hg * 4 + 4, 64:128],
                               c2 == 0, c2 == 1, [CKVT, KVU], [PSB[b]])
                        bv = bank(b)[:, 0:256].rearrange("p (a two e) -> p a two e", two=2, e=64)
                        act(vx5[:, kt, :, 0, 0:64], bv[:, :, 0, :], AF.Copy, [PSB[b]], [VX])
                        cp(vx5[:, kt, :, 1, 64:128], bv[:, :, 1, :], [PSB[b]], [VX])
                    hblocks = [(h, tg) for h in range(hg * 4, hg * 4 + 4) for tg in range(NTG)]
                    prep_head(hblocks[0][0])
                    prep_q(hblocks[0][0], hblocks[0][1], bc)
                    for idx, (h, tg) in enumerate(hblocks):
                        if idx + 1 < len(hblocks):
                            h2, tg2 = hblocks[idx + 1]
                            if h2 != h:
                                prep_head(h2)
                            prep_q(h2, tg2, bc + 1)
                        attn(h, tg, bc)
                        bc += 1
                st["nrot"] = 6

                areset()
                fuT, FUT = aget([128, 4, TT], BF16, "fuT")
                gF, GFB = aget([128, 4, TT], BF16, "gF")
                AB, ABB = aget([128, NC, 4, 256], BF16, "AB")
                wf = [ws.get(wsrc(W_in, 2720 + i * 256, 256), 8, 256) for i in range(2)]
                for tg in range(NTG):
                    tgsl = slice(tg * 512, (tg + 1) * 512)
                    for g in range(4):
                        b = nb()
                        for k in range(8):
                            mm(bank(b), wf[g // 2][0][:, k, (g % 2) * 128:(g % 2 + 1) * 128], hT[:, k, tgsl], k == 0, k == 7,
                               [wf[g // 2][1], HT], [PSB[b]])
                        act(fuT[:, g, tgsl], bank(b), AF.Copy, [PSB[b]], [FUT])
                wfz = [ws.get(wsrc(W_in, 3232 + i * 256, 256), 8, 256) for i in range(2)]
                for tg in range(NTG):
                    tgsl = slice(tg * 512, (tg + 1) * 512)
                    for g in range(4):
                        b = nb()
                        for k in range(8):
                            mm(bank(b), wfz[g // 2][0][:, k, (g % 2) * 128:(g % 2 + 1) * 128], hT[:, k, tgsl], k == 0, k == 7,
                               [wfz[g // 2][1], HT], [PSB[b]])
                        act(gF[:, g, tgsl], bank(b), AF.Silu, [PSB[b]], [GFB])
                for n in range(NC):
                    nsl = slice(n * 128, (n + 1) * 128)
                    bb = [nb(), nb()]
                    for g in range(4):
                        mm(bank(bb[g // 2])[:, (g % 2) * 256:(g % 2 + 1) * 256], fuT[:, g, nsl], cwsw[:], True, True, [FUT, CWSW], [PSB[bb[g // 2]]])
                    act(AB[:, n, 0:2, :], bank(bb[0]).rearrange("p (g x) -> p g x", g=2), AF.Copy, [PSB[bb[0]]], [ABB])
                    cp(AB[:, n, 2:4, :], bank(bb[1]).rearrange("p (g x) -> p g x", g=2), [PSB[bb[1]]], [ABB])
                for kb in range(TT // 256):
                    wc, WC = ws.get(wsrc(D[f"dftc{j}"], kb * 256, 256), NC, 256)
                    wsn, WSN = ws.get(wsrc(D[f"dfts{j}"], kb * 256, 256), NC, 256)
                    ksl = slice(kb * 256, (kb + 1) * 256)
                    for g in range(4):
                        b = nb()
                        for n in range(NC):
                            mm(bank(b)[:, 0:256], AB[:, n, g, 0:128], wc[:, n, :], n == 0, False, [ABB, WC], [PSB[b]])
                            mm(bank(b)[:, 0:256], AB[:, n, g, 128:256], wsn[:, n, :], False, n == NC - 1, [ABB, WSN], [PSB[b]])
                        tt(brT[:, 8 + g, ksl], bank(b)[:, 0:256], gF[:, g, ksl], ALU.mult, [PSB[b], GFB], [BRT])

                ck(7)
                areset()
                mT, MT = aget([128, 8, TT], BF16, "mT")
                sig = [aget([128, 512], F32, f"sig{i}") for i in range(3)]
                tm = [aget([128, 512], F32, f"tm{i}") for i in range(3)]
                for cpair in range(4):
                    gl = [ws.get(wsrc(W_in, 3744 + n * 1024 + cpair * 256, 256), 8, 256) for n in range(3)]
                    wb = [ws.get(wsrc(D["w_br"][l, n], cpair * 256, 256), 4, 256) for n in range(3)]
                    for cc in range(2):
                        c = cpair * 2 + cc
                        csl = slice(cc * 128, (cc + 1) * 128)
                        for tg in range(NTG):
                            tgsl = slice(tg * 512, (tg + 1) * 512)
                            for n in range(3):
                                bg = nb()
                                for k in range(8):
                                    mm(bank(bg), gl[n][0][:, k, csl], hT[:, k, tgsl], k == 0, k == 7, [gl[n][1], HT], [PSB[bg]])
                                act(sig[n][0], bank(bg), AF.Sigmoid, [PSB[bg]], [sig[n][1]])
                                bp = nb()
                                for k4 in range(4):
                                    mm(bank(bp), wb[n][0][:, k4, csl], brT[:, n * 4 + k4, tgsl], k4 == 0, k4 == 3, [wb[n][1], BRT], [PSB[bp]])
                                tt(tm[n][0], bank(bp), sig[n][0], ALU.mult, [PSB[bp], sig[n][1]], [tm[n][1]])
                            tt(tm[0][0], tm[0][0], tm[1][0], ALU.add, [tm[0][1], tm[1][1]], [tm[0][1]])
                            tt(mT[:, c, tgsl], tm[0][0], tm[2][0], ALU.add, [tm[0][1], tm[2][1]], [MT])
                for cpair in range(4):
                    wo, WO = ws.get(wsrc(D["w_out"][l], cpair * 256, 256), 8, 256)
                    for cc in range(2):
                        c = cpair * 2 + cc
                        csl = slice(cc * 128, (cc + 1) * 128)
                        for tg in range(NTG):
                            tgsl = slice(tg * 512, (tg + 1) * 512)
                            b = nb()
                            for k in range(8):
                                mm(bank(b), wo[:, k, csl], mT[:, k, tgsl], k == 0, k == 7, [WO, MT], [PSB[b]])
                            stt(xT[:, c, tgsl], bank(b), modT[:, l, 16 + c, j:j + 1], xT[:, c, tgsl], ALU.mult, ALU.add,
                                [PSB[b], MODT, XT], [XT])

                ck(8 + l)
            areset()
            sq = [aget([128, 512], BF16, f"fsq{i}") for i in range(2)]
            rsb = [aget([128, 512], F32, f"frs{i}") for i in range(NTG)]
            yT, YT = aget([128, 8, 512], F32, "yT")
            yst = [aget([128, 1024], F32, f"yst{i}") for i in range(2)]
            for tg in range(NTG):
                norm_stats(rsb[tg][0], rsb[tg][1], sq, slice(tg * 512, (tg + 1) * 512))
            for tg in range(NTG):
                tgsl = slice(tg * 512, (tg + 1) * 512)
                rs, RS = rsb[tg]
                for k in range(8):
                    stt(yT[:, k, :], xT[:, k, tgsl], fgT[:, k:k + 1], rs, ALU.mult, ALU.mult, [XT, FGT, RS], [YT])
                for c4 in range(4):
                    n = tg * 4 + c4
                    ya, YA = yst[n % 2]
                    bA, bB = nb(), nb()
                    for k in range(8):
                        bk = bA if k < 4 else bB
                        tr(bank(bk)[:, (k % 4) * 128:(k % 4 + 1) * 128], yT[:, k, c4 * 128:(c4 + 1) * 128], [YT], [PSB[bk]])
                    cp(ya[:, 0:512], bank(bA), [PSB[bA]], [YA])
                    act(ya[:, 512:1024], bank(bB), AF.Copy, [PSB[bB]], [YA])
                    dma(D["y"][tok0 + n * 128:tok0 + (n + 1) * 128, :], ya, [YA], ())

    S.plan = True
    try:
        body()
    except _Stop:
        pass
    S.plan = False
    try:
        body()
    except _Stop:
        pass
    print('ops', S.cnt, {q: sum(v) // 16 for q, v in S.dcnt.items()}, flush=True)
    S.finish()
    S.emit()
    return nc


def _consts(is_sample):
    bf = ml_dtypes.bfloat16
    c = {}
    c["ident"] = np.eye(128, dtype=np.float32)
    ci = np.arange(128)
    ang = 2 * np.pi * np.outer(ci, ci) / 128
    c["cwsw"] = np.concatenate([np.cos(ang), np.sin(ang)], 1).astype(bf)

    def dft(T, seq):
        t = np.arange(T)
        same = (t[:, None] // seq) == (t[None, :] // seq)
        a = 2 * np.pi * np.outer(t % seq, t % seq) / seq
        nrm = (seq * 128) ** -0.5
        return (np.cos(a) * same * nrm).astype(bf), (-np.sin(a) * same * nrm).astype(bf)
    c["dftc0"], c["dfts0"] = dft(1024, 1024 if is_sample else 256)
    c["dftc1"], c["dfts1"] = dft(512, 256)

    def rope(T, real):
        out = np.zeros((32, 2, T), np.float32)
        out[:, 0, :] = 1.0
        if real:
            t = np.arange(T)
            pos = [t // 64, t % 64]
            inv = 10000.0 ** (-np.arange(8, dtype=np.float32) / 8)
            for d in range(32):
                hh, e = d // 16, d % 16
                f, second = e % 8, e // 8
                a = pos[hh].astype(np.float32) * inv[f]
                out[d, 0] = np.cos(a)
                out[d, 1] = np.sin(a) * (1.0 if second else -1.0)
        return out
    c["rope0"] = rope(1024, is_sample)
    c["rope1"] = rope(512, False)

    def masks(T, NP, seq, past_ok):
        NK = NP + T
        U = np.zeros((5, NK), np.float32)
        W = np.full((5, T), -BIG, np.float32)
        U[0, :NP] = 1
        if past_ok:
            W[0, :] = 0
        kj = np.arange(T)
        for s in range(T // seq):
            U[1 + s, NP + s * seq:NP + (s + 1) * seq] = 1
            W[1 + s, s * seq:(s + 1) * seq] = 0
        return U.astype(bf), W.astype(bf)
    c["mku0"], c["mkw0"] = masks(1024, 512, 1024 if is_sample else 256, is_sample)
    c["mku1"], c["mkw1"] = masks(512, 0, 256, False)
    keep = np.ones((2, 2, 8), np.float32)

    def packed(NC):
        kf = np.array([0.0 if n % 2 == 0 else 1.0 for n in range(8)], np.float32)
        kb = np.array([0.0 if n % 2 == 1 else 1.0 for n in range(8)], np.float32)
        return kf, kb
    if not is_sample:
        keep[0, 0], keep[0, 1] = packed(8)
    keep[1, 0], keep[1, 1] = packed(4)
    c["keep"] = np.broadcast_to(keep.reshape(1, 32), (128, 32)).copy()
    jj, ii = np.meshgrid(np.arange(128), np.arange(128), indexing="ij")
    rc = np.zeros((128, 4, 128), np.float32)
    rc[:, 0] = np.maximum(ii - jj, 0)
    rc[:, 1] = np.maximum(jj - ii, 0)
    rc[:, 2] = np.where(ii == jj, 2.0, 1.0) * 0.125
    rc[:64, 3] = np.arange(128)[None, :] + 1
    rc[64:, 3] = 128 - np.arange(128)[None, :]
    c["rc"] = rc
    c["ez"] = np.stack([127 - np.arange(128), np.arange(128)], 1).astype(np.float32)
    return c


_NC_CACHE = {}


def kernel(x_prompt, x_sample, cache_ckv, cache_krope, state_ret, c, c_ctx, norm_g, w_mod, b_mod, w_in,
           ret_decay_logit, q_norm_g, w_q_up, kv_norm_g, w_kv_up, w_branch, w_out, final_norm_g):
    f32 = np.float32
    A = lambda a: np.ascontiguousarray(np.asarray(a, dtype=f32))
    x_prompt, x_sample, cache_ckv, cache_krope, state_ret = map(A, (x_prompt, x_sample, cache_ckv, cache_krope, state_ret))
    c, c_ctx, norm_g, w_mod, b_mod, w_in = map(A, (c, c_ctx, norm_g, w_mod, b_mod, w_in))
    ret_decay_logit, q_norm_g, w_q_up, kv_norm_g, w_kv_up, w_branch, w_out, final_norm_g = map(
        A, (ret_decay_logit, q_norm_g, w_q_up, kv_norm_g, w_kv_up, w_branch, w_out, final_norm_g))
    if "nc" not in _NC_CACHE:
        _NC_CACHE["nc"] = build_program()
    nc = _NC_CACHE["nc"]

    rq = w_in[:, :, 0:256].reshape(2, 1024, 4, 1, 64)
    w_rqd = A(np.broadcast_to(rq, (2, 1024, 4, 2, 64)).reshape(2, 1024, 512))
    swap = np.array([(d // 16) * 16 + ((d % 16) + 8) % 16 for d in range(32)])
    kr = w_in[:, :, 2176:2208]
    z64 = np.zeros((2, 1024, 64), f32)
    w_krp = A(np.concatenate([z64, kr, z64, kr[:, :, swap]], 2))
    qu = w_q_up.reshape(2, 384, 8, 96)
    qsw = np.concatenate([np.zeros((2, 384, 8, 64), f32), qu[:, :, :, 64:][:, :, :, swap]], 3)
    w_qu2 = A(np.concatenate([qu, qsw], 3).reshape(2, 384, 1536))
    fm = lambda v, k: A(v.reshape(k, 128).T)
    shared = dict(
        ngT=A(np.stack([fm(norm_g[l], 8) for l in range(2)], 1)),
        bmT=A(np.stack([fm(b_mod[l], 24) for l in range(2)], 1)),
        fgT=fm(final_norm_g, 8),
        gqT=A(np.stack([fm(q_norm_g[l], 3) for l in range(2)], 1)),
        gkv=A(kv_norm_g.reshape(512)), rdl=A(ret_decay_logit.reshape(16)),
        w_mod=w_mod, w_in=w_in, w_rqd=w_rqd, w_krp=w_krp, w_qu2=w_qu2, w_kvu=w_kv_up, w_br=w_branch, w_out=w_out,
    )
    cs = {True: _consts(True), False: _consts(False)}
    in_maps = []
    for core in range(8):
        samp = core < 4
        sp = [2 * core, 2 * core + 1]
        if samp:
            xl = x_sample[core]
            cv_l = c[core]
            cck, ckr_, s0 = cache_ckv[core], cache_krope[core], state_ret[core]
        else:
            lp = [16 + 4 * (core - 4) + s for s in range(4)]
            xl = x_prompt[lp].reshape(1024, 1024)
            cv_l = c_ctx
            cck, ckr_, s0 = np.zeros((2, 512, 256), f32), np.zeros((2, 512, 32), f32), np.zeros((2, 2, 4, 64, 128), f32)
        xin = A(np.concatenate([xl, x_prompt[sp].reshape(512, 1024)], 0))
        cvT = A(np.stack([fm(cv_l, 8), fm(c_ctx, 8)], 2))
        m = dict(shared)
        m.update(cs[samp])
        m.update(xin=xin, cvT=cvT, cckv=A(cck), ckr=A(ckr_), s0=A(s0))
        in_maps.append(m)
    res = run_bass_kernel_spmd(nc, in_maps, core_ids=list(range(8)))
    R = res.results
    y_prompt = np.zeros((32, 256, 1024), f32)
    y_sample = np.zeros((4, 1024, 1024), f32)
    new_ckv = np.zeros((32, 2, 256, 256), f32)
    new_kr = np.zeros((32, 2, 256, 32), f32)
    new_ret = np.zeros((32, 2, 2, 4, 64, 128), f32)
    for b in range(32):
        if b < 16:
            core, off, slot = b // 2, 1024 + (b % 2) * 256, 4 + (b % 2)
        else:
            core, off, slot = 4 + (b - 16) // 4, ((b - 16) % 4) * 256, (b - 16) % 4
        r = R[core]
        y_prompt[b] = r["y"][off:off + 256]
        new_ckv[b] = r["o_ckv"][:, off:off + 256]
        new_kr[b] = r["o_kr"][:, off:off + 256]
        st_ = r["o_ret"][:, slot]
        new_ret[b] = st_.reshape(2, 4, 2, 64, 128).transpose(0, 2, 1, 3, 4)
    for b in range(4):
        y_sample[b] = R[b]["y"][0:1024]
    return (y_prompt, y_sample, new_ckv, new_kr, new_ret)
```
